# Optimizing a Trainium2 kernel written in Bass

```python
import math
import jax, jax.numpy as jnp
from jax import lax
import numpy as np

D_MODEL = 1024
BATCH = 2
SEQ = 8192
DEPTH = 1
DEC_BATCH = 32
DEC_SEQ = 8
PAST_LEN = 8192
PAGE_SIZE = 128

HEAD_DIM = 64
A_HEADS = 8
B_HEADS = 8
A_WIDTH = A_HEADS * HEAD_DIM
B_WIDTH = B_HEADS * HEAD_DIM
IDX_HEADS = 4
IDX_DIM = 64
DSA_TOPK = 256
MOBA_BLOCK = 256
MOBA_TOPK = 3
D_FF = 4 * D_MODEL
REL_BUCKETS = 32
REL_EXACT = REL_BUCKETS // 2
REL_MAX_DIST = 128
DSA_QBLOCK = 128
MOBA_QBLOCK = 64
ALPHA = (2.0 * DEPTH) ** 0.25
BETA = (8.0 * DEPTH) ** -0.25
LN_EPS = 1e-5
NEG = -1e30
DSA_ROW = 2 * A_WIDTH + IDX_DIM
MOBA_ROW = 2 * B_WIDTH
IN_SPLITS = [A_WIDTH, A_WIDTH, A_WIDTH, IDX_HEADS * IDX_DIM, IDX_HEADS, IDX_DIM,
             B_WIDTH, B_WIDTH, B_WIDTH, D_MODEL, D_MODEL]
V_COLUMN_GROUPS = (2, 8)
D_IN = sum(IN_SPLITS)
SPLIT_POINTS = [sum(IN_SPLITS[:i + 1]) for i in range(len(IN_SPLITS) - 1)]

kernel_name = 'dsa_moba_gated_hybrid_step'


def layer_norm(x, g, b):
    xf = x.astype(jnp.float32)
    mu = jnp.mean(xf, axis=-1, keepdims=True)
    var = jnp.mean(jnp.square(xf - mu), axis=-1, keepdims=True)
    y = (xf - mu) * lax.rsqrt(var + LN_EPS) * g.astype(jnp.float32) + b.astype(jnp.float32)
    return y.astype(x.dtype)


def t5_bucket(dist):
    n = jnp.maximum(dist, 0)
    nf = jnp.maximum(n, 1).astype(jnp.float32)
    large = REL_EXACT + (jnp.log(nf / REL_EXACT) / math.log(REL_MAX_DIST / REL_EXACT)
                         * (REL_BUCKETS - REL_EXACT)).astype(jnp.int32)
    large = jnp.minimum(large, REL_BUCKETS - 1)
    return jnp.where(n < REL_EXACT, n, large)


def dsa_attend(q, qi, wi, pos_q, k, v, ki, tab, topk):
    L = k.shape[1]
    s = jnp.einsum('bqhd,bsd->bqhs', qi, ki) * (IDX_DIM ** -0.5)
    score = jnp.einsum('bqhs,bqh->bqs', jax.nn.relu(s), wi) * (IDX_HEADS ** -0.5)
    kpos = jnp.arange(L, dtype=jnp.int32)
    causal = kpos[None, :] <= pos_q[:, None]
    score = jnp.where(causal[None], score, -jnp.inf)
    _, idx = lax.top_k(score, topk)
    kg = jax.vmap(lambda kk, ii: kk[ii])(k, idx)
    vg = jax.vmap(lambda vv, ii: vv[ii])(v, idx)
    dist = pos_q[None, :, None] - idx
    valid = dist >= 0
    bias = jnp.moveaxis(tab[t5_bucket(dist)], -1, 2).astype(jnp.float32)
    logits = jnp.einsum('bqhd,bqkhd->bqhk', q, kg).astype(jnp.float32) * (HEAD_DIM ** -0.5) + bias
    logits = jnp.where(valid[:, :, None, :], logits, NEG)
    p = jax.nn.softmax(logits, axis=-1).astype(vg.dtype)
    return jnp.einsum('bqhk,bqkhd->bqhd', p, vg)


def moba_blocks(k, v):
    B_, L, H, D = k.shape
    nb = -(-L // MOBA_BLOCK)
    pad = nb * MOBA_BLOCK - L
    kb = jnp.pad(k, ((0, 0), (0, pad), (0, 0), (0, 0))).reshape(B_, nb, MOBA_BLOCK, H, D)
    vb = jnp.pad(v, ((0, 0), (0, pad), (0, 0), (0, 0))).reshape(B_, nb, MOBA_BLOCK, H, D)
    means = jnp.mean(kb.astype(jnp.float32), axis=2)
    return kb.transpose(0, 3, 1, 2, 4), vb.transpose(0, 3, 1, 2, 4), means


def moba_attend(q, pos_q, kbt, vbt, means, tab):
    B_, Tq, H, D = q.shape
    nb = means.shape[1]
    gate = jnp.einsum('bqhd,bnhd->bqhn', q.astype(jnp.float32), means)
    own = pos_q // MOBA_BLOCK
    past = jnp.arange(nb, dtype=jnp.int32)[None, :] < own[:, None]
    gate = jnp.where(past[None, :, None, :], gate, -jnp.inf)
    _, sel = lax.top_k(gate, min(MOBA_TOPK, nb))
    sel_ok = sel < own[None, :, None, None]
    own_b = jnp.broadcast_to(own[None, :, None, None], (B_, Tq, H, 1)).astype(sel.dtype)
    blocks = jnp.concatenate([sel, own_b], axis=-1)
    blk_ok = jnp.concatenate([sel_ok, jnp.ones((B_, Tq, H, 1), bool)], axis=-1)
    b_ix = jnp.arange(B_)[:, None, None, None]
    h_ix = jnp.arange(H)[None, None, :, None]
    kg = kbt[b_ix, h_ix, blocks]
    vg = vbt[b_ix, h_ix, blocks]
    kpos = blocks[..., None] * MOBA_BLOCK + jnp.arange(MOBA_BLOCK, dtype=jnp.int32)
    dist = pos_q[None, :, None, None, None] - kpos
    valid = blk_ok[..., None] & (dist >= 0)
    bias = tab.T[h_ix[..., None], t5_bucket(dist)].astype(jnp.float32)
    logits = jnp.einsum('bqhd,bqhjpd->bqhjp', q, kg).astype(jnp.float32) * (HEAD_DIM ** -0.5) + bias
    logits = jnp.where(valid, logits, NEG).reshape(B_, Tq, H, -1)
    p = jax.nn.softmax(logits, axis=-1).reshape(valid.shape).astype(vg.dtype)
    return jnp.einsum('bqhjp,bqhjpd->bqhd', p, vg)


def map_queries(fn, qblock, pos, *qs):
    T = pos.shape[0]
    if T <= qblock or T % qblock:
        return fn(*qs, pos)
    n = T // qblock
    chunks = tuple(jnp.moveaxis(q.reshape(q.shape[0], n, qblock, *q.shape[2:]), 1, 0) for q in qs)
    out = lax.map(lambda c: fn(*c[:-1], c[-1]), chunks + (pos.reshape(n, qblock),))
    out = jnp.moveaxis(out, 0, 1)
    return out.reshape(out.shape[0], T, *out.shape[3:])


def layer(x, pos, past_dsa, past_moba, w_in, w_a_up, w_b_up, w_out, ln1_g, ln1_b,
          w_ff1, w_ff2, ln2_g, ln2_b, rel_bias):
    B_, T, _ = x.shape
    q_a, k_a, v_a, q_i, w_i, k_i, q_b, k_b, v_b, g_a, g_b = jnp.split(x @ w_in, SPLIT_POINTS, axis=-1)
    rows_dsa = jnp.concatenate([k_a, v_a, k_i], axis=-1)
    rows_moba = jnp.concatenate([k_b, v_b], axis=-1)
    keys_dsa = rows_dsa if past_dsa is None else jnp.concatenate([past_dsa.astype(rows_dsa.dtype), rows_dsa], axis=1)
    keys_moba = rows_moba if past_moba is None else jnp.concatenate([past_moba.astype(rows_moba.dtype), rows_moba], axis=1)
    L = keys_dsa.shape[1]
    tab_a = rel_bias[:, :A_HEADS]
    tab_b = rel_bias[:, A_HEADS:]

    ka, va, ki = jnp.split(keys_dsa, [A_WIDTH, 2 * A_WIDTH], axis=-1)
    ka = ka.reshape(B_, L, A_HEADS, HEAD_DIM)
    va = va.reshape(B_, L, A_HEADS, HEAD_DIM)
    topk = min(DSA_TOPK, L // 4)

    def dsa_fn(qa, qi, wi, pq):
        return dsa_attend(qa, qi, wi, pq, ka, va, ki, tab_a, topk)

    o_a = map_queries(dsa_fn, DSA_QBLOCK, pos, q_a.reshape(B_, T, A_HEADS, HEAD_DIM),
                      q_i.reshape(B_, T, IDX_HEADS, IDX_DIM), w_i)

    kbm, vbm = jnp.split(keys_moba, [B_WIDTH], axis=-1)
    kbt, vbt, means = moba_blocks(kbm.reshape(B_, L, B_HEADS, HEAD_DIM), vbm.reshape(B_, L, B_HEADS, HEAD_DIM))

    def moba_fn(qb, pq):
        return moba_attend(qb, pq, kbt, vbt, means, tab_b)

    o_b = map_queries(moba_fn, MOBA_QBLOCK, pos, q_b.reshape(B_, T, B_HEADS, HEAD_DIM))

    mixed = (jax.nn.sigmoid(g_a) * (o_a.reshape(B_, T, A_WIDTH) @ w_a_up)
             + jax.nn.sigmoid(g_b) * (o_b.reshape(B_, T, B_WIDTH) @ w_b_up)) @ w_out
    h = layer_norm(ALPHA * x + mixed, ln1_g, ln1_b)
    f = jnp.square(jax.nn.relu(h @ w_ff1)) @ w_ff2
    y = layer_norm(ALPHA * h + f, ln2_g, ln2_b)
    return y, rows_dsa, rows_moba


def setup_inputs(seed: int = 0) -> dict:
    key = jax.random.key(seed)
    ks = jax.random.split(key, 24)

    def nrm(k, shape, scale):
        return jax.random.normal(k, shape, jnp.float32) * scale

    n_pages = PAST_LEN // PAGE_SIZE
    n_used = DEC_BATCH * n_pages
    n_pool = n_used + max(1, n_used // 4)
    x_prompt = nrm(ks[0], (BATCH, SEQ, D_MODEL), 1.0)
    x_sample = nrm(ks[1], (DEC_BATCH, DEC_SEQ, D_MODEL), 1.0)
    cache_dsa = nrm(ks[2], (DEPTH, n_pool, PAGE_SIZE, DSA_ROW), 1.0)
    cache_moba = nrm(ks[3], (DEPTH, n_pool, PAGE_SIZE, MOBA_ROW), 1.0)
    page_table = jax.random.permutation(ks[4], n_pool)[:n_used].reshape(DEC_BATCH, n_pages).astype(jnp.int32)
    in_keys = jax.random.split(ks[5], len(IN_SPLITS))
    w_in = jnp.concatenate(
        [nrm(in_keys[i], (DEPTH, D_MODEL, w), D_MODEL ** -0.5 * (BETA if i in V_COLUMN_GROUPS else 1.0))
         for i, w in enumerate(IN_SPLITS)], axis=-1)
    rel_bias = nrm(ks[6], (REL_BUCKETS, A_HEADS + B_HEADS), 0.1)
    w_a_up = nrm(ks[7], (DEPTH, A_WIDTH, D_MODEL), A_WIDTH ** -0.5)
    w_b_up = nrm(ks[8], (DEPTH, B_WIDTH, D_MODEL), B_WIDTH ** -0.5)
    w_out = nrm(ks[9], (DEPTH, D_MODEL, D_MODEL), D_MODEL ** -0.5 * BETA)
    ln1_g = 1.0 + nrm(ks[10], (DEPTH, D_MODEL), 0.01)
    ln1_b = nrm(ks[11], (DEPTH, D_MODEL), 0.01)
    w_ff1 = nrm(ks[12], (DEPTH, D_MODEL, D_FF), D_MODEL ** -0.5)
    w_ff2 = nrm(ks[13], (DEPTH, D_FF, D_MODEL), D_FF ** -0.5 * BETA)
    ln2_g = 1.0 + nrm(ks[14], (DEPTH, D_MODEL), 0.01)
    ln2_b = nrm(ks[15], (DEPTH, D_MODEL), 0.01)
    return {'x_prompt': x_prompt, 'x_sample': x_sample, 'cache_dsa': cache_dsa, 'cache_moba': cache_moba,
            'page_table': page_table, 'w_in': w_in, 'rel_bias': rel_bias, 'w_a_up': w_a_up, 'w_b_up': w_b_up,
            'w_out': w_out, 'ln1_g': ln1_g, 'ln1_b': ln1_b, 'w_ff1': w_ff1, 'w_ff2': w_ff2,
            'ln2_g': ln2_g, 'ln2_b': ln2_b}


def reference(x_prompt, x_sample, cache_dsa, cache_moba, page_table, w_in, rel_bias, w_a_up, w_b_up,
              w_out, ln1_g, ln1_b, w_ff1, w_ff2, ln2_g, ln2_b):
    n_seq = page_table.shape[0]
    past_len = page_table.shape[1] * PAGE_SIZE
    pos_prompt = jnp.arange(x_prompt.shape[1], dtype=jnp.int32)
    pos_sample = past_len + jnp.arange(x_sample.shape[1], dtype=jnp.int32)
    yp, ys = x_prompt, x_sample
    dsa_p, moba_p, dsa_s, moba_s = [], [], [], []
    for l in range(DEPTH):
        params = (w_in[l], w_a_up[l], w_b_up[l], w_out[l], ln1_g[l], ln1_b[l],
                  w_ff1[l], w_ff2[l], ln2_g[l], ln2_b[l], rel_bias)
        yp, rdp, rmp = layer(yp, pos_prompt, None, None, *params)
        past_dsa = cache_dsa[l][page_table].reshape(n_seq, past_len, DSA_ROW)
        past_moba = cache_moba[l][page_table].reshape(n_seq, past_len, MOBA_ROW)
        ys, rds, rms = layer(ys, pos_sample, past_dsa, past_moba, *params)
        dsa_p.append(rdp)
        moba_p.append(rmp)
        dsa_s.append(rds)
        moba_s.append(rms)
    return (yp, ys, jnp.stack(dsa_p), jnp.stack(moba_p), jnp.stack(dsa_s), jnp.stack(moba_s))
```

```python
import math
import numpy as np
from contextlib import ExitStack
import concourse.bass as bass
import concourse.mybir as mybir
from concourse.bass_utils import run_bass_kernel_spmd

F32 = mybir.dt.float32
BF16 = mybir.dt.bfloat16
I32 = mybir.dt.int32
AF = mybir.ActivationFunctionType
ALU = mybir.AluOpType
AX = mybir.AxisListType

D = 1024
T = 8192
NT = 64
QA, KA, VA, QI, WI, KI, QB, KB, VB, GA, GB, DIN = 0, 512, 1024, 1536, 1792, 1796, 1860, 2372, 2884, 3396, 4420, 5444
ALPHA = 2.0 ** 0.25
LN_EPS = 1e-5
NEGM = -30000.0
BIG = 1e30
NIT = 16
NIT2 = 14
EPS_TIE = 1e-12
DFF = 4096
NSAMP = 32
DBG_NBLK = 16
DBG_SAMPLE = True
DBG_LEVEL = 9
DBG_NQG = 4
DBG_Q = 99
DBG_NSEQ = 4


class _Stop(Exception):
    pass


MUTE = [False]


def chk(k):
    if DBG_Q < k:
        MUTE[0] = True

ENGS = ["sync", "scalar", "gpsimd", "vector", "tensor"]
EPOCH = 4096


class Buf:
    __slots__ = ("w", "r", "x")

    def __init__(self):
        self.w = None
        self.r = []
        self.x = False


class TT:
    def __init__(self, t):
        self.t = t
        self.b = Buf()

    def __getitem__(self, k):
        return self.t[k]


class Prog:
    NDMA = 16

    def __init__(self, nc, es):
        self.nc = nc
        self.es = es
        self.ops = {e: [] for e in ENGS}
        self.cnt = {}
        self.sems = {}
        self.waited = {e: {} for e in ENGS}
        self.dma_n = {e: 0 for e in ENGS}
        self.ncomp = {e: 0 for e in ENGS}
        self.last = {}

    def _sem(self, key):
        if key not in self.sems:
            self.sems[key] = self.es.enter_context(self.nc.semaphore(key))
            self.cnt[key] = 0
        return key

    def _need(self, reads, writes):
        need = {}

        def add(t):
            if t is None:
                return
            k, v = t
            if need.get(k, 0) < v:
                need[k] = v
        for b in reads:
            add(b.w)
        for b in writes:
            add(b.w)
            for r in b.r:
                add(r)
        return need

    def _waits(self, eng, need):
        waits = []
        wd = self.waited[eng]
        for k, v in need.items():
            if wd.get(k, 0) < v:
                wd[k] = v
                waits.append((k, v))
        return waits

    def _mark(self, tok, reads, writes):
        for b in reads:
            b.r.append(tok)
            if len(b.r) > 64:
                mx = {}
                for k, v in b.r:
                    if mx.get(k, 0) < v:
                        mx[k] = v
                b.r = list(mx.items())
        for b in writes:
            b.w = tok
            b.r = []
        self.last[tok[0]] = tok[1]

    @staticmethod
    def _bufs(xs):
        return [x.b if isinstance(x, TT) else x for x in xs]

    def op(self, eng, fn, reads=(), writes=()):
        if MUTE[0]:
            return None
        reads = self._bufs(reads)
        writes = self._bufs(writes)
        xr = [b for b in reads if b.x]
        if xr:
            writes = list(writes) + [b for b in xr if b not in writes]
            reads = [b for b in reads if not b.x]
        waits = self._waits(eng, self._need(reads, writes))
        n = self.ncomp[eng]
        self.ncomp[eng] += 1
        key = self._sem("c_%s_%d" % (eng, n // EPOCH))
        self.cnt[key] += 1
        tok = (key, self.cnt[key])
        self.ops[eng].append((waits, fn, key, 1))
        self._mark(tok, reads, writes)
        return tok

    def dma(self, eng, fn, reads=(), writes=()):
        if MUTE[0]:
            return None
        reads = self._bufs(reads)
        writes = self._bufs(writes)
        n = self.dma_n[eng]
        self.dma_n[eng] += 1
        key = self._sem("d_%s_%d" % (eng, n % self.NDMA))
        need = self._need(reads, writes)
        prev = self.cnt[key]
        if prev > 0 and need.get(key, 0) < prev:
            need[key] = prev
        waits = self._waits(eng, need)
        self.cnt[key] += 16
        tok = (key, self.cnt[key])
        self.ops[eng].append((waits, fn, key, 16))
        self._mark(tok, reads, writes)
        return tok

    def barrier(self):
        toks = [(k, v) for k, v in self.cnt.items() if v > 0]
        for e in ENGS:
            waits = self._waits(e, dict(toks))
            if waits:
                self.ops[e].append((waits, None, None, 0))

    def emit(self):
        nc = self.nc
        with nc.Block() as block:
            for ename in ENGS:
                ops = self.ops[ename]
                if not ops:
                    continue

                def body(eng, ops=ops):
                    for waits, fn, key, inc in ops:
                        for k, v in waits:
                            eng.wait_ge(self.sems[k], v)
                        if fn is not None:
                            fn(eng).then_inc(self.sems[key], inc)
                getattr(block, ename)(body)


def t5_bucket_np(n):
    n = np.maximum(n, 0)
    nf = np.maximum(n, 1).astype(np.float32)
    large = 16 + (np.log(nf / np.float32(16)) / np.float32(math.log(128 / 16)) * np.float32(16)).astype(np.int32)
    large = np.minimum(large, 31)
    return np.where(n < 16, n, large)


def build_program():
    MUTE[0] = False
    nc = bass.Bass("TRN2", target_bir_lowering=False)

    def din(name, shape, dt=F32):
        return nc.dram_tensor(name, list(shape), dt, kind="ExternalInput").ap()

    def dout(name, shape, dt=F32):
        return nc.dram_tensor(name, list(shape), dt, kind="ExternalOutput").ap()

    def dscr(name, shape, dt):
        return nc.dram_tensor(name, list(shape), dt, kind="Internal").ap()

    xs = din("xs", [T, D])
    xsm = din("xsm", [NSAMP, D])
    w_in = din("w_in", [D, DIN])
    w_a_up = din("w_a_up", [512, D])
    w_b_up = din("w_b_up", [512, D])
    w_out = din("w_out", [D, D])
    w_ff1 = din("w_ff1", [D, DFF])
    w_ff2 = din("w_ff2", [DFF, D])
    lnrep = din("lnrep", [128, 4, D])
    ident_d = din("ident", [128, 128])
    wtab_d = din("wtab", [128, 16, 1024])
    b31_d = din("b31", [128, 16])
    bval_d = din("bval", [128, 32])
    ablk_d = din("ablk", [32, 32, 128])
    btab_d = din("btab", [128, 9, 512])
    pastb_d = din("pastb", [4, 128, 2, 32])

    ptrep_d = din("ptrep", [128, 256], I32)
    iot_d = din("iot", [128, 1], I32)
    cdsa = din("cdsa", [2560 * 128, 1088])
    cmoba = din("cmoba", [2560 * 128, 1024])
    y_p = dout("y_p", [2048, D])
    y_s = dout("y_s", [NSAMP, D])
    dsa_p = dout("dsa_p", [2048, 1088])
    moba_p = dout("moba_p", [2048, 1024])
    dsa_s = dout("dsa_s", [NSAMP, 1088])
    moba_s = dout("moba_s", [NSAMP, 1024])

    kaT_d = dscr("kaT_d", [4, 128, T], BF16)
    kbT_d = dscr("kbT_d", [4, 128, T], BF16)
    kiT_d = dscr("kiT_d", [64, T], BF16)
    va_d = dscr("va_d", [NT, 128, 520], BF16)
    vb_d = dscr("vb_d", [NT, 128, 520], BF16)
    qaT_d = dscr("qaT_d", [4, 128, 4, 512], BF16)
    qbT_d = dscr("qbT_d", [4, 128, 4, 512], BF16)
    qiT_d = dscr("qiT_d", [4, 128, 2, 512], BF16)
    sga_d = dscr("sga_d", [4, 8, 128, 512], F32)
    sgb_d = dscr("sgb_d", [4, 8, 128, 512], F32)
    skaT_d = dscr("skaT_d", [4, 4, 128, 8320], BF16)
    skbT_d = dscr("skbT_d", [4, 4, 128, 8320], BF16)
    skiT_d = dscr("skiT_d", [4, 64, 8704], BF16)
    sva_d = dscr("sva_d", [4, 65, 128, 520], BF16)
    svb_d = dscr("svb_d", [4, 65, 128, 520], BF16)

    out_toks = []

    with ExitStack() as es:
        P = Prog(nc, es)

        uid = [0]

        def sb(st, name, shape, dt):
            uid[0] += 1
            return TT(st.enter_context(nc.sbuf_tensor("s_%s_%d" % (name, uid[0]), list(shape), dt)))

        def ps(st, name, shape, dt):
            return TT(st.enter_context(nc.psum_tensor("p_" + name, list(shape), dt)))

        banks = [ps(es, "bank%d" % i, [128, 512], F32) for i in range(8)]
        for bk_ in banks:
            bk_.b.x = True
        ident = sb(es, "identf", [128, 128], F32)
        identb = sb(es, "identb", [128, 128], BF16)
        b31 = sb(es, "b31", [128, 16], F32)
        lohi = sb(es, "lohi", [128, 16, 8], F32)
        lohis = sb(es, "lohis", [128, 8], F32)
        meansT = sb(es, "meansT", [128, 4, 32], F32)
        meansTb = sb(es, "meansTb", [128, 4, 32], BF16)
        ones_t = sb(es, "ones_t", [128, 64], F32)
        P.dma("sync", lambda e: e.dma_start(out=ident[:], in_=ident_d), writes=[ident])
        P.dma("gpsimd", lambda e: e.dma_start(out=identb[:], in_=ident_d), writes=[identb])
        P.dma("sync", lambda e: e.dma_start(out=b31[:], in_=b31_d), writes=[b31])
        P.op("vector", lambda e: e.memset(ones_t[:], 1.0), writes=[ones_t])

        qaTs = sb(es, "qaTs", [128, 4, NSAMP], BF16)
        qbTs = sb(es, "qbTs", [128, 4, NSAMP], BF16)
        qiTs = sb(es, "qiTs", [128, 2, NSAMP], BF16)
        qiTm = [sb(es, "qiTm%d" % i, [128, 2, NSAMP], BF16) for i in range(4)]
        qbTm = [sb(es, "qbTm%d" % i, [128, 4, NSAMP], BF16) for i in range(4)]
        sgas = sb(es, "sgas", [128, 8, NSAMP], F32)
        sgbs = sb(es, "sgbs", [128, 8, NSAMP], F32)
        smeansTb = [sb(es, "smeansTb%d" % i, [128, 4, 32], BF16) for i in range(4)]
        qTall = TT(None)
        maskall = TT(None)
        oall = TT(None)
        pastb_t = TT(None)
        sscr = TT(None)
        bank_rr = [0]

        def next_bank(lo=0, hi=8):
            i = bank_rr[0]
            if not (lo <= i < hi):
                i = lo
            bank_rr[0] = i + 1 if i + 1 < hi else lo
            return banks[i]

        evac_rr = [0]

        def evac(out_ap, in_ap, reads, writes, scale=None, func=None, eng=None):
            if eng is None:
                eng = "scalar" if (evac_rr[0] % 2 == 0) else "vector"
                evac_rr[0] += 1
            if func is not None:
                eng = "scalar"
            if eng == "scalar":
                f = func if func is not None else AF.Copy
                if scale is None:
                    P.op("scalar", lambda e: e.activation(out=out_ap, in_=in_ap, func=f), reads=reads, writes=writes)
                else:
                    P.op("scalar", lambda e: e.activation(out=out_ap, in_=in_ap, func=f, scale=scale), reads=reads, writes=writes)
            else:
                if scale is None:
                    P.op("vector", lambda e: e.tensor_copy(out=out_ap, in_=in_ap), reads=reads, writes=writes)
                else:
                    P.op("vector", lambda e: e.tensor_scalar(out=out_ap, in0=in_ap, scalar1=float(scale), scalar2=None, op0=ALU.mult),
                         reads=reads, writes=writes)

        with ExitStack() as s1:
            win = sb(s1, "win", [128, 8, DIN], BF16)
            for c in range(8):
                P.dma("gpsimd", lambda e, c=c: e.dma_start(out=win[:, c, :], in_=w_in[c * 128:(c + 1) * 128, :]), writes=[win])
            xin = [sb(s1, "xin%d" % i, [128, 4, D], F32) for i in range(2)]
            XT = [sb(s1, "XT%d" % i, [128, 8, 512], BF16) for i in range(2)]
            kstg = [sb(s1, "kstg%d" % i, [128, 512], BF16) for i in range(4)]
            vstg = [sb(s1, "vstg%d" % i, [128, 8, 65], BF16) for i in range(4)]
            fstg = [sb(s1, "fstg%d" % i, [128, 512], F32) for i in range(3)]
            rowd = [sb(s1, "rowd%d" % i, [128, 1088], F32) for i in range(2)]
            rowm = [sb(s1, "rowm%d" % i, [128, 1024], F32) for i in range(2)]
            qiw = sb(s1, "qiw", [128, 256], F32)
            wsb = sb(s1, "wsb", [128, 4], F32)
            qistg = sb(s1, "qistg", [128, 2, 512], BF16)
            for v in vstg:
                P.op("vector", lambda e, v=v: e.memset(v[:], 1.0), writes=[v])
            kst_i = [0]
            vst_i = [0]
            fst_i = [0]

            def proj_fm(col0, M, xt, scale=None, func=None, N=512):
                bk = next_bank()
                for dc in range(8):
                    P.op("tensor", lambda e, dc=dc, bk=bk: e.matmul(bk[0:M, 0:N], lhsT=win[:, dc, col0:col0 + M], rhs=xt[:, dc, 0:N],
                                                                  start=(dc == 0), stop=(dc == 7)),
                         reads=[win, xt], writes=[bk])
                return bk

            def proj_tm(col0, N, xt, t, ntok=128):
                bk = next_bank()
                for dc in range(8):
                    P.op("tensor", lambda e, dc=dc, bk=bk: e.matmul(bk[0:ntok, 0:N], lhsT=xt[:, dc, t * 128:t * 128 + ntok],
                                                                  rhs=win[:, dc, col0:col0 + N], start=(dc == 0), stop=(dc == 7)),
                         reads=[win, xt], writes=[bk])
                return bk

            def load_xT(src_rows_ap, xi, xt, ntile, ntok=128):
                for t in range(ntile):
                    pass
                P.dma("sync", lambda e: e.dma_start(out=xi[0:ntok, 0:ntile, :], in_=src_rows_ap.rearrange("(t p) d -> p t d", p=ntok)),
                      writes=[xi])
                for c in range(8):
                    bk = next_bank()
                    for t in range(ntile):
                        P.op("tensor", lambda e, c=c, t=t, bk=bk: e.transpose(out=bk[:, t * 128:t * 128 + ntok],
                                                                            in_=xi[0:ntok, t, c * 128:(c + 1) * 128],
                                                                            identity=ident[0:ntok, 0:ntok]),
                             reads=[xi, ident], writes=[bk])
                    w = ntile * 128 if ntok == 128 else ntok
                    evac(xt[:, c, 0:w], bk[:, 0:w], [bk], [xt])

            for sbk in range(DBG_NBLK):
                xi = xin[sbk % 2]
                xt = XT[sbk % 2]
                load_xT(xs[sbk * 512:(sbk + 1) * 512, :], xi, xt, 4)
                own = (sbk % 4 == 3) and DBG_LEVEL >= 5
                mq = sbk // 4
                for (col0, M, dst, is_kb, cidx) in [] if DBG_LEVEL < 3 else (
                        [(KA + 128 * c, 128, kaT_d[c], False, c) for c in range(4)]
                        + [(KI, 64, kiT_d, False, 0)]
                        + [(KB + 128 * c, 128, kbT_d[c], True, c) for c in range(4)]):
                    bk = proj_fm(col0, M, xt)
                    st = kstg[kst_i[0] % 4]
                    kst_i[0] += 1
                    evac(st[0:M, :], bk[0:M, :], [bk], [st])
                    if is_kb:
                        for hb in range(2):
                            P.op("vector", lambda e, bk=bk, cidx=cidx, sbk=sbk, hb=hb: e.tensor_reduce(
                                out=meansT[:, cidx, 2 * sbk + hb:2 * sbk + hb + 1], in_=bk[:, hb * 256:(hb + 1) * 256],
                                axis=AX.X, op=ALU.add), reads=[bk], writes=[meansT])
                    P.dma("sync", lambda e, st=st, dst=dst, M=M, sbk=sbk: e.dma_start(out=dst[0:M, sbk * 512:(sbk + 1) * 512], in_=st[0:M, :]),
                          reads=[st])
                for t in range(4 if DBG_LEVEL >= 4 else 0):
                    u = sbk * 4 + t
                    if own:
                        rd = rowd[t % 2]
                        rm = rowm[t % 2]
                    for (col0, dst, which) in ((VA, va_d, 0), (VB, vb_d, 1)):
                        bk = proj_tm(col0, 512, xt, t)
                        vs_ = vstg[vst_i[0] % 4]
                        vst_i[0] += 1
                        evac(vs_[:, :, 0:64], bk[:, :].rearrange("p (h d) -> p h d", h=8), [bk], [vs_])
                        P.dma("sync", lambda e, vs_=vs_, dst=dst, u=u: e.dma_start(out=dst[u], in_=vs_[:].rearrange("p h d -> p (h d)")),
                              reads=[vs_])
                        if own:
                            tgt = rd if which == 0 else rm
                            evac(tgt[:, 512:1024], bk[:, :], [bk], [tgt])
                    if own:
                        bk = proj_tm(KA, 512, xt, t)
                        evac(rd[:, 0:512], bk[:, :], [bk], [rd])
                        bk = proj_tm(KI, 64, xt, t)
                        evac(rd[:, 1024:1088], bk[:, 0:64], [bk], [rd])
                        bk = proj_tm(KB, 512, xt, t)
                        evac(rm[:, 0:512], bk[:, :], [bk], [rm])
                        r0 = (mq * 4 + t) * 128
                        out_toks.append(P.dma("sync", lambda e, rd=rd, r0=r0: e.dma_start(out=dsa_p[r0:r0 + 128, :], in_=rd[:]), reads=[rd]))
                        out_toks.append(P.dma("sync", lambda e, rm=rm, r0=r0: e.dma_start(out=moba_p[r0:r0 + 128, :], in_=rm[:]), reads=[rm]))
                        qt = mq * 4 + t
                        bk = proj_tm(WI, 4, xt, t)
                        P.op("vector", lambda e, bk=bk: e.tensor_copy(out=wsb[:], in_=bk[:, 0:4]), reads=[bk], writes=[wsb])
                        P.op("vector", lambda e, qt=qt: e.tensor_scalar(out=lohi[:, qt, 0:4], in0=wsb[:], scalar1=0.0, scalar2=-BIG,
                                                                    op0=ALU.is_le, op1=ALU.mult), reads=[wsb], writes=[lohi])
                        P.op("vector", lambda e, qt=qt: e.tensor_scalar(out=lohi[:, qt, 4:8], in0=wsb[:], scalar1=0.0, scalar2=BIG,
                                                                    op0=ALU.is_gt, op1=ALU.mult), reads=[wsb], writes=[lohi])
                        bk = proj_tm(QI, 256, xt, t)
                        for h in range(4):
                            P.op("vector", lambda e, bk=bk, h=h: e.tensor_scalar(out=qiw[:, h * 64:(h + 1) * 64], in0=bk[:, h * 64:(h + 1) * 64],
                                                                             scalar1=wsb[:, h:h + 1], scalar2=None, op0=ALU.mult),
                                 reads=[bk, wsb], writes=[qiw])
                        for pc in range(2):
                            bk2 = next_bank()
                            P.op("tensor", lambda e, bk2=bk2, pc=pc: e.transpose(out=bk2[:, 0:128], in_=qiw[:, pc * 128:(pc + 1) * 128],
                                                                               identity=ident[:]), reads=[qiw, ident], writes=[bk2])
                            evac(qistg[:, pc, t * 128:(t + 1) * 128], bk2[:, 0:128], [bk2], [qistg])
                if own:
                    P.dma("sync", lambda e, mq=mq: e.dma_start(out=qiT_d[mq], in_=qistg[:]), reads=[qistg])
                    for (col0, dst) in ((QA, qaT_d), (QB, qbT_d)):
                        for c in range(4):
                            bk = proj_fm(col0 + 128 * c, 128, xt)
                            st = kstg[kst_i[0] % 4]
                            kst_i[0] += 1
                            evac(st[:], bk[:], [bk], [st], scale=0.125)
                            P.dma("sync", lambda e, st=st, dst=dst, mq=mq, c=c: e.dma_start(out=dst[mq, :, c, :], in_=st[:]), reads=[st])
                    for (col0, dst) in ((GA, sga_d), (GB, sgb_d)):
                        for c in range(8):
                            bk = proj_fm(col0 + 128 * c, 128, xt)
                            st = fstg[fst_i[0] % 3]
                            fst_i[0] += 1
                            evac(st[:], bk[:], [bk], [st], func=AF.Sigmoid)
                            P.dma("sync", lambda e, st=st, dst=dst, mq=mq, c=c: e.dma_start(out=dst[mq, c], in_=st[:]), reads=[st])

            xi = xin[0]
            assert True
            xt = XT[0]
            load_xT(xsm[:, :], xi, xt, 1, ntok=NSAMP)
            rd = rowd[0]
            rm = rowm[0]
            for (col0, N, tgt, o0) in ((KA, 512, rd, 0), (VA, 512, rd, 512), (KI, 64, rd, 1024), (KB, 512, rm, 0), (VB, 512, rm, 512)):
                bk = proj_tm(col0, N, xt, 0, ntok=NSAMP)
                evac(tgt[0:NSAMP, o0:o0 + N], bk[0:NSAMP, 0:N], [bk], [tgt])
            out_toks.append(P.dma("sync", lambda e: e.dma_start(out=dsa_s, in_=rd[0:NSAMP, :]), reads=[rd]))
            out_toks.append(P.dma("sync", lambda e: e.dma_start(out=moba_s, in_=rm[0:NSAMP, :]), reads=[rm]))
            if DBG_SAMPLE:
                zt = sb(s1, "zt", [128, 520], BF16)
                P.op("vector", lambda e: e.memset(zt[:], 0.0), writes=[zt])
                for (col0, dstT) in ((QA, qaTs), (QB, qbTs)):
                    for c in range(4):
                        bk = proj_fm(col0 + 128 * c, 128, xt, N=NSAMP)
                        evac(dstT[:, c, :], bk[:, 0:NSAMP], [bk], [dstT, qTall], scale=0.125)
                for (col0, dstT) in ((GA, sgas), (GB, sgbs)):
                    for c in range(8):
                        bk = proj_fm(col0 + 128 * c, 128, xt, N=NSAMP)
                        evac(dstT[:, c, :], bk[:, 0:NSAMP], [bk], [dstT], func=AF.Sigmoid)
                for (col0, M, dstd, cidx) in ([(KA + 128 * c, 128, skaT_d, c) for c in range(4)] + [(KB + 128 * c, 128, skbT_d, c) for c in range(4)] + [(KI, 64, skiT_d, None)]):
                    bk = proj_fm(col0, M, xt, N=NSAMP)
                    st = kstg[kst_i[0] % 4]
                    kst_i[0] += 1
                    evac(st[0:M, 0:NSAMP], bk[0:M, 0:NSAMP], [bk], [st])
                    for s in range(4):
                        dd = dstd[s, cidx] if cidx is not None else dstd[s]
                        P.dma("sync", lambda e, st=st, dd=dd, M=M, s=s: e.dma_start(out=dd[0:M, 8192:8200], in_=st[0:M, 8 * s:8 * s + 8]), reads=[st], writes=[sscr])
                        wz = (8320 - 8200) if cidx is not None else (8704 - 8200)
                        P.dma("sync", lambda e, dd=dd, M=M, wz=wz: e.dma_start(out=dd[0:M, 8200:8200 + wz], in_=zt[0:M, 0:wz]), reads=[zt], writes=[sscr])
                for (src, o0, dstd) in ((rd, 512, sva_d), (rm, 512, svb_d)):
                    vs_ = vstg[vst_i[0] % 4]
                    vst_i[0] += 1
                    P.op("vector", lambda e, vs_=vs_, src=src, o0=o0: e.tensor_copy(out=vs_[0:NSAMP, :, 0:64], in_=src[0:NSAMP, o0:o0 + 512].rearrange("p (h d) -> p h d", h=8)),
                         reads=[src], writes=[vs_])
                    for s in range(4):
                        P.dma("sync", lambda e, vs_=vs_, dstd=dstd, s=s: e.dma_start(out=dstd[s, 64, 0:8, :], in_=vs_[8 * s:8 * s + 8].rearrange("p h d -> p (h d)")),
                              reads=[vs_], writes=[sscr])
                        P.dma("sync", lambda e, dstd=dstd, s=s: e.dma_start(out=dstd[s, 64, 8:128, :], in_=zt[0:120, :]), reads=[zt], writes=[sscr])
                bk = proj_tm(WI, 4, xt, 0, ntok=NSAMP)
                P.op("vector", lambda e, bk=bk: e.tensor_copy(out=wsb[0:NSAMP, :], in_=bk[0:NSAMP, 0:4]), reads=[bk], writes=[wsb])
                P.op("vector", lambda e: e.tensor_scalar(out=lohis[0:NSAMP, 0:4], in0=wsb[0:NSAMP, :], scalar1=0.0, scalar2=-BIG, op0=ALU.is_le, op1=ALU.mult), reads=[wsb], writes=[lohis])
                P.op("vector", lambda e: e.tensor_scalar(out=lohis[0:NSAMP, 4:8], in0=wsb[0:NSAMP, :], scalar1=0.0, scalar2=BIG, op0=ALU.is_gt, op1=ALU.mult), reads=[wsb], writes=[lohis])
                bk = proj_tm(QI, 256, xt, 0, ntok=NSAMP)
                for h in range(4):
                    P.op("vector", lambda e, bk=bk, h=h: e.tensor_scalar(out=qiw[0:NSAMP, h * 64:(h + 1) * 64], in0=bk[0:NSAMP, h * 64:(h + 1) * 64],
                                                                     scalar1=wsb[0:NSAMP, h:h + 1], scalar2=None, op0=ALU.mult), reads=[bk, wsb], writes=[qiw])
                for pc in range(2):
                    bk2 = next_bank()
                    P.op("tensor", lambda e, bk2=bk2, pc=pc: e.transpose(out=bk2[:, 0:NSAMP], in_=qiw[0:NSAMP, pc * 128:(pc + 1) * 128], identity=ident[0:NSAMP, 0:NSAMP]),
                         reads=[qiw, ident], writes=[bk2])
                    evac(qiTs[:, pc, :], bk2[:, 0:NSAMP], [bk2], [qiTs])
                for s in range(4):
                    P.op("vector", lambda e, s=s: e.memset(qiTm[s][:], 0.0), writes=[qiTm[s]])
                    P.op("vector", lambda e, s=s: e.tensor_copy(out=qiTm[s][:, :, 8 * s:8 * s + 8], in_=qiTs[:, :, 8 * s:8 * s + 8]), reads=[qiTs], writes=[qiTm[s]])
                    P.op("vector", lambda e, s=s: e.memset(qbTm[s][:], 0.0), writes=[qbTm[s]])
                    P.op("vector", lambda e, s=s: e.tensor_copy(out=qbTm[s][:, :, 8 * s:8 * s + 8], in_=qbTs[:, :, 8 * s:8 * s + 8]), reads=[qbTs], writes=[qbTm[s]])
            P.op("vector", lambda e: e.tensor_scalar(out=meansTb[:], in0=meansT[:], scalar1=1.0 / 256.0, scalar2=None, op0=ALU.mult),
                 reads=[meansT], writes=[meansTb])
            P.barrier()

        lnr = sb(es, "lnr", [128, 4, D], F32)
        P.dma("sync", lambda e: e.dma_start(out=lnr[:], in_=lnrep), writes=[lnr])
        ablk = sb(es, "ablk", [32, 32, 128], BF16)
        P.dma("gpsimd", lambda e: e.dma_start(out=ablk[:], in_=ablk_d), writes=[ablk])
        gb2 = sb(es, "gb2", [128, 32], F32)
        P.dma("sync", lambda e: e.dma_start(out=gb2[:], in_=bval_d), writes=[gb2])

        def layer_norm(z, nt, gi, out_ap, sm, st6):
            for hf in range(2):
                P.op("vector", lambda e, hf=hf: e.bn_stats(out=st6[0:nt, hf, :], in_=z[0:nt, hf * 512:(hf + 1) * 512]), reads=[z], writes=[st6])
            P.op("vector", lambda e: e.bn_aggr(out=sm[0:nt, 0:2], in_=st6[0:nt].rearrange("p a b -> p (a b)")), reads=[st6], writes=[sm])
            P.op("vector", lambda e: e.tensor_scalar(out=sm[0:nt, 2:3], in0=sm[0:nt, 1:2], scalar1=LN_EPS, scalar2=None, op0=ALU.add), reads=[sm], writes=[sm])
            P.op("scalar", lambda e: e.activation(out=sm[0:nt, 3:4], in_=sm[0:nt, 2:3], func=AF.Sqrt), reads=[sm], writes=[sm])
            P.op("vector", lambda e: e.reciprocal(out=sm[0:nt, 4:5], in_=sm[0:nt, 3:4]), reads=[sm], writes=[sm])
            P.op("vector", lambda e: e.tensor_scalar(out=z[0:nt, :], in0=z[0:nt, :], scalar1=sm[0:nt, 0:1], scalar2=sm[0:nt, 4:5], op0=ALU.subtract, op1=ALU.mult),
                 reads=[z, sm], writes=[z])
            P.op("vector", lambda e: e.tensor_tensor(out=z[0:nt, :], in0=z[0:nt, :], in1=lnr[0:nt, gi, :], op=ALU.mult), reads=[z, lnr], writes=[z])
            return lambda e: e.tensor_tensor(out=out_ap, in0=z[0:nt, :], in1=lnr[0:nt, gi + 1, :], op=ALU.add)

        def indexer(st, NP, nch, emit_scores, lohi_fn, bi_fn, mbT, qoff, btab):
            nk = nch * 512
            sc = sb(st, "sc", [128, nk], F32)
            junk = sb(st, "junk", [128, nk], BF16)
            tmp = [sb(st, "tmpr%d" % i, [128, 512], F32) for i in range(2)]
            mbc = [sb(st, "mbc%d" % i, [128, 512], F32) for i in range(2)]
            sm = sb(st, "sma", [128, 64], F32)
            CMIN, CMAX, LO, HI, MID, CNT, GE, D1, D2 = 0, 20, 40, 41, 42, 43, 44, 45, 46
            tmp_i = 0
            for c in range(nch):
                emit_scores(c)
                scc = sc[0:NP, c * 512:(c + 1) * 512]
                P.op("vector", lambda e, scc=scc: e.tensor_scalar(out=scc, in0=banks[0][0:NP, :], scalar1=lohi_fn(0), scalar2=lohi_fn(4), op0=ALU.max, op1=ALU.min),
                     reads=[banks[0], lohi, lohis], writes=[sc])
                for h in range(1, 4):
                    tp = tmp[tmp_i % 2]
                    tmp_i += 1
                    P.op("vector", lambda e, tp=tp, h=h: e.tensor_scalar(out=tp[0:NP, :], in0=banks[h][0:NP, :], scalar1=lohi_fn(h), scalar2=lohi_fn(4 + h),
                                                                     op0=ALU.max, op1=ALU.min), reads=[banks[h], lohi, lohis], writes=[tp])
                    P.op("vector", lambda e, tp=tp, scc=scc: e.tensor_tensor(out=scc, in0=scc, in1=tp[0:NP, :], op=ALU.add), reads=[sc, tp], writes=[sc])
                P.op("vector", lambda e, scc=scc, c=c: e.tensor_reduce(out=sm[0:NP, CMIN + c:CMIN + c + 1], in_=scc, axis=AX.X, op=ALU.min), reads=[sc], writes=[sm])
                P.op("vector", lambda e, scc=scc, c=c: e.tensor_reduce(out=sm[0:NP, CMAX + c:CMAX + c + 1], in_=scc, axis=AX.X, op=ALU.max), reads=[sc], writes=[sm])
                bi = bi_fn(c)
                P.op("vector", lambda e, scc=scc, c=c, bi=bi: e.scalar_tensor_tensor(out=scc, in0=scc, scalar=float(-EPS_TIE * 512 * c), in1=btab[0:NP, bi, :],
                                                                                 op0=ALU.add, op1=ALU.add), reads=[sc, btab], writes=[sc])
            chk(2)
            P.op("vector", lambda e: e.tensor_reduce(out=sm[0:NP, LO:LO + 1], in_=sm[0:NP, CMIN:CMIN + nch], axis=AX.X, op=ALU.min), reads=[sm], writes=[sm])
            P.op("vector", lambda e: e.tensor_reduce(out=sm[0:NP, HI:HI + 1], in_=sm[0:NP, CMAX:CMAX + nch], axis=AX.X, op=ALU.max), reads=[sm], writes=[sm])
            P.op("vector", lambda e: e.tensor_scalar(out=sm[0:NP, LO:LO + 1], in0=sm[0:NP, LO:LO + 1], scalar1=-1.0, scalar2=None, op0=ALU.add), reads=[sm], writes=[sm])
            P.op("vector", lambda e: e.tensor_scalar(out=sm[0:NP, HI:HI + 1], in0=sm[0:NP, HI:HI + 1], scalar1=1.0, scalar2=None, op0=ALU.add), reads=[sm], writes=[sm])
            steps = [None] * NIT + [float(-EPS_TIE * (nk + 64)), 1e-30] + [None] * NIT2
            for pv in steps:
                if pv is None:
                    P.op("vector", lambda e: e.tensor_scalar(out=sm[0:NP, MID:MID + 1], in0=sm[0:NP, LO:LO + 1], scalar1=sm[0:NP, HI:HI + 1], scalar2=0.5,
                                                             op0=ALU.add, op1=ALU.mult), reads=[sm], writes=[sm])
                else:
                    P.op("vector", lambda e, pv=pv: e.tensor_scalar(out=sm[0:NP, MID:MID + 1], in0=sm[0:NP, LO:LO + 1], scalar1=pv, scalar2=sm[0:NP, HI:HI + 1],
                                                                  op0=ALU.max, op1=ALU.min), reads=[sm], writes=[sm])
                P.op("vector", lambda e: e.tensor_scalar(out=junk[0:NP, 0:nk], in0=sc[0:NP, 0:nk], scalar1=sm[0:NP, MID:MID + 1], scalar2=0.0,
                                                         op0=ALU.is_ge, op1=ALU.add, accum_out=sm[0:NP, CNT:CNT + 1]), reads=[sc, sm], writes=[junk, sm])
                P.op("vector", lambda e: e.tensor_scalar(out=sm[0:NP, GE:GE + 1], in0=sm[0:NP, CNT:CNT + 1], scalar1=255.5, scalar2=None, op0=ALU.is_ge),
                     reads=[sm], writes=[sm])
                P.op("vector", lambda e: e.tensor_tensor(out=sm[0:NP, D1:D1 + 1], in0=sm[0:NP, MID:MID + 1], in1=sm[0:NP, LO:LO + 1], op=ALU.subtract), reads=[sm], writes=[sm])
                P.op("vector", lambda e: e.tensor_tensor(out=sm[0:NP, D2:D2 + 1], in0=sm[0:NP, HI:HI + 1], in1=sm[0:NP, MID:MID + 1], op=ALU.subtract), reads=[sm], writes=[sm])
                P.op("vector", lambda e: e.scalar_tensor_tensor(out=sm[0:NP, LO:LO + 1], in0=sm[0:NP, D1:D1 + 1], scalar=sm[0:NP, GE:GE + 1], in1=sm[0:NP, LO:LO + 1],
                                                                op0=ALU.mult, op1=ALU.add), reads=[sm], writes=[sm])
                P.op("vector", lambda e: e.scalar_tensor_tensor(out=sm[0:NP, HI:HI + 1], in0=sm[0:NP, D2:D2 + 1], scalar=sm[0:NP, GE:GE + 1], in1=sm[0:NP, MID:MID + 1],
                                                                op0=ALU.mult, op1=ALU.add), reads=[sm], writes=[sm])
            chk(3)
            for c in range(nch):
                mb_ = mbc[c % 2]
                P.op("vector", lambda e, mb_=mb_, c=c: e.tensor_scalar(out=mb_[0:NP, :], in0=sc[0:NP, c * 512:(c + 1) * 512], scalar1=sm[0:NP, LO:LO + 1],
                                                                   scalar2=NEGM, op0=ALU.is_lt, op1=ALU.mult), reads=[sc, sm], writes=[mb_])
                bk = banks[4 + (c % 4)]
                for t in range(4):
                    P.op("tensor", lambda e, bk=bk, mb_=mb_, t=t: e.transpose(out=bk[:, t * 128:t * 128 + NP], in_=mb_[0:NP, t * 128:(t + 1) * 128],
                                                                         identity=ident[0:NP, 0:NP]), reads=[mb_, ident], writes=[bk])
                evac(mbT[:, 4 * c:4 * c + 4, qoff:qoff + NP], bk[:, :].rearrange("p (t q) -> p t q", t=4)[:, :, 0:NP], [bk], [mbT], eng="scalar")

        def moba_gate(st, NP, emit_gate, bias_ap, ubo, mbBT, qoff):
            gsb = sb(st, "gsb", [128, 8, 32], F32)
            m8 = sb(st, "m8", [128, 8, 8], F32)
            thr = sb(st, "thr", [128, 8], F32)
            mbB = sb(st, "mbB", [128, 8, 32], F32)
            bk = banks[0]
            emit_gate(bk)
            for h in range(8):
                P.op("vector", lambda e, h=h: e.tensor_tensor(out=gsb[0:NP, h, :], in0=bk[0:NP, h * 32:(h + 1) * 32], in1=bias_ap, op=ALU.add),
                     reads=[bk, pastb_t], writes=[gsb])
            for h in range(8):
                P.op("vector", lambda e, h=h: e.max(out=m8[0:NP, h, :], in_=gsb[0:NP, h, :]), reads=[gsb], writes=[m8])
            P.op("vector", lambda e: e.tensor_scalar(out=thr[0:NP, :], in0=m8[0:NP, :, 2], scalar1=-1e29, scalar2=None, op0=ALU.max), reads=[m8], writes=[thr])
            for h in range(8):
                P.op("vector", lambda e, h=h: e.tensor_scalar(out=mbB[0:NP, h, :], in0=gsb[0:NP, h, :], scalar1=thr[0:NP, h:h + 1], scalar2=NEGM,
                                                          op0=ALU.is_lt, op1=ALU.mult), reads=[gsb, thr], writes=[mbB])
            if ubo is not None:
                P.op("vector", lambda e: e.memset(mbB[0:NP, :, ubo:ubo + 1], 0.0), writes=[mbB])
            for g in range(2):
                bk2 = banks[4 + g]
                for hh in range(4):
                    h = 4 * g + hh
                    P.op("tensor", lambda e, bk2=bk2, h=h, hh=hh: e.transpose(out=bk2[0:32, hh * 128:hh * 128 + NP], in_=mbB[0:NP, h, :], identity=ident[0:NP, 0:NP]),
                         reads=[mbB, ident], writes=[bk2])
                evac(mbBT[0:32, 4 * g:4 * g + 4, qoff:qoff + NP], bk2[0:32, :].rearrange("p (h q) -> p h q", h=4)[:, :, 0:NP], [bk2], [mbBT], eng="vector")

        class AttBufs:
            def __init__(self, st):
                self.kTs = [sb(st, "kTs%d" % i, [128, 2048], BF16) for i in range(2)]
                self.vss = [sb(st, "vss%d" % i, [128, 16, 130], BF16) for i in range(2)]
                self.PTs = [sb(st, "PT%d" % i, [128, 512], BF16) for i in range(4)]
                self.wts = [sb(st, "wts%d" % i, [128, 1024], BF16) for i in range(4)]
                self.rd = sb(st, "rd", [128, 512], F32)
                self.rb = sb(st, "rb", [128, 512], F32)
                self.kv_i = 0
                self.pt_i = 0
                self.wt_i = 0
                self.lb_i = 0

        def attend(A, kT_fn, v_fn, q_fn, o_fn, hoff, ntl, NQ, diag_fn, mask_fn):
            nkb = (ntl + 15) // 16
            for p in range(4):
                wth = []
                for hh in range(2):
                    w_ = A.wts[A.wt_i % 4]
                    A.wt_i += 1
                    P.dma("gpsimd", lambda e, w_=w_, hd=hoff + 2 * p + hh: e.dma_start(out=w_[:], in_=wtab_d[:, hd, :]), writes=[w_])
                    wth.append(w_)
                for kb in range(nkb):
                    nt_ = min(16, ntl - 16 * kb)
                    kt = A.kTs[A.kv_i % 2]
                    vs = A.vss[A.kv_i % 2]
                    A.kv_i += 1
                    P.dma("sync", lambda e, kt=kt, p=p, kb=kb, nt_=nt_: e.dma_start(out=kt[:, 0:nt_ * 128], in_=kT_fn(p)[:, kb * 2048:kb * 2048 + nt_ * 128]), writes=[kt])
                    P.dma("sync", lambda e, vs=vs, p=p, kb=kb, nt_=nt_: e.dma_start(
                        out=vs[:, 0:nt_, :], in_=v_fn()[kb * 16:kb * 16 + nt_, :, p * 130:(p + 1) * 130].rearrange("t k f -> k t f")), writes=[vs])
                    for tl in range(nt_):
                        u = kb * 16 + tl
                        x0 = diag_fn(u)
                        diag = x0 is not None
                        for hh in range(2):
                            h = 2 * p + hh
                            r0 = 64 * hh
                            L = banks[A.lb_i % 4]
                            A.lb_i += 1
                            OT = banks[4 + hh]
                            mk = mask_fn(u, h)
                            last1 = (mk is None) and (not diag)
                            P.op("tensor", lambda e, L=L, kt=kt, r0=r0, tl=tl, p=p, last1=last1: e.matmul(
                                L[:, 0:NQ], lhsT=kt[r0:r0 + 64, tl * 128:(tl + 1) * 128], rhs=q_fn(r0, p), start=True, stop=last1),
                                reads=[kt, qTall], writes=[L])
                            if mk is not None:
                                lh, rh = mk
                                P.op("tensor", lambda e, L=L, lh=lh, rh=rh, diag=diag: e.matmul(L[:, 0:NQ], lhsT=lh, rhs=rh, start=False, stop=(not diag)),
                                     reads=[identb, ablk, maskall], writes=[L])
                            if diag:
                                P.op("tensor", lambda e, L=L, w_=wth[hh], x0=x0: e.matmul(L[:, 0:NQ], lhsT=identb[:], rhs=w_[:, x0:x0 + NQ], start=False, stop=True),
                                     reads=[identb, wth[hh]], writes=[L])
                            PT = A.PTs[A.pt_i % 4]
                            A.pt_i += 1
                            if diag:
                                P.op("scalar", lambda e, PT=PT, L=L: e.activation(out=PT[:, 0:NQ], in_=L[:, 0:NQ], func=AF.Exp), reads=[L], writes=[PT])
                            else:
                                P.op("scalar", lambda e, PT=PT, L=L, hd=hoff + h: e.activation(out=PT[:, 0:NQ], in_=L[:, 0:NQ], func=AF.Exp, bias=b31[:, hd:hd + 1]),
                                     reads=[L, b31], writes=[PT])
                            P.op("tensor", lambda e, OT=OT, vs=vs, tl=tl, hh=hh, PT=PT, u=u: e.matmul(
                                OT[0:65, 0:NQ], lhsT=vs[:, tl, hh * 65:(hh + 1) * 65], rhs=PT[:, 0:NQ], start=(u == 0), stop=(u == ntl - 1)),
                                reads=[vs, PT], writes=[OT])
                for hh in range(2):
                    h = 2 * p + hh
                    OT = banks[4 + hh]
                    P.op("vector", lambda e, OT=OT: e.reciprocal(out=A.rd[64:65, 0:NQ], in_=OT[64:65, 0:NQ]), reads=[OT], writes=[A.rd])
                    bx = banks[A.lb_i % 4]
                    A.lb_i += 1
                    P.op("tensor", lambda e, bx=bx: e.matmul(bx[0:64, 0:NQ], lhsT=ones_t[64:65, 0:64], rhs=A.rd[64:65, 0:NQ], start=True, stop=True),
                         reads=[ones_t, A.rd], writes=[bx])
                    P.op("scalar", lambda e, bx=bx: e.activation(out=A.rb[0:64, 0:NQ], in_=bx[0:64, 0:NQ], func=AF.Copy), reads=[bx], writes=[A.rb])
                    P.op("vector", lambda e, OT=OT, h=h: e.tensor_tensor(out=o_fn(h), in0=OT[0:64, 0:NQ], in1=A.rb[0:64, 0:NQ], op=ALU.mult),
                         reads=[OT, A.rb], writes=[oall])

        def phase3(NT, tiles, oaT, obT, sg_fn, x_fn, y_fn):
            with ExitStack() as s3:
                nti = len(tiles)
                hres = sb(s3, "hres", [128, nti, D], F32)
                hT = sb(s3, "hT", [128, 8, 512], BF16)
                sm3 = sb(s3, "sm3", [128, 8], F32)
                st6 = sb(s3, "st6", [128, 2, 6], F32)
                z = sb(s3, "z", [128, D], F32)
                with ExitStack() as s3a:
                    wau = sb(s3a, "wau", [64, 8, D], BF16)
                    wbu = sb(s3a, "wbu", [64, 8, D], BF16)
                    wo = sb(s3a, "wo", [128, 8, D], BF16)
                    P.dma("gpsimd", lambda e: e.dma_start(out=wau[:], in_=w_a_up.rearrange("(h d) n -> d h n", d=64)), writes=[wau])
                    P.dma("gpsimd", lambda e: e.dma_start(out=wbu[:], in_=w_b_up.rearrange("(h d) n -> d h n", d=64)), writes=[wbu])
                    P.dma("gpsimd", lambda e: e.dma_start(out=wo[:], in_=w_out.rearrange("(c p) n -> p c n", p=128)), writes=[wo])
                    mixT = sb(s3a, "mixT", [128, 8, 512], BF16)
                    sga = [sb(s3a, "sga%d" % i, [128, 512], F32) for i in range(2)]
                    sgb = [sb(s3a, "sgb%d" % i, [128, 512], F32) for i in range(2)]
                    ma = sb(s3a, "ma", [128, 512], F32)
                    xre = [sb(s3a, "xre%d" % i, [128, D], F32) for i in range(2)]
                    for fc in range(8):
                        sa_ = sga[fc % 2]
                        sb_ = sgb[fc % 2]
                        sg_fn(0, fc, sa_)
                        sg_fn(1, fc, sb_)
                        bA = banks[(2 * fc) % 8]
                        bB = banks[(2 * fc + 1) % 8]
                        for h in range(8):
                            P.op("tensor", lambda e, bA=bA, h=h, fc=fc: e.matmul(bA[:, 0:NT], lhsT=wau[0:64, h, fc * 128:(fc + 1) * 128], rhs=oaT[0:64, h, 0:NT],
                                                                             start=(h == 0), stop=(h == 7)), reads=[wau, oaT], writes=[bA])
                        for h in range(8):
                            P.op("tensor", lambda e, bB=bB, h=h, fc=fc: e.matmul(bB[:, 0:NT], lhsT=wbu[0:64, h, fc * 128:(fc + 1) * 128], rhs=obT[0:64, h, 0:NT],
                                                                             start=(h == 0), stop=(h == 7)), reads=[wbu, obT], writes=[bB])
                        P.op("vector", lambda e, bA=bA, sa_=sa_: e.tensor_tensor(out=ma[:, 0:NT], in0=bA[:, 0:NT], in1=sa_[:, 0:NT], op=ALU.mult), reads=[bA, sa_], writes=[ma])
                        P.op("vector", lambda e, bB=bB, sb_=sb_: e.tensor_tensor(out=sb_[:, 0:NT], in0=bB[:, 0:NT], in1=sb_[:, 0:NT], op=ALU.mult), reads=[bB, sb_], writes=[sb_])
                        P.op("vector", lambda e, sb_=sb_, fc=fc: e.tensor_tensor(out=mixT[:, fc, 0:NT], in0=ma[:, 0:NT], in1=sb_[:, 0:NT], op=ALU.add), reads=[ma, sb_], writes=[mixT])
                    for t, (t0, nt) in enumerate(tiles):
                        xr = xre[t % 2]
                        x_fn(t, xr)
                        for hf in range(2):
                            bk = banks[(2 * t + hf) % 8]
                            for c in range(8):
                                P.op("tensor", lambda e, bk=bk, c=c, t0=t0, nt=nt, hf=hf: e.matmul(bk[0:nt, :], lhsT=mixT[:, c, t0:t0 + nt], rhs=wo[:, c, hf * 512:(hf + 1) * 512],
                                                                                         start=(c == 0), stop=(c == 7)), reads=[mixT, wo], writes=[bk])
                            P.op("vector", lambda e, bk=bk, xr=xr, hf=hf, nt=nt: e.scalar_tensor_tensor(out=z[0:nt, hf * 512:(hf + 1) * 512], in0=xr[0:nt, hf * 512:(hf + 1) * 512],
                                                                                                scalar=float(ALPHA), in1=bk[0:nt, :], op0=ALU.mult, op1=ALU.add),
                                 reads=[bk, xr], writes=[z])
                        fin = layer_norm(z, nt, 0, hres[0:nt, t, :], sm3, st6)
                        P.op("vector", fin, reads=[z, lnr], writes=[hres])
                        for g in range(2):
                            bk = banks[(g + 2 * t) % 8]
                            for cc in range(4):
                                c = 4 * g + cc
                                P.op("tensor", lambda e, bk=bk, cc=cc, c=c, t=t, nt=nt: e.transpose(out=bk[:, cc * 128:cc * 128 + nt], in_=hres[0:nt, t, c * 128:(c + 1) * 128],
                                                                                          identity=ident[0:nt, 0:nt]), reads=[hres, ident], writes=[bk])
                            evac(hT[:, 4 * g:4 * g + 4, t0:t0 + nt], bk[:, :].rearrange("p (c q) -> p c q", c=4)[:, :, 0:nt], [bk], [hT])
                    P.barrier()
                chk(9)
                with ExitStack() as s3b:
                    uT = sb(s3b, "uT", [128, 32, 512], BF16)
                    wf1 = [sb(s3b, "wf1_%d" % i, [128, 8, 512], BF16) for i in range(2)]
                    wf2 = [sb(s3b, "wf2_%d" % i, [128, 8, D], BF16) for i in range(2)]
                    rl = [sb(s3b, "rl%d" % i, [128, 512], F32) for i in range(2)]
                    yst = [sb(s3b, "yst%d" % i, [128, D], F32) for i in range(2)]
                    rl_i = 0
                    for fb in range(8):
                        w1 = wf1[fb % 2]
                        P.dma("gpsimd", lambda e, w1=w1, fb=fb: e.dma_start(out=w1[:], in_=w_ff1[:, fb * 512:(fb + 1) * 512].rearrange("(c p) n -> p c n", p=128)), writes=[w1])
                        for fc in range(4):
                            bk = banks[(fb * 4 + fc) % 8]
                            for c in range(8):
                                P.op("tensor", lambda e, bk=bk, w1=w1, c=c, fc=fc: e.matmul(bk[:, 0:NT], lhsT=w1[:, c, fc * 128:(fc + 1) * 128], rhs=hT[:, c, 0:NT],
                                                                                    start=(c == 0), stop=(c == 7)), reads=[w1, hT], writes=[bk])
                            r_ = rl[rl_i % 2]
                            rl_i += 1
                            P.op("scalar", lambda e, bk=bk, r_=r_: e.activation(out=r_[:, 0:NT], in_=bk[:, 0:NT], func=AF.Relu), reads=[bk], writes=[r_])
                            P.op("vector", lambda e, r_=r_, fb=fb, fc=fc: e.tensor_tensor(out=uT[:, fb * 4 + fc, 0:NT], in0=r_[:, 0:NT], in1=r_[:, 0:NT], op=ALU.mult),
                                 reads=[r_], writes=[uT])
                    for blk in range(4):
                        w2 = wf2[blk % 2]
                        P.dma("gpsimd", lambda e, w2=w2, blk=blk: e.dma_start(out=w2[:], in_=w_ff2[blk * 1024:(blk + 1) * 1024, :].rearrange("(c p) n -> p c n", p=128)), writes=[w2])
                        for cc in range(8):
                            ch = blk * 8 + cc
                            for t, (t0, nt) in enumerate(tiles):
                                for hf in range(2):
                                    bk = banks[2 * t + hf]
                                    P.op("tensor", lambda e, bk=bk, w2=w2, cc=cc, ch=ch, t0=t0, nt=nt, hf=hf: e.matmul(
                                        bk[0:nt, :], lhsT=uT[:, ch, t0:t0 + nt], rhs=w2[:, cc, hf * 512:(hf + 1) * 512], start=(ch == 0), stop=(ch == 31)),
                                        reads=[uT, w2], writes=[bk])
                    for t, (t0, nt) in enumerate(tiles):
                        for hf in range(2):
                            bk = banks[2 * t + hf]
                            P.op("vector", lambda e, bk=bk, t=t, hf=hf, nt=nt: e.scalar_tensor_tensor(out=z[0:nt, hf * 512:(hf + 1) * 512], in0=hres[0:nt, t, hf * 512:(hf + 1) * 512],
                                                                                              scalar=float(ALPHA), in1=bk[0:nt, :], op0=ALU.mult, op1=ALU.add),
                                 reads=[bk, hres], writes=[z])
                        ys = yst[t % 2]
                        fin = layer_norm(z, nt, 2, ys[0:nt, :], sm3, st6)
                        P.op("vector", fin, reads=[z, lnr], writes=[ys])
                        out_toks.append(y_fn(t, ys))
                    P.barrier()


        def qgroup(mq):
            nch = 4 * mq + 4
            ntl = 16 * mq + 16
            with ExitStack() as sq:
                oaT = sb(sq, "oaT", [128, 8, 512], BF16)
                obT = sb(sq, "obT", [128, 8, 512], BF16)
                s2 = ExitStack()
                sq.callback(s2.close)
                mbT = sb(s2, "mbT", [128, 64, 512], BF16)
                qaT_g = sb(s2, "qaT_g", [128, 4, 512], BF16)
                qbT_g = sb(s2, "qbT_g", [128, 4, 512], BF16)
                qiT_g = sb(s2, "qiT_g", [128, 2, 512], BF16)
                P.dma("sync", lambda e: e.dma_start(out=qaT_g[:], in_=qaT_d[mq]), writes=[qaT_g, qTall])
                P.dma("sync", lambda e: e.dma_start(out=qbT_g[:], in_=qbT_d[mq]), writes=[qbT_g, qTall])
                P.dma("sync", lambda e: e.dma_start(out=qiT_g[:], in_=qiT_d[mq]), writes=[qiT_g])
                chk(1)
                with ExitStack() as sa:
                    btab = sb(sa, "btab", [128, 9, 512], F32)
                    P.dma("sync", lambda e: e.dma_start(out=btab[:], in_=btab_d), writes=[btab])
                    kib = [sb(sa, "kib%d" % i, [128, 2048], BF16) for i in range(2)]
                    for i in range(4):
                        qt = 4 * mq + i

                        def emit_scores(c, i=i):
                            kb_ = kib[(c // 4) % 2]
                            if c % 4 == 0:
                                for hf in range(2):
                                    P.dma("sync", lambda e, kb_=kb_, c=c, hf=hf: e.dma_start(
                                        out=kb_[64 * hf:64 * hf + 64, :], in_=kiT_d[:, (c // 4) * 2048:(c // 4 + 1) * 2048]), writes=[kb_])
                            for h in range(4):
                                r0 = 64 * (h % 2)
                                P.op("tensor", lambda e, h=h, r0=r0, c=c, kb_=kb_: e.matmul(
                                    banks[h][:, :], lhsT=qiT_g[r0:r0 + 64, h // 2, i * 128:(i + 1) * 128],
                                    rhs=kb_[r0:r0 + 64, (c % 4) * 512:(c % 4 + 1) * 512], start=True, stop=True),
                                    reads=[qiT_g, kb_], writes=[banks[h]])
                        with ExitStack() as si:
                            indexer(si, 128, nch, emit_scores, lambda k, qt=qt: lohi[:, qt, k:k + 1],
                                    lambda c, i=i: (c if c < 3 else (4 + i if c == nch - 1 else 3)), mbT, i * 128, btab)
                        P.op("vector", lambda e: e.memset(ones_t[0:1, 0:1], 1.0), reads=[mbT], writes=[maskall, ones_t])
                    chk(4)
                    P.barrier()
                chk(5)
                with ExitStack() as sbb:
                    A = AttBufs(sbb)
                    mbBT = sb(sbb, "mbBT", [32, 8, 512], BF16)
                    pastb = sb(sbb, "pastb", [128, 2, 32], F32)
                    P.dma("sync", lambda e: e.dma_start(out=pastb[:], in_=pastb_d[mq]), writes=[pastb])
                    for a in range(2):
                        P.op("vector", lambda e, a=a: e.tensor_tensor(out=pastb[:, a, :], in0=pastb[:, a, :], in1=gb2[:], op=ALU.add),
                             reads=[pastb, gb2], writes=[pastb, pastb_t])
                    for i in range(4):
                        def emit_gate(bk, i=i):
                            for h in range(8):
                                r0 = 64 * (h % 2)
                                P.op("tensor", lambda e, h=h, r0=r0: e.matmul(bk[:, h * 32:(h + 1) * 32], lhsT=qbT_g[r0:r0 + 64, h // 2, i * 128:(i + 1) * 128],
                                                                          rhs=meansTb[r0:r0 + 64, h // 2, :], start=True, stop=True),
                                     reads=[qbT_g, meansTb], writes=[bk])
                        with ExitStack() as sg_:
                            moba_gate(sg_, 128, emit_gate, pastb[:, i // 2, :], 8 * mq + 6 + i // 2, mbBT, i * 128)
                    P.op("vector", lambda e: e.memset(ones_t[0:1, 0:1], 1.0), reads=[mbBT], writes=[maskall, ones_t])
                    chk(6)

                    def diag_fn(u):
                        return (512 - 128 * (u - (ntl - 5))) if u >= ntl - 5 else None
                    if DBG_LEVEL >= 7:
                        attend(A, lambda p: kaT_d[p], lambda: va_d, lambda r0, p: qaT_g[r0:r0 + 64, p, :], lambda h: oaT[0:64, h, :], 0, ntl, 512, diag_fn,
                               lambda u, h: (identb[:], mbT[:, u, :]))
                    chk(7)
                    if DBG_LEVEL >= 8:
                        attend(A, lambda p: kbT_d[p], lambda: vb_d, lambda r0, p: qbT_g[r0:r0 + 64, p, :], lambda h: obT[0:64, h, :], 8, ntl, 512, diag_fn,
                               lambda u, h: (ablk[0:32, u // 2, :], mbBT[0:32, h, :]))
                    chk(8)
                    P.op("vector", lambda e: e.memset(ones_t[0:1, 0:1], 1.0), reads=[oall], writes=[oaT, obT, ones_t])
                    P.barrier()
                s2.close()

                def sg_fn(which, fc, dst):
                    src = sga_d if which == 0 else sgb_d
                    P.dma("sync", lambda e: e.dma_start(out=dst[:], in_=src[mq, fc]), writes=[dst])

                def x_fn(t, xr):
                    r0 = (16 * mq + 12 + t) * 128
                    P.dma("sync", lambda e: e.dma_start(out=xr[:], in_=xs[r0:r0 + 128, :]), writes=[xr])

                def y_fn(t, ys):
                    r0 = (4 * mq + t) * 128
                    return P.dma("sync", lambda e: e.dma_start(out=y_p[r0:r0 + 128, :], in_=ys[:]), reads=[ys])
                phase3(512, [(0, 128), (128, 128), (256, 128), (384, 128)], oaT, obT, sg_fn, x_fn, y_fn)

        for mq_ in range(DBG_NQG):
            try:
                qgroup(mq_)
            except _Stop:
                pass
        def sample_group():
            with ExitStack() as ss1:
                ptr = sb(ss1, "ptr", [128, 256], I32)
                iot = sb(ss1, "iot", [128, 1], I32)
                idx = sb(ss1, "idx", [128, 256], I32)
                P.dma("sync", lambda e: e.dma_start(out=ptr[:], in_=ptrep_d), writes=[ptr])
                P.dma("sync", lambda e: e.dma_start(out=iot[:], in_=iot_d), writes=[iot])
                P.op("vector", lambda e: e.tensor_scalar(out=idx[:], in0=ptr[:], scalar1=128.0, scalar2=iot[:, 0:1], op0=ALU.mult, op1=ALU.add),
                     reads=[ptr, iot], writes=[idx])
                gd = [sb(ss1, "gd%d" % i, [128, 1088], F32) for i in range(8)]
                gm = [sb(ss1, "gm%d" % i, [128, 1024], F32) for i in range(8)]
                kst = [sb(ss1, "kst%d" % i, [128, 512], BF16) for i in range(4)]
                vst = [sb(ss1, "vst%d" % i, [128, 8, 65], BF16) for i in range(4)]
                msum = sb(ss1, "msum", [128, 4, 32], F32)
                for v in vst:
                    P.op("vector", lambda e, v=v: e.memset(v[:], 1.0), writes=[v])
                ki_ = 0
                vi_ = 0
                for s in range(DBG_NSEQ):
                    for blk in range(16):
                        gds = []
                        gms = []
                        for t in range(4):
                            pg = blk * 4 + t
                            col = s * 64 + pg
                            g1 = gd[(blk % 2) * 4 + t]
                            g2 = gm[(blk % 2) * 4 + t]
                            P.dma("gpsimd", lambda e, g1=g1, col=col: e.indirect_dma_start(out=g1[:, :], out_offset=None, in_=cdsa[:, :],
                                                                                         in_offset=bass.IndirectOffsetOnAxis(ap=idx[:, col:col + 1], axis=0)),
                                  reads=[idx], writes=[g1])
                            P.dma("gpsimd", lambda e, g2=g2, col=col: e.indirect_dma_start(out=g2[:, :], out_offset=None, in_=cmoba[:, :],
                                                                                         in_offset=bass.IndirectOffsetOnAxis(ap=idx[:, col:col + 1], axis=0)),
                                  reads=[idx], writes=[g2])
                            gds.append(g1)
                            gms.append(g2)
                        for (srcs, c0, M, dst, cidx) in ([(gds, 128 * c, 128, skaT_d[s, c], None) for c in range(4)] + [(gds, 1024, 64, skiT_d[s], None)]
                                                        + [(gms, 128 * c, 128, skbT_d[s, c], c) for c in range(4)]):
                            bk = next_bank()
                            for t in range(4):
                                P.op("tensor", lambda e, bk=bk, g_=srcs[t], c0=c0, M=M, t=t: e.transpose(out=bk[0:M, t * 128:(t + 1) * 128], in_=g_[:, c0:c0 + M], identity=ident[:]),
                                     reads=[srcs[t], ident], writes=[bk])
                            if cidx is not None:
                                for hb in range(2):
                                    P.op("vector", lambda e, bk=bk, cidx=cidx, blk=blk, hb=hb: e.tensor_reduce(
                                        out=msum[:, cidx, 2 * blk + hb:2 * blk + hb + 1], in_=bk[:, hb * 256:(hb + 1) * 256], axis=AX.X, op=ALU.add), reads=[bk], writes=[msum])
                            st = kst[ki_ % 4]
                            ki_ += 1
                            evac(st[0:M, :], bk[0:M, :], [bk], [st])
                            P.dma("sync", lambda e, st=st, dst=dst, M=M, blk=blk: e.dma_start(out=dst[0:M, blk * 512:(blk + 1) * 512], in_=st[0:M, :]), reads=[st], writes=[sscr])
                        for t in range(4):
                            pg = blk * 4 + t
                            for (g_, dstd) in ((gds[t], sva_d), (gms[t], svb_d)):
                                vs_ = vst[vi_ % 4]
                                vi_ += 1
                                P.op("gpsimd", lambda e, vs_=vs_, g_=g_: e.tensor_copy(out=vs_[:, :, 0:64], in_=g_[:, 512:1024].rearrange("p (h d) -> p h d", h=8)),
                                     reads=[g_], writes=[vs_])
                                P.dma("sync", lambda e, vs_=vs_, dstd=dstd, s=s, pg=pg: e.dma_start(out=dstd[s, pg], in_=vs_[:].rearrange("p h d -> p (h d)")), reads=[vs_], writes=[sscr])
                    P.op("vector", lambda e, s=s: e.tensor_scalar(out=smeansTb[s][:], in0=msum[:], scalar1=1.0 / 256.0, scalar2=None, op0=ALU.mult),
                         reads=[msum], writes=[smeansTb[s]])
                P.barrier()
            chk(11)
            with ExitStack() as sq:
                oaTs = sb(sq, "oaTs", [128, 8, 512], BF16)
                obTs = sb(sq, "obTs", [128, 8, 512], BF16)
                s2 = ExitStack()
                sq.callback(s2.close)
                mbTs = sb(s2, "mbTs", [128, 68, NSAMP], BF16)
                with ExitStack() as sa:
                    btab = sb(sa, "btab", [128, 9, 512], F32)
                    P.dma("sync", lambda e: e.dma_start(out=btab[:], in_=btab_d), writes=[btab])
                    kibs = [sb(sa, "kibs%d" % i, [128, 2048], BF16) for i in range(4)]

                    def emit_scores(c):
                        if c % 4 == 0:
                            wd_ = min(2048, 8704 - (c // 4) * 2048)
                            for s in range(4):
                                for hf in range(2):
                                    P.dma("sync", lambda e, s=s, c=c, hf=hf, wd_=wd_: e.dma_start(
                                        out=kibs[s][64 * hf:64 * hf + 64, 0:wd_], in_=skiT_d[s][:, (c // 4) * 2048:(c // 4) * 2048 + wd_]), reads=[sscr], writes=[kibs[s]])
                        for h in range(4):
                            r0 = 64 * (h % 2)
                            for s in range(4):
                                P.op("tensor", lambda e, h=h, r0=r0, c=c, s=s: e.matmul(
                                    banks[h][0:NSAMP, :], lhsT=qiTm[s][r0:r0 + 64, h // 2, :], rhs=kibs[s][r0:r0 + 64, (c % 4) * 512:(c % 4 + 1) * 512],
                                    start=(s == 0), stop=(s == 3)), reads=[qiTm[s], kibs[s]], writes=[banks[h]])
                    with ExitStack() as si:
                        indexer(si, NSAMP, 17, emit_scores, lambda k: lohis[0:NSAMP, k:k + 1], lambda c: (8 if c == 16 else 3), mbTs, 0, btab)
                    P.op("vector", lambda e: e.memset(ones_t[0:1, 0:1], 1.0), reads=[mbTs], writes=[maskall, ones_t])
                    P.barrier()
                chk(12)
                with ExitStack() as sbb:
                    A = AttBufs(sbb)
                    mbBTs = sb(sbb, "mbBTs", [32, 8, NSAMP], BF16)
                    zb = sb(sbb, "zb", [128, 32], F32)
                    P.op("vector", lambda e: e.memset(zb[:], 0.0), writes=[zb, pastb_t])

                    def emit_gate(bk):
                        for h in range(8):
                            r0 = 64 * (h % 2)
                            for s in range(4):
                                P.op("tensor", lambda e, h=h, r0=r0, s=s: e.matmul(bk[0:NSAMP, h * 32:(h + 1) * 32], lhsT=qbTm[s][r0:r0 + 64, h // 2, :],
                                                                               rhs=smeansTb[s][r0:r0 + 64, h // 2, :], start=(s == 0), stop=(s == 3)),
                                     reads=[qbTm[s], smeansTb[s]], writes=[bk])
                    with ExitStack() as sg_:
                        moba_gate(sg_, NSAMP, emit_gate, zb[0:NSAMP, :], None, mbBTs, 0)
                    P.op("vector", lambda e: e.memset(ones_t[0:1, 0:1], 1.0), reads=[mbBTs, sscr], writes=[maskall, ones_t])
                    chk(13)

                    def diag_fn(u):
                        return 512 if u == 63 else (384 if u == 64 else None)
                    for s in range(DBG_NSEQ):
                        q0 = 8 * s
                        attend(A, lambda p, s=s: skaT_d[s, p], lambda s=s: sva_d[s], lambda r0, p, q0=q0: qaTs[r0:r0 + 64, p, q0:q0 + 8],
                               lambda h, q0=q0: oaTs[0:64, h, q0:q0 + 8], 0, 65, 8, diag_fn, lambda u, h, q0=q0: (identb[:], mbTs[:, u, q0:q0 + 8]))
                        attend(A, lambda p, s=s: skbT_d[s, p], lambda s=s: svb_d[s], lambda r0, p, q0=q0: qbTs[r0:r0 + 64, p, q0:q0 + 8],
                               lambda h, q0=q0: obTs[0:64, h, q0:q0 + 8], 8, 65, 8, diag_fn,
                               lambda u, h, q0=q0: ((ablk[0:32, u // 2, :], mbBTs[0:32, h, q0:q0 + 8]) if u < 64 else None))
                    P.op("vector", lambda e: e.memset(ones_t[0:1, 0:1], 1.0), reads=[oall], writes=[oaTs, obTs, ones_t])
                    P.barrier()
                s2.close()
                chk(14)

                def sg_fn(which, fc, dst):
                    src = sgas if which == 0 else sgbs
                    P.op("vector", lambda e: e.tensor_copy(out=dst[:, 0:NSAMP], in_=src[:, fc, :]), reads=[src], writes=[dst])

                def x_fn(t, xr):
                    P.dma("sync", lambda e: e.dma_start(out=xr[0:NSAMP, :], in_=xsm[:, :]), writes=[xr])

                def y_fn(t, ys):
                    return P.dma("sync", lambda e: e.dma_start(out=y_s[:, :], in_=ys[0:NSAMP, :]), reads=[ys])
                phase3(NSAMP, [(0, NSAMP)], oaTs, obTs, sg_fn, x_fn, y_fn)

        if DBG_SAMPLE:
            sample_group()
        P.barrier()
        for e in ["sync"]:
            waits = P._waits(e, dict([t for t in out_toks if t is not None]))
            if waits:
                P.ops[e].append((waits, None, None, 0))
        P.emit()
    return nc


def host_consts(rel_bias):
    ki = np.arange(128)[:, None]
    x = np.arange(1024)[None, :]
    d = x - ki - 384
    bkt = t5_bucket_np(d)
    wt = np.empty((128, 16, 1024), np.float32)
    for h in range(16):
        wt[:, h, :] = np.where(d >= 0, rel_bias[bkt, h], np.float32(NEGM))
    b31 = np.broadcast_to(rel_bias[31][None, :], (128, 16)).astype(np.float32).copy()
    qi = np.arange(128)[:, None]
    kk = np.arange(512)[None, :]
    cm = np.empty((128, 4, 512), np.float32)
    for i in range(4):
        cm[:, i, :] = np.where(kk <= i * 128 + qi, 0.0, -BIG)
    pertb = np.broadcast_to((-EPS_TIE * np.arange(512, dtype=np.float64)).astype(np.float32)[None, :], (128, 512)).copy()
    ablk = np.zeros((32, 32, 128), np.float32)
    for u in range(32):
        ablk[u, u, :] = 1.0
    pastb = np.zeros((4, 128, 2, 32), np.float32)
    for mq in range(4):
        for a in range(2):
            pastb[mq, :, a, 8 * mq + 6 + a:] = -BIG
    return wt, b31, cm, pertb, ablk, pastb


def core_consts(j, cm, pertb):
    nph = (12 - 4 * j) * 128
    phb = np.zeros((128, 1536), np.float32)
    phb[:, :nph] = -BIG
    btab = np.empty((128, 9, 512), np.float32)
    kk = np.arange(512)[None, :]
    btab[:, 8, :] = pertb + np.where(kk <= (np.arange(128)[:, None] % 8), 0.0, -BIG).astype(np.float32)
    for c in range(3):
        btab[:, c, :] = pertb + phb[:, c * 512:(c + 1) * 512]
    btab[:, 3, :] = pertb
    for i in range(4):
        btab[:, 4 + i, :] = pertb + cm[:, i, :]
    bval = np.zeros((128, 32), np.float32)
    bval[:, :nph // 256] = -BIG
    return btab, bval, nph


def kernel(x_prompt, x_sample, cache_dsa, cache_moba, page_table, w_in, rel_bias, w_a_up, w_b_up,
           w_out, ln1_g, ln1_b, w_ff1, w_ff2, ln2_g, ln2_b):
    x_prompt = np.asarray(x_prompt, np.float32)
    x_sample = np.asarray(x_sample, np.float32)
    rel_bias = np.asarray(rel_bias, np.float32)
    wt, b31, cm, pertb, ablk, pastb = host_consts(rel_bias)
    lnrep = np.stack([np.broadcast_to(np.asarray(a, np.float32)[0][None, :], (128, D)) for a in (ln1_g, ln1_b, ln2_g, ln2_b)], axis=1).copy()
    ident = np.eye(128, dtype=np.float32)
    common = dict(w_in=np.ascontiguousarray(np.asarray(w_in, np.float32)[0]),
                  w_a_up=np.ascontiguousarray(np.asarray(w_a_up, np.float32)[0]),
                  w_b_up=np.ascontiguousarray(np.asarray(w_b_up, np.float32)[0]),
                  w_out=np.ascontiguousarray(np.asarray(w_out, np.float32)[0]),
                  w_ff1=np.ascontiguousarray(np.asarray(w_ff1, np.float32)[0]),
                  w_ff2=np.ascontiguousarray(np.asarray(w_ff2, np.float32)[0]),
                  lnrep=lnrep, ident=ident, wtab=wt, b31=b31, ablk=ablk, pastb=pastb)
    page_table = np.asarray(page_table, np.int32)
    iot = np.arange(128, dtype=np.int32)[:, None].copy()
    cdsa = np.asarray(cache_dsa, np.float32)[0].reshape(2560 * 128, 1088)
    cmoba = np.asarray(cache_moba, np.float32)[0].reshape(2560 * 128, 1024)
    in_maps = []
    for c in range(8):
        b, j = c // 4, c % 4
        btab, bval, nph = core_consts(j, cm, pertb)
        xsl = np.zeros((T, D), np.float32)
        xsl[nph:] = x_prompt[b, :T - nph]
        m = dict(common)
        ptrep = np.ascontiguousarray(np.broadcast_to(page_table[4 * c:4 * c + 4].reshape(1, 256), (128, 256)))
        m.update(xs=xsl, xsm=np.ascontiguousarray(x_sample[4 * c:4 * c + 4].reshape(NSAMP, D)), btab=btab, bval=bval, ptrep=ptrep, iot=iot, cdsa=cdsa, cmoba=cmoba)
        in_maps.append(m)
    nc = build_program()
    res = run_bass_kernel_spmd(nc, in_maps, core_ids=list(range(8)))
    y_p = np.zeros((2, T, D), np.float32)
    dsa_pp = np.zeros((1, 2, T, 1088), np.float32)
    moba_pp = np.zeros((1, 2, T, 1024), np.float32)
    y_s = np.zeros((32, 8, D), np.float32)
    dsa_ss = np.zeros((1, 32, 8, 1088), np.float32)
    moba_ss = np.zeros((1, 32, 8, 1024), np.float32)
    for c in range(8):
        b, j = c // 4, c % 4
        r = res.results[c]
        for m in range(4):
            g0 = (4 * m + j) * 512
            y_p[b, g0:g0 + 512] = r["y_p"][m * 512:(m + 1) * 512]
            dsa_pp[0, b, g0:g0 + 512] = r["dsa_p"][m * 512:(m + 1) * 512]
            moba_pp[0, b, g0:g0 + 512] = r["moba_p"][m * 512:(m + 1) * 512]
        y_s[4 * c:4 * c + 4] = r["y_s"].reshape(4, 8, D)
        dsa_ss[0, 4 * c:4 * c + 4] = r["dsa_s"].reshape(4, 8, 1088)
        moba_ss[0, 4 * c:4 * c + 4] = r["moba_s"].reshape(4, 8, 1024)
    return (y_p, y_s, dsa_pp, moba_pp, dsa_ss, moba_ss)
```

```python
import math
import numpy as np
from contextlib import ExitStack
import concourse.bass as bass
import concourse.mybir as mybir
from concourse.bass_utils import run_bass_kernel_spmd

F32 = mybir.dt.float32
BF16 = mybir.dt.bfloat16
I32 = mybir.dt.int32
AF = mybir.ActivationFunctionType
ALU = mybir.AluOpType
AX = mybir.AxisListType

D = 1024
T = 8192
NT = 64
QA, KA, VA, QI, WI, KI, QB, KB, VB, GA, GB, DIN = 0, 512, 1024, 1536, 1792, 1796, 1860, 2372, 2884, 3396, 4420, 5444
ALPHA = 2.0 ** 0.25
LN_EPS = 1e-5
NEGM = -30000.0
BIG = 1e30
NIT = 16
NIT2 = 14
ACT_SPLIT = 0.45
EPS_TIE = 1e-12
DFF = 4096
NSAMP = 32
DBG_NBLK = 16
DBG_SAMPLE = True
DBG_LEVEL = 9
DBG_NQG = 4
DBG_Q = 99
DBG_NSEQ = 4


class _Stop(Exception):
    pass


MUTE = [False]


def chk(k):
    if DBG_Q < k:
        MUTE[0] = True

ENGS = ["sync", "scalar", "gpsimd", "vector", "tensor"]
EPOCH = 4096


class Buf:
    __slots__ = ("w", "r", "x")

    def __init__(self):
        self.w = None
        self.r = []
        self.x = False


class TT:
    def __init__(self, t):
        self.t = t
        self.b = Buf()

    def __getitem__(self, k):
        return self.t[k]


class Prog:
    NDMA = 16

    def __init__(self, nc, es):
        self.nc = nc
        self.es = es
        self.ops = {e: [] for e in ENGS}
        self.cnt = {}
        self.sems = {}
        self.waited = {e: {} for e in ENGS}
        self.dma_n = {e: 0 for e in ENGS}
        self.ncomp = {e: 0 for e in ENGS}
        self.last = {}

    def _sem(self, key):
        if key not in self.sems:
            self.sems[key] = self.es.enter_context(self.nc.semaphore(key))
            self.cnt[key] = 0
        return key

    def _need(self, reads, writes):
        need = {}

        def add(t):
            if t is None:
                return
            k, v = t
            if need.get(k, 0) < v:
                need[k] = v
        for b in reads:
            add(b.w)
        for b in writes:
            add(b.w)
            for r in b.r:
                add(r)
        return need

    def _waits(self, eng, need):
        waits = []
        wd = self.waited[eng]
        for k, v in need.items():
            if wd.get(k, 0) < v:
                wd[k] = v
                waits.append((k, v))
        return waits

    def _mark(self, tok, reads, writes):
        for b in reads:
            b.r.append(tok)
            if len(b.r) > 64:
                mx = {}
                for k, v in b.r:
                    if mx.get(k, 0) < v:
                        mx[k] = v
                b.r = list(mx.items())
        for b in writes:
            b.w = tok
            b.r = []
        self.last[tok[0]] = tok[1]

    @staticmethod
    def _bufs(xs):
        return [x.b if isinstance(x, TT) else x for x in xs]

    def op(self, eng, fn, reads=(), writes=()):
        if MUTE[0]:
            return None
        reads = self._bufs(reads)
        writes = self._bufs(writes)
        xr = [b for b in reads if b.x]
        if xr:
            writes = list(writes) + [b for b in xr if b not in writes]
            reads = [b for b in reads if not b.x]
        waits = self._waits(eng, self._need(reads, writes))
        key = "c_" + eng
        self.cnt[key] = self.cnt.get(key, 0) + 1
        tok = (key, self.cnt[key])
        self.ops[eng].append((waits, fn, key, 1))
        self._mark(tok, reads, writes)
        return tok

    def dma(self, eng, fn, reads=(), writes=()):
        if MUTE[0]:
            return None
        reads = self._bufs(reads)
        writes = self._bufs(writes)
        n = self.dma_n[eng]
        self.dma_n[eng] += 1
        key = self._sem("d_%s_%d" % (eng, n % self.NDMA))
        need = self._need(reads, writes)
        prev = self.cnt[key]
        if prev > 0 and need.get(key, 0) < prev:
            need[key] = prev
        waits = self._waits(eng, need)
        self.cnt[key] += 16
        tok = (key, self.cnt[key])
        self.ops[eng].append((waits, fn, key, 16))
        self._mark(tok, reads, writes)
        return tok

    def barrier(self):
        toks = [(k, v) for k, v in self.cnt.items() if v > 0]
        for e in ENGS:
            waits = self._waits(e, dict(toks))
            if waits:
                self.ops[e].append((waits, None, None, 0))

    def emit(self):
        nc = self.nc
        ref = {}
        for e in ENGS:
            for waits, fn, key, inc in self.ops[e]:
                for k, v in waits:
                    if k.startswith("c_"):
                        ref.setdefault(k, set()).add(v)
        rank = {}
        for k, vs in ref.items():
            for i, v in enumerate(sorted(vs)):
                rank[(k, v)] = i
                self._sem("%s_%d" % (k, i // EPOCH))

        def csem(k, v):
            r = rank[(k, v)]
            return self.sems["%s_%d" % (k, r // EPOCH)], r % EPOCH + 1

        with nc.Block() as block:
            for ename in ENGS:
                ops = self.ops[ename]
                if not ops:
                    continue

                def body(eng, ops=ops, ename=ename):
                    n = 0
                    for waits, fn, key, inc in ops:
                        for k, v in waits:
                            if k.startswith("c_"):
                                sm_, val = csem(k, v)
                                eng.wait_ge(sm_, val)
                            else:
                                eng.wait_ge(self.sems[k], v)
                        if fn is not None:
                            if key.startswith("c_"):
                                n += 1
                                ins = fn(eng)
                                if (key, n) in rank:
                                    sm_, val = csem(key, n)
                                    ins.then_inc(sm_, 1)
                            else:
                                fn(eng).then_inc(self.sems[key], inc)
                getattr(block, ename)(body)


def t5_bucket_np(n):
    n = np.maximum(n, 0)
    nf = np.maximum(n, 1).astype(np.float32)
    large = 16 + (np.log(nf / np.float32(16)) / np.float32(math.log(128 / 16)) * np.float32(16)).astype(np.int32)
    large = np.minimum(large, 31)
    return np.where(n < 16, n, large)


def build_program():
    MUTE[0] = False
    nc = bass.Bass("TRN2", target_bir_lowering=False)

    def din(name, shape, dt=F32):
        return nc.dram_tensor(name, list(shape), dt, kind="ExternalInput").ap()

    def dout(name, shape, dt=F32):
        return nc.dram_tensor(name, list(shape), dt, kind="ExternalOutput").ap()

    def dscr(name, shape, dt):
        return nc.dram_tensor(name, list(shape), dt, kind="Internal").ap()

    xs = din("xs", [T, D])
    xsm = din("xsm", [NSAMP, D])
    w_in = din("w_in", [D, DIN])
    w_a_up = din("w_a_up", [512, D])
    w_b_up = din("w_b_up", [512, D])
    w_out = din("w_out", [D, D])
    w_ff1 = din("w_ff1", [D, DFF])
    w_ff2 = din("w_ff2", [DFF, D])
    lnrep = din("lnrep", [128, 4, D])
    ident_d = din("ident", [128, 128])
    wtab_d = din("wtab", [128, 16, 1024])
    b31_d = din("b31", [128, 16])
    bval_d = din("bval", [128, 32])
    ablk_d = din("ablk", [32, 32, 128])
    btab_d = din("btab", [128, 9, 512])
    pastb_d = din("pastb", [4, 128, 2, 32])

    ptrep_d = din("ptrep", [128, 256], I32)
    iot_d = din("iot", [128, 1], I32)
    cdsa = din("cdsa", [2560 * 128, 1088]) if DBG_SAMPLE else None
    cmoba = din("cmoba", [2560 * 128, 1024]) if DBG_SAMPLE else None
    y_p = dout("y_p", [2048, D])
    y_s = dout("y_s", [NSAMP, D])
    dsa_p = dout("dsa_p", [2048, 1088])
    moba_p = dout("moba_p", [2048, 1024])
    dsa_s = dout("dsa_s", [NSAMP, 1088])
    moba_s = dout("moba_s", [NSAMP, 1024])

    kaT_d = dscr("kaT_d", [4, 128, T], BF16)
    kbT_d = dscr("kbT_d", [4, 128, T], BF16)
    kiT_d = dscr("kiT_d", [64, T], BF16)
    va_d = dscr("va_d", [NT, 128, 520], BF16)
    vb_d = dscr("vb_d", [NT, 128, 520], BF16)
    qaT_d = dscr("qaT_d", [4, 128, 4, 512], BF16)
    qbT_d = dscr("qbT_d", [4, 128, 4, 512], BF16)
    qiT_d = dscr("qiT_d", [4, 128, 2, 512], BF16)
    sga_d = dscr("sga_d", [4, 8, 128, 512], F32)
    sgb_d = dscr("sgb_d", [4, 8, 128, 512], F32)
    skaT_d = dscr("skaT_d", [4, 4, 128, 8320], BF16)
    skbT_d = dscr("skbT_d", [4, 4, 128, 8320], BF16)
    skiT_d = dscr("skiT_d", [4, 64, 8704], BF16)
    sva_d = dscr("sva_d", [4, 65, 128, 520], BF16)
    svb_d = dscr("svb_d", [4, 65, 128, 520], BF16)

    out_toks = []

    with ExitStack() as es:
        P = Prog(nc, es)

        uid = [0]

        def sb(st, name, shape, dt):
            uid[0] += 1
            try:
                return TT(st.enter_context(nc.sbuf_tensor("s_%s_%d" % (name, uid[0]), list(shape), dt)))
            except BaseException as ex:
                print("SB ALLOC FAIL", name, shape, ex)
                raise

        def ps(st, name, shape, dt):
            return TT(st.enter_context(nc.psum_tensor("p_" + name, list(shape), dt)))

        banks = [ps(es, "bank%d" % i, [128, 512], F32) for i in range(8)]
        for bk_ in banks:
            bk_.b.x = True
        ident = sb(es, "identf", [128, 128], F32)
        identb = sb(es, "identb", [128, 128], BF16)
        b31 = sb(es, "b31", [128, 16], F32)
        lohi = sb(es, "lohi", [128, 16, 8], F32)
        lohis = sb(es, "lohis", [128, 8], F32)
        meansT = sb(es, "meansT", [128, 4, 32], F32)
        meansTb = sb(es, "meansTb", [128, 4, 32], BF16)
        ones_t = sb(es, "ones_t", [128, 64], F32)
        P.dma("sync", lambda e: e.dma_start(out=ident[:], in_=ident_d), writes=[ident])
        P.dma("gpsimd", lambda e: e.dma_start(out=identb[:], in_=ident_d), writes=[identb])
        P.dma("sync", lambda e: e.dma_start(out=b31[:], in_=b31_d), writes=[b31])
        P.op("vector", lambda e: e.memset(ones_t[:], 1.0), writes=[ones_t])

        qaTs = sb(es, "qaTs", [128, 4, NSAMP], BF16)
        qbTs = sb(es, "qbTs", [128, 4, NSAMP], BF16)
        qiTs = sb(es, "qiTs", [128, 2, NSAMP], BF16)
        qiTm = [sb(es, "qiTm%d" % i, [128, 2, NSAMP], BF16) for i in range(4)]
        qbTm = [sb(es, "qbTm%d" % i, [128, 4, NSAMP], BF16) for i in range(4)]
        sgas = sb(es, "sgas", [128, 8, NSAMP], F32)
        sgbs = sb(es, "sgbs", [128, 8, NSAMP], F32)
        smeansTb = [sb(es, "smeansTb%d" % i, [128, 4, 32], BF16) for i in range(4)]
        qTall = TT(None)
        maskall = TT(None)
        oall = TT(None)
        pastb_t = TT(None)
        sscr = TT(None)
        bank_rr = [0]

        def next_bank(lo=0, hi=8):
            i = bank_rr[0]
            if not (lo <= i < hi):
                i = lo
            bank_rr[0] = i + 1 if i + 1 < hi else lo
            return banks[i]

        evac_rr = [0]

        def evac(out_ap, in_ap, reads, writes, scale=None, func=None, eng=None):
            if eng is None:
                eng = "scalar" if (evac_rr[0] % 2 == 0) else "vector"
                evac_rr[0] += 1
            if func is not None:
                eng = "scalar"
            if eng == "scalar":
                f = func if func is not None else AF.Copy
                if scale is None:
                    P.op("scalar", lambda e: e.activation(out=out_ap, in_=in_ap, func=f), reads=reads, writes=writes)
                else:
                    P.op("scalar", lambda e: e.activation(out=out_ap, in_=in_ap, func=f, scale=scale), reads=reads, writes=writes)
            else:
                if scale is None:
                    P.op("vector", lambda e: e.tensor_copy(out=out_ap, in_=in_ap), reads=reads, writes=writes)
                else:
                    P.op("vector", lambda e: e.tensor_scalar(out=out_ap, in0=in_ap, scalar1=float(scale), scalar2=None, op0=ALU.mult),
                         reads=reads, writes=writes)

        with ExitStack() as s1:
            win = sb(s1, "win", [128, 8, DIN], BF16)
            for c in range(8):
                P.dma("gpsimd", lambda e, c=c: e.dma_start(out=win[:, c, :], in_=w_in[c * 128:(c + 1) * 128, :]), writes=[win])
            xin = [sb(s1, "xin%d" % i, [128, 4, D], F32) for i in range(2)]
            XT = [sb(s1, "XT%d" % i, [128, 8, 512], BF16) for i in range(2)]
            kstg = [sb(s1, "kstg%d" % i, [128, 512], BF16) for i in range(4)]
            vstg = [sb(s1, "vstg%d" % i, [128, 8, 65], BF16) for i in range(4)]
            fstg = [sb(s1, "fstg%d" % i, [128, 512], F32) for i in range(3)]
            rowd = [sb(s1, "rowd%d" % i, [128, 1088], F32) for i in range(2)]
            rowm = [sb(s1, "rowm%d" % i, [128, 1024], F32) for i in range(2)]
            qiw = sb(s1, "qiw", [128, 256], F32)
            wsb = sb(s1, "wsb", [128, 4], F32)
            qistg = sb(s1, "qistg", [128, 2, 512], BF16)
            for v in vstg:
                P.op("vector", lambda e, v=v: e.memset(v[:], 1.0), writes=[v])
            kst_i = [0]
            vst_i = [0]
            fst_i = [0]

            def proj_fm(col0, M, xt, scale=None, func=None, N=512):
                bk = next_bank()
                for dc in range(8):
                    P.op("tensor", lambda e, dc=dc, bk=bk: e.matmul(bk[0:M, 0:N], lhsT=win[:, dc, col0:col0 + M], rhs=xt[:, dc, 0:N],
                                                                  start=(dc == 0), stop=(dc == 7)),
                         reads=[win, xt], writes=[bk])
                return bk

            def proj_tm(col0, N, xt, t, ntok=128):
                bk = next_bank()
                for dc in range(8):
                    P.op("tensor", lambda e, dc=dc, bk=bk: e.matmul(bk[0:ntok, 0:N], lhsT=xt[:, dc, t * 128:t * 128 + ntok],
                                                                  rhs=win[:, dc, col0:col0 + N], start=(dc == 0), stop=(dc == 7)),
                         reads=[win, xt], writes=[bk])
                return bk

            def load_xT(src_rows_ap, xi, xt, ntile, ntok=128, preloaded=False):
                for t in range(ntile):
                    pass
                if not preloaded:
                    P.dma("gpsimd", lambda e: e.dma_start(out=xi[0:ntok, 0:ntile, :], in_=src_rows_ap.rearrange("(t p) d -> p t d", p=ntok)),
                          writes=[xi])
                for c in range(8):
                    bk = next_bank()
                    for t in range(ntile):
                        P.op("tensor", lambda e, c=c, t=t, bk=bk: e.transpose(out=bk[:, t * 128:t * 128 + ntok],
                                                                            in_=xi[0:ntok, t, c * 128:(c + 1) * 128],
                                                                            identity=ident[0:ntok, 0:ntok]),
                             reads=[xi, ident], writes=[bk])
                    w = ntile * 128 if ntok == 128 else ntok
                    evac(xt[:, c, 0:w], bk[:, 0:w], [bk], [xt])

            for sbk in range(DBG_NBLK):
                xi = xin[sbk % 2]
                xt = XT[sbk % 2]
                if sbk == 0:
                    P.dma("gpsimd", lambda e, xi=xi: e.dma_start(out=xi[:, :, :], in_=xs[0:512, :].rearrange("(t p) d -> p t d", p=128)), writes=[xi])
                load_xT(xs[sbk * 512:(sbk + 1) * 512, :], xi, xt, 4, preloaded=True)
                if sbk + 1 < DBG_NBLK:
                    xn = xin[(sbk + 1) % 2]
                    P.dma("gpsimd", lambda e, xn=xn, sbk=sbk: e.dma_start(out=xn[:, :, :], in_=xs[(sbk + 1) * 512:(sbk + 2) * 512, :].rearrange("(t p) d -> p t d", p=128)),
                          writes=[xn])
                own = (sbk % 4 == 3) and DBG_LEVEL >= 5
                mq = sbk // 4
                for (col0, M, dst, is_kb, cidx) in [] if DBG_LEVEL < 3 else (
                        [(KA + 128 * c, 128, kaT_d[c], False, c) for c in range(4)]
                        + [(KI, 64, kiT_d, False, 0)]
                        + [(KB + 128 * c, 128, kbT_d[c], True, c) for c in range(4)]):
                    bk = proj_fm(col0, M, xt)
                    st = kstg[kst_i[0] % 4]
                    kst_i[0] += 1
                    evac(st[0:M, :], bk[0:M, :], [bk], [st])
                    if is_kb:
                        for hb in range(2):
                            P.op("vector", lambda e, bk=bk, cidx=cidx, sbk=sbk, hb=hb: e.tensor_reduce(
                                out=meansT[:, cidx, 2 * sbk + hb:2 * sbk + hb + 1], in_=bk[:, hb * 256:(hb + 1) * 256],
                                axis=AX.X, op=ALU.add), reads=[bk], writes=[meansT])
                    P.dma("sync", lambda e, st=st, dst=dst, M=M, sbk=sbk: e.dma_start(out=dst[0:M, sbk * 512:(sbk + 1) * 512], in_=st[0:M, :]),
                          reads=[st])
                for t in range(4 if DBG_LEVEL >= 4 else 0):
                    u = sbk * 4 + t
                    if own:
                        rd = rowd[t % 2]
                        rm = rowm[t % 2]
                    for (col0, dst, which) in ((VA, va_d, 0), (VB, vb_d, 1)):
                        bk = proj_tm(col0, 512, xt, t)
                        vs_ = vstg[vst_i[0] % 4]
                        vst_i[0] += 1
                        evac(vs_[:, :, 0:64], bk[:, :].rearrange("p (h d) -> p h d", h=8), [bk], [vs_])
                        P.dma("sync", lambda e, vs_=vs_, dst=dst, u=u: e.dma_start(out=dst[u], in_=vs_[:].rearrange("p h d -> p (h d)")),
                              reads=[vs_])
                        if own:
                            tgt = rd if which == 0 else rm
                            evac(tgt[:, 512:1024], bk[:, :], [bk], [tgt])
                    if own:
                        bk = proj_tm(KA, 512, xt, t)
                        evac(rd[:, 0:512], bk[:, :], [bk], [rd])
                        bk = proj_tm(KI, 64, xt, t)
                        evac(rd[:, 1024:1088], bk[:, 0:64], [bk], [rd])
                        bk = proj_tm(KB, 512, xt, t)
                        evac(rm[:, 0:512], bk[:, :], [bk], [rm])
                        r0 = (mq * 4 + t) * 128
                        out_toks.append(P.dma("sync", lambda e, rd=rd, r0=r0: e.dma_start(out=dsa_p[r0:r0 + 128, :], in_=rd[:]), reads=[rd]))
                        out_toks.append(P.dma("sync", lambda e, rm=rm, r0=r0: e.dma_start(out=moba_p[r0:r0 + 128, :], in_=rm[:]), reads=[rm]))
                        qt = mq * 4 + t
                        bk = proj_tm(WI, 4, xt, t)
                        P.op("vector", lambda e, bk=bk: e.tensor_copy(out=wsb[:], in_=bk[:, 0:4]), reads=[bk], writes=[wsb])
                        P.op("vector", lambda e, qt=qt: e.tensor_scalar(out=lohi[:, qt, 0:4], in0=wsb[:], scalar1=0.0, scalar2=-BIG,
                                                                    op0=ALU.is_le, op1=ALU.mult), reads=[wsb], writes=[lohi])
                        P.op("vector", lambda e, qt=qt: e.tensor_scalar(out=lohi[:, qt, 4:8], in0=wsb[:], scalar1=0.0, scalar2=BIG,
                                                                    op0=ALU.is_gt, op1=ALU.mult), reads=[wsb], writes=[lohi])
                        bk = proj_tm(QI, 256, xt, t)
                        for h in range(4):
                            P.op("vector", lambda e, bk=bk, h=h: e.tensor_scalar(out=qiw[:, h * 64:(h + 1) * 64], in0=bk[:, h * 64:(h + 1) * 64],
                                                                             scalar1=wsb[:, h:h + 1], scalar2=None, op0=ALU.mult),
                                 reads=[bk, wsb], writes=[qiw])
                        for pc in range(2):
                            bk2 = next_bank()
                            P.op("tensor", lambda e, bk2=bk2, pc=pc: e.transpose(out=bk2[:, 0:128], in_=qiw[:, pc * 128:(pc + 1) * 128],
                                                                               identity=ident[:]), reads=[qiw, ident], writes=[bk2])
                            evac(qistg[:, pc, t * 128:(t + 1) * 128], bk2[:, 0:128], [bk2], [qistg])
                if own:
                    P.dma("sync", lambda e, mq=mq: e.dma_start(out=qiT_d[mq], in_=qistg[:]), reads=[qistg])
                    for (col0, dst) in ((QA, qaT_d), (QB, qbT_d)):
                        for c in range(4):
                            bk = proj_fm(col0 + 128 * c, 128, xt)
                            st = kstg[kst_i[0] % 4]
                            kst_i[0] += 1
                            evac(st[:], bk[:], [bk], [st], scale=0.125)
                            P.dma("sync", lambda e, st=st, dst=dst, mq=mq, c=c: e.dma_start(out=dst[mq, :, c, :], in_=st[:]), reads=[st])
                    for (col0, dst) in ((GA, sga_d), (GB, sgb_d)):
                        for c in range(8):
                            bk = proj_fm(col0 + 128 * c, 128, xt)
                            st = fstg[fst_i[0] % 3]
                            fst_i[0] += 1
                            evac(st[:], bk[:], [bk], [st], func=AF.Sigmoid)
                            P.dma("sync", lambda e, st=st, dst=dst, mq=mq, c=c: e.dma_start(out=dst[mq, c], in_=st[:]), reads=[st])

            xi = xin[0]
            assert True
            xt = XT[0]
            load_xT(xsm[:, :], xi, xt, 1, ntok=NSAMP)
            rd = rowd[0]
            rm = rowm[0]
            for (col0, N, tgt, o0) in ((KA, 512, rd, 0), (VA, 512, rd, 512), (KI, 64, rd, 1024), (KB, 512, rm, 0), (VB, 512, rm, 512)):
                bk = proj_tm(col0, N, xt, 0, ntok=NSAMP)
                evac(tgt[0:NSAMP, o0:o0 + N], bk[0:NSAMP, 0:N], [bk], [tgt])
            out_toks.append(P.dma("sync", lambda e: e.dma_start(out=dsa_s, in_=rd[0:NSAMP, :]), reads=[rd]))
            out_toks.append(P.dma("sync", lambda e: e.dma_start(out=moba_s, in_=rm[0:NSAMP, :]), reads=[rm]))
            if DBG_SAMPLE:
                zt = sb(s1, "zt", [128, 520], BF16)
                P.op("vector", lambda e: e.memset(zt[:], 0.0), writes=[zt])
                for (col0, dstT) in ((QA, qaTs), (QB, qbTs)):
                    for c in range(4):
                        bk = proj_fm(col0 + 128 * c, 128, xt, N=NSAMP)
                        evac(dstT[:, c, :], bk[:, 0:NSAMP], [bk], [dstT, qTall], scale=0.125)
                for (col0, dstT) in ((GA, sgas), (GB, sgbs)):
                    for c in range(8):
                        bk = proj_fm(col0 + 128 * c, 128, xt, N=NSAMP)
                        evac(dstT[:, c, :], bk[:, 0:NSAMP], [bk], [dstT], func=AF.Sigmoid)
                for (col0, M, dstd, cidx) in ([(KA + 128 * c, 128, skaT_d, c) for c in range(4)] + [(KB + 128 * c, 128, skbT_d, c) for c in range(4)] + [(KI, 64, skiT_d, None)]):
                    bk = proj_fm(col0, M, xt, N=NSAMP)
                    st = kstg[kst_i[0] % 4]
                    kst_i[0] += 1
                    evac(st[0:M, 0:NSAMP], bk[0:M, 0:NSAMP], [bk], [st])
                    for s in range(4):
                        dd = dstd[s, cidx] if cidx is not None else dstd[s]
                        P.dma("sync", lambda e, st=st, dd=dd, M=M, s=s: e.dma_start(out=dd[0:M, 8192:8200], in_=st[0:M, 8 * s:8 * s + 8]), reads=[st], writes=[sscr])
                        wz = (8320 - 8200) if cidx is not None else (8704 - 8200)
                        P.dma("sync", lambda e, dd=dd, M=M, wz=wz: e.dma_start(out=dd[0:M, 8200:8200 + wz], in_=zt[0:M, 0:wz]), reads=[zt], writes=[sscr])
                for (src, o0, dstd) in ((rd, 512, sva_d), (rm, 512, svb_d)):
                    vs_ = vstg[vst_i[0] % 4]
                    vst_i[0] += 1
                    P.op("vector", lambda e, vs_=vs_, src=src, o0=o0: e.tensor_copy(out=vs_[0:NSAMP, :, 0:64], in_=src[0:NSAMP, o0:o0 + 512].rearrange("p (h d) -> p h d", h=8)),
                         reads=[src], writes=[vs_])
                    for s in range(4):
                        P.dma("sync", lambda e, vs_=vs_, dstd=dstd, s=s: e.dma_start(out=dstd[s, 64, 0:8, :], in_=vs_[8 * s:8 * s + 8].rearrange("p h d -> p (h d)")),
                              reads=[vs_], writes=[sscr])
                        P.dma("sync", lambda e, dstd=dstd, s=s: e.dma_start(out=dstd[s, 64, 8:128, :], in_=zt[0:120, :]), reads=[zt], writes=[sscr])
                bk = proj_tm(WI, 4, xt, 0, ntok=NSAMP)
                P.op("vector", lambda e, bk=bk: e.tensor_copy(out=wsb[0:NSAMP, :], in_=bk[0:NSAMP, 0:4]), reads=[bk], writes=[wsb])
                P.op("vector", lambda e: e.tensor_scalar(out=lohis[0:NSAMP, 0:4], in0=wsb[0:NSAMP, :], scalar1=0.0, scalar2=-BIG, op0=ALU.is_le, op1=ALU.mult), reads=[wsb], writes=[lohis])
                P.op("vector", lambda e: e.tensor_scalar(out=lohis[0:NSAMP, 4:8], in0=wsb[0:NSAMP, :], scalar1=0.0, scalar2=BIG, op0=ALU.is_gt, op1=ALU.mult), reads=[wsb], writes=[lohis])
                bk = proj_tm(QI, 256, xt, 0, ntok=NSAMP)
                for h in range(4):
                    P.op("vector", lambda e, bk=bk, h=h: e.tensor_scalar(out=qiw[0:NSAMP, h * 64:(h + 1) * 64], in0=bk[0:NSAMP, h * 64:(h + 1) * 64],
                                                                     scalar1=wsb[0:NSAMP, h:h + 1], scalar2=None, op0=ALU.mult), reads=[bk, wsb], writes=[qiw])
                for pc in range(2):
                    bk2 = next_bank()
                    P.op("tensor", lambda e, bk2=bk2, pc=pc: e.transpose(out=bk2[:, 0:NSAMP], in_=qiw[0:NSAMP, pc * 128:(pc + 1) * 128], identity=ident[0:NSAMP, 0:NSAMP]),
                         reads=[qiw, ident], writes=[bk2])
                    evac(qiTs[:, pc, :], bk2[:, 0:NSAMP], [bk2], [qiTs])
                for s in range(4):
                    P.op("vector", lambda e, s=s: e.memset(qiTm[s][:], 0.0), writes=[qiTm[s]])
                    P.op("vector", lambda e, s=s: e.tensor_copy(out=qiTm[s][:, :, 8 * s:8 * s + 8], in_=qiTs[:, :, 8 * s:8 * s + 8]), reads=[qiTs], writes=[qiTm[s]])
                    P.op("vector", lambda e, s=s: e.memset(qbTm[s][:], 0.0), writes=[qbTm[s]])
                    P.op("vector", lambda e, s=s: e.tensor_copy(out=qbTm[s][:, :, 8 * s:8 * s + 8], in_=qbTs[:, :, 8 * s:8 * s + 8]), reads=[qbTs], writes=[qbTm[s]])
            P.op("vector", lambda e: e.tensor_scalar(out=meansTb[:], in0=meansT[:], scalar1=1.0 / 256.0, scalar2=None, op0=ALU.mult),
                 reads=[meansT], writes=[meansTb])
            P.barrier()

        lnr = sb(es, "lnr", [128, 4, D], F32)
        P.dma("sync", lambda e: e.dma_start(out=lnr[:], in_=lnrep), writes=[lnr])
        ablk = sb(es, "ablk", [32, 32, 128], BF16)
        P.dma("gpsimd", lambda e: e.dma_start(out=ablk[:], in_=ablk_d), writes=[ablk])
        gb2 = sb(es, "gb2", [128, 32], F32)
        P.dma("sync", lambda e: e.dma_start(out=gb2[:], in_=bval_d), writes=[gb2])

        def layer_norm(z, nt, gi, out_ap, sm, st6):
            for hf in range(2):
                P.op("vector", lambda e, hf=hf: e.bn_stats(out=st6[0:nt, hf, :], in_=z[0:nt, hf * 512:(hf + 1) * 512]), reads=[z], writes=[st6])
            P.op("vector", lambda e: e.bn_aggr(out=sm[0:nt, 0:2], in_=st6[0:nt].rearrange("p a b -> p (a b)")), reads=[st6], writes=[sm])
            P.op("vector", lambda e: e.tensor_scalar(out=sm[0:nt, 2:3], in0=sm[0:nt, 1:2], scalar1=LN_EPS, scalar2=None, op0=ALU.add), reads=[sm], writes=[sm])
            P.op("scalar", lambda e: e.activation(out=sm[0:nt, 3:4], in_=sm[0:nt, 2:3], func=AF.Sqrt), reads=[sm], writes=[sm])
            P.op("vector", lambda e: e.reciprocal(out=sm[0:nt, 4:5], in_=sm[0:nt, 3:4]), reads=[sm], writes=[sm])
            P.op("vector", lambda e: e.tensor_scalar(out=z[0:nt, :], in0=z[0:nt, :], scalar1=sm[0:nt, 0:1], scalar2=sm[0:nt, 4:5], op0=ALU.subtract, op1=ALU.mult),
                 reads=[z, sm], writes=[z])
            P.op("vector", lambda e: e.tensor_tensor(out=z[0:nt, :], in0=z[0:nt, :], in1=lnr[0:nt, gi, :], op=ALU.mult), reads=[z, lnr], writes=[z])
            return lambda e: e.tensor_tensor(out=out_ap, in0=z[0:nt, :], in1=lnr[0:nt, gi + 1, :], op=ALU.add)

        def indexer(st, NP, nch, emit_scores, lohi_fn, bi_fn, mbT, qoff, btab):
            nk = nch * 512
            n1 = (int(nk * ACT_SPLIT) // 512) * 512 if nk >= 2048 else nk
            n2 = nk - n1
            sc = sb(st, "sc", [128, nk], F32)
            junk = sb(st, "junk", [128, n1], BF16)
            tmp = [sb(st, "tmpr%d" % i, [128, 512], F32) for i in range(2)]
            mbc = [sb(st, "mbc%d" % i, [128, 512], F32) for i in range(2)]
            sm = sb(st, "sma", [128, 64], F32)
            CMIN, CMAX, LO, HI, MID, CNT, GE, D1, D2 = 0, 20, 40, 41, 42, 43, 44, 45, 46
            tmp_i = 0
            for c in range(nch):
                emit_scores(c)
                scc = sc[0:NP, c * 512:(c + 1) * 512]
                P.op("vector", lambda e, scc=scc: e.tensor_scalar(out=scc, in0=banks[0][0:NP, :], scalar1=lohi_fn(0), scalar2=lohi_fn(4), op0=ALU.max, op1=ALU.min),
                     reads=[banks[0], lohi, lohis], writes=[sc])
                for h in range(1, 4):
                    tp = tmp[tmp_i % 2]
                    tmp_i += 1
                    P.op("vector", lambda e, tp=tp, h=h: e.tensor_scalar(out=tp[0:NP, :], in0=banks[h][0:NP, :], scalar1=lohi_fn(h), scalar2=lohi_fn(4 + h),
                                                                     op0=ALU.max, op1=ALU.min), reads=[banks[h], lohi, lohis], writes=[tp])
                    P.op("vector", lambda e, tp=tp, scc=scc: e.tensor_tensor(out=scc, in0=scc, in1=tp[0:NP, :], op=ALU.add), reads=[sc, tp], writes=[sc])
                P.op("vector", lambda e, scc=scc, c=c: e.tensor_reduce(out=sm[0:NP, CMIN + c:CMIN + c + 1], in_=scc, axis=AX.X, op=ALU.min), reads=[sc], writes=[sm])
                P.op("vector", lambda e, scc=scc, c=c: e.tensor_reduce(out=sm[0:NP, CMAX + c:CMAX + c + 1], in_=scc, axis=AX.X, op=ALU.max), reads=[sc], writes=[sm])
                bi = bi_fn(c)
                P.op("vector", lambda e, scc=scc, c=c, bi=bi: e.scalar_tensor_tensor(out=scc, in0=scc, scalar=float(-EPS_TIE * 512 * c), in1=btab[0:NP, bi, :],
                                                                                 op0=ALU.add, op1=ALU.add), reads=[sc, btab], writes=[sc])
            chk(2)
            P.op("vector", lambda e: e.tensor_reduce(out=sm[0:NP, LO:LO + 1], in_=sm[0:NP, CMIN:CMIN + nch], axis=AX.X, op=ALU.min), reads=[sm], writes=[sm])
            P.op("vector", lambda e: e.tensor_reduce(out=sm[0:NP, HI:HI + 1], in_=sm[0:NP, CMAX:CMAX + nch], axis=AX.X, op=ALU.max), reads=[sm], writes=[sm])
            P.op("vector", lambda e: e.tensor_scalar(out=sm[0:NP, LO:LO + 1], in0=sm[0:NP, LO:LO + 1], scalar1=-1.0, scalar2=None, op0=ALU.add), reads=[sm], writes=[sm])
            P.op("vector", lambda e: e.tensor_scalar(out=sm[0:NP, HI:HI + 1], in0=sm[0:NP, HI:HI + 1], scalar1=1.0, scalar2=None, op0=ALU.add), reads=[sm], writes=[sm])
            steps = [None] * NIT + [float(-EPS_TIE * (nk + 64)), 1e-30] + [None] * NIT2
            midt = sb(st, "midt", [128, 2], F32)
            sact = sb(st, "sact", [128, 2], F32)
            junk2 = sb(st, "junk2", [128, max(n2, 8)], BF16)
            for pv in steps:
                if pv is None:
                    P.op("vector", lambda e: e.tensor_scalar(out=midt[0:NP, 0:1], in0=sm[0:NP, LO:LO + 1], scalar1=sm[0:NP, HI:HI + 1], scalar2=0.5,
                                                             op0=ALU.add, op1=ALU.mult), reads=[sm], writes=[midt])
                else:
                    P.op("vector", lambda e, pv=pv: e.tensor_scalar(out=midt[0:NP, 0:1], in0=sm[0:NP, LO:LO + 1], scalar1=pv, scalar2=sm[0:NP, HI:HI + 1],
                                                                  op0=ALU.max, op1=ALU.min), reads=[sm], writes=[midt])
                if n2 > 0:
                    P.op("scalar", lambda e: e.activation(out=junk2[0:NP, 0:n2], in_=sc[0:NP, n1:nk], func=AF.Sign, bias=midt[0:NP, 0:1], scale=-1.0,
                                                          accum_out=sact[0:NP, 0:1]), reads=[sc, midt], writes=[junk2, sact])
                P.op("vector", lambda e: e.tensor_scalar(out=junk[0:NP, 0:n1], in0=sc[0:NP, 0:n1], scalar1=midt[0:NP, 0:1], scalar2=0.0,
                                                         op0=ALU.is_ge, op1=ALU.add, accum_out=sm[0:NP, CNT:CNT + 1]), reads=[sc, midt], writes=[junk, sm])
                if n2 > 0:
                    P.op("vector", lambda e: e.scalar_tensor_tensor(out=sm[0:NP, CNT:CNT + 1], in0=sact[0:NP, 0:1], scalar=-0.5, in1=sm[0:NP, CNT:CNT + 1],
                                                                    op0=ALU.mult, op1=ALU.add), reads=[sm, sact], writes=[sm])
                P.op("vector", lambda e: e.tensor_scalar(out=sm[0:NP, GE:GE + 1], in0=sm[0:NP, CNT:CNT + 1], scalar1=float(255.5 - 0.5 * n2), scalar2=None, op0=ALU.is_ge),
                     reads=[sm], writes=[sm])
                P.op("vector", lambda e: e.tensor_tensor(out=sm[0:NP, D1:D1 + 1], in0=midt[0:NP, 0:1], in1=sm[0:NP, LO:LO + 1], op=ALU.subtract), reads=[sm, midt], writes=[sm])
                P.op("vector", lambda e: e.tensor_tensor(out=sm[0:NP, D2:D2 + 1], in0=sm[0:NP, HI:HI + 1], in1=midt[0:NP, 0:1], op=ALU.subtract), reads=[sm, midt], writes=[sm])
                P.op("vector", lambda e: e.scalar_tensor_tensor(out=sm[0:NP, LO:LO + 1], in0=sm[0:NP, D1:D1 + 1], scalar=sm[0:NP, GE:GE + 1], in1=sm[0:NP, LO:LO + 1],
                                                                op0=ALU.mult, op1=ALU.add), reads=[sm], writes=[sm])
                P.op("vector", lambda e: e.scalar_tensor_tensor(out=sm[0:NP, HI:HI + 1], in0=sm[0:NP, D2:D2 + 1], scalar=sm[0:NP, GE:GE + 1], in1=midt[0:NP, 0:1],
                                                                op0=ALU.mult, op1=ALU.add), reads=[sm, midt], writes=[sm])
            chk(3)
            for c in range(nch):
                mb_ = mbc[c % 2]
                P.op("vector", lambda e, mb_=mb_, c=c: e.tensor_scalar(out=mb_[0:NP, :], in0=sc[0:NP, c * 512:(c + 1) * 512], scalar1=sm[0:NP, LO:LO + 1],
                                                                   scalar2=None, op0=ALU.is_ge), reads=[sc, sm], writes=[mb_])
                bk = banks[4 + (c % 4)]
                for t in range(4):
                    P.op("tensor", lambda e, bk=bk, mb_=mb_, t=t: e.transpose(out=bk[:, t * 128:t * 128 + NP], in_=mb_[0:NP, t * 128:(t + 1) * 128],
                                                                         identity=ident[0:NP, 0:NP]), reads=[mb_, ident], writes=[bk])
                evac(mbT[:, 4 * c:4 * c + 4, qoff:qoff + NP], bk[:, :].rearrange("p (t q) -> p t q", t=4)[:, :, 0:NP], [bk], [mbT], eng="scalar")

        def moba_gate(st, NP, emit_gate, bias_ap, ubo, mbBT, qoff):
            gsb = sb(st, "gsb", [128, 8, 32], F32)
            m8 = sb(st, "m8", [128, 8, 8], F32)
            thr = sb(st, "thr", [128, 8], F32)
            mbB = sb(st, "mbB", [128, 8, 32], F32)
            bk = banks[0]
            emit_gate(bk)
            for h in range(8):
                P.op("vector", lambda e, h=h: e.tensor_tensor(out=gsb[0:NP, h, :], in0=bk[0:NP, h * 32:(h + 1) * 32], in1=bias_ap, op=ALU.add),
                     reads=[bk, pastb_t], writes=[gsb])
            for h in range(8):
                P.op("vector", lambda e, h=h: e.max(out=m8[0:NP, h, :], in_=gsb[0:NP, h, :]), reads=[gsb], writes=[m8])
            P.op("vector", lambda e: e.tensor_scalar(out=thr[0:NP, :], in0=m8[0:NP, :, 2], scalar1=-1e29, scalar2=None, op0=ALU.max), reads=[m8], writes=[thr])
            for h in range(8):
                P.op("vector", lambda e, h=h: e.tensor_scalar(out=mbB[0:NP, h, :], in0=gsb[0:NP, h, :], scalar1=thr[0:NP, h:h + 1], scalar2=NEGM,
                                                          op0=ALU.is_lt, op1=ALU.mult), reads=[gsb, thr], writes=[mbB])
            if ubo is not None:
                P.op("vector", lambda e: e.memset(mbB[0:NP, :, ubo:ubo + 1], 0.0), writes=[mbB])
            for g in range(2):
                bk2 = banks[4 + g]
                for hh in range(4):
                    h = 4 * g + hh
                    P.op("tensor", lambda e, bk2=bk2, h=h, hh=hh: e.transpose(out=bk2[0:32, hh * 128:hh * 128 + NP], in_=mbB[0:NP, h, :], identity=ident[0:NP, 0:NP]),
                         reads=[mbB, ident], writes=[bk2])
                evac(mbBT[0:32, 4 * g:4 * g + 4, qoff:qoff + NP], bk2[0:32, :].rearrange("p (h q) -> p h q", h=4)[:, :, 0:NP], [bk2], [mbBT], eng="vector")

        class AttBufs:
            def __init__(self, st):
                self.kTs = [sb(st, "kTs%d" % i, [128, 2048], BF16) for i in range(2)]
                self.vss = [sb(st, "vss%d" % i, [128, 16, 130], BF16) for i in range(2)]
                self.PTs = [sb(st, "PT%d" % i, [128, 512], BF16) for i in range(4)]
                self.wts = [sb(st, "wts%d" % i, [128, 1024], BF16) for i in range(4)]
                self.rd = sb(st, "rd", [128, 512], F32)
                self.rb = sb(st, "rb", [128, 512], F32)
                self.kv_i = 0
                self.pt_i = 0
                self.wt_i = 0
                self.lb_i = 0

        def attend(A, kT_fn, v_fn, q_fn, o_fn, hoff, ntl, NQ, diag_fn, mask_fn):
            nkb = (ntl + 15) // 16
            SKEW = 2
            for p in range(4):
                pend = []
                wth = []
                for hh in range(2):
                    w_ = A.wts[A.wt_i % 4]
                    A.wt_i += 1
                    P.dma("gpsimd", lambda e, w_=w_, hd=hoff + 2 * p + hh: e.dma_start(out=w_[:], in_=wtab_d[:, hd, :]), writes=[w_])
                    wth.append(w_)
                for kb in range(nkb):
                    nt_ = min(16, ntl - 16 * kb)
                    kt = A.kTs[A.kv_i % 2]
                    vs = A.vss[A.kv_i % 2]
                    A.kv_i += 1
                    P.dma("sync", lambda e, kt=kt, p=p, kb=kb, nt_=nt_: e.dma_start(out=kt[:, 0:nt_ * 128], in_=kT_fn(p)[:, kb * 2048:kb * 2048 + nt_ * 128]), writes=[kt])
                    P.dma("sync", lambda e, vs=vs, p=p, kb=kb, nt_=nt_: e.dma_start(
                        out=vs[:, 0:nt_, :], in_=v_fn()[kb * 16:kb * 16 + nt_, :, p * 130:(p + 1) * 130].rearrange("t k f -> k t f")), writes=[vs])
                    for tl in range(nt_):
                        u = kb * 16 + tl
                        x0 = diag_fn(u)
                        diag = x0 is not None
                        for hh in range(2):
                            h = 2 * p + hh
                            r0 = 64 * hh
                            L = banks[A.lb_i % 4]
                            A.lb_i += 1
                            OT = banks[4 + hh]
                            mk = mask_fn(u, h)
                            last1 = (mk is None or mk[0] is None) and (not diag)
                            P.op("tensor", lambda e, L=L, kt=kt, r0=r0, tl=tl, p=p, last1=last1: e.matmul(
                                L[:, 0:NQ], lhsT=kt[r0:r0 + 64, tl * 128:(tl + 1) * 128], rhs=q_fn(r0, p), start=True, stop=last1),
                                reads=[kt, qTall], writes=[L])
                            mulmask = None
                            if mk is not None and mk[0] is None:
                                mulmask = mk[1]
                                mk = None
                            last1 = (mk is None) and (not diag)
                            if mk is not None:
                                lh, rh = mk
                                P.op("tensor", lambda e, L=L, lh=lh, rh=rh, diag=diag: e.matmul(L[:, 0:NQ], lhsT=lh, rhs=rh, start=False, stop=(not diag)),
                                     reads=[identb, ablk, maskall], writes=[L])
                            if diag:
                                P.op("tensor", lambda e, L=L, w_=wth[hh], x0=x0: e.matmul(L[:, 0:NQ], lhsT=identb[:], rhs=w_[:, x0:x0 + NQ], start=False, stop=True),
                                     reads=[identb, wth[hh]], writes=[L])
                            PT = A.PTs[A.pt_i % 4]
                            A.pt_i += 1
                            if diag:
                                P.op("scalar", lambda e, PT=PT, L=L: e.activation(out=PT[:, 0:NQ], in_=L[:, 0:NQ], func=AF.Exp), reads=[L], writes=[PT])
                            else:
                                P.op("scalar", lambda e, PT=PT, L=L, hd=hoff + h: e.activation(out=PT[:, 0:NQ], in_=L[:, 0:NQ], func=AF.Exp, bias=b31[:, hd:hd + 1]),
                                     reads=[L, b31], writes=[PT])
                            if mulmask is not None:
                                P.op("vector", lambda e, PT=PT, mm_=mulmask: e.tensor_tensor(out=PT[:, 0:NQ], in0=PT[:, 0:NQ], in1=mm_, op=ALU.mult),
                                     reads=[PT, maskall], writes=[PT])
                            pend.append((lambda e, OT=OT, vs=vs, tl=tl, hh=hh, PT=PT, u=u: e.matmul(
                                OT[0:65, 0:NQ], lhsT=vs[:, tl, hh * 65:(hh + 1) * 65], rhs=PT[:, 0:NQ], start=(u == 0), stop=(u == ntl - 1)),
                                [vs, PT], [OT]))
                            if len(pend) > SKEW:
                                f_, r_, w_2 = pend.pop(0)
                                P.op("tensor", f_, reads=r_, writes=w_2)
                while pend:
                    f_, r_, w_2 = pend.pop(0)
                    P.op("tensor", f_, reads=r_, writes=w_2)
                for hh in range(2):
                    h = 2 * p + hh
                    OT = banks[4 + hh]
                    P.op("vector", lambda e, OT=OT: e.reciprocal(out=A.rd[64:65, 0:NQ], in_=OT[64:65, 0:NQ]), reads=[OT], writes=[A.rd])
                    bx = banks[A.lb_i % 4]
                    A.lb_i += 1
                    P.op("tensor", lambda e, bx=bx: e.matmul(bx[0:64, 0:NQ], lhsT=ones_t[64:65, 0:64], rhs=A.rd[64:65, 0:NQ], start=True, stop=True),
                         reads=[ones_t, A.rd], writes=[bx])
                    P.op("scalar", lambda e, bx=bx: e.activation(out=A.rb[0:64, 0:NQ], in_=bx[0:64, 0:NQ], func=AF.Copy), reads=[bx], writes=[A.rb])
                    P.op("vector", lambda e, OT=OT, h=h: e.tensor_tensor(out=o_fn(h), in0=OT[0:64, 0:NQ], in1=A.rb[0:64, 0:NQ], op=ALU.mult),
                         reads=[OT, A.rb], writes=[oall])

        def phase3(NT, tiles, oaT, obT, sg_fn, x_fn, y_fn):
            with ExitStack() as s3:
                nti = len(tiles)
                hres = sb(s3, "hres", [128, nti, D], F32)
                hT = sb(s3, "hT", [128, 8, 512], BF16)
                sm3 = sb(s3, "sm3", [128, 8], F32)
                st6 = sb(s3, "st6", [128, 2, 6], F32)
                z = sb(s3, "z", [128, D], F32)
                with ExitStack() as s3a:
                    wau = sb(s3a, "wau", [64, 8, D], BF16)
                    wbu = sb(s3a, "wbu", [64, 8, D], BF16)
                    wo = sb(s3a, "wo", [128, 8, D], BF16)
                    P.dma("gpsimd", lambda e: e.dma_start(out=wau[:], in_=w_a_up.rearrange("(h d) n -> d h n", d=64)), writes=[wau])
                    P.dma("gpsimd", lambda e: e.dma_start(out=wbu[:], in_=w_b_up.rearrange("(h d) n -> d h n", d=64)), writes=[wbu])
                    P.dma("gpsimd", lambda e: e.dma_start(out=wo[:], in_=w_out.rearrange("(c p) n -> p c n", p=128)), writes=[wo])
                    mixT = sb(s3a, "mixT", [128, 8, 512], BF16)
                    sga = [sb(s3a, "sga%d" % i, [128, 512], F32) for i in range(2)]
                    sgb = [sb(s3a, "sgb%d" % i, [128, 512], F32) for i in range(2)]
                    ma = sb(s3a, "ma", [128, 512], F32)
                    xre = [sb(s3a, "xre%d" % i, [128, D], F32) for i in range(2)]
                    for fc in range(8):
                        sa_ = sga[fc % 2]
                        sb_ = sgb[fc % 2]
                        sg_fn(0, fc, sa_)
                        sg_fn(1, fc, sb_)
                        bA = banks[(2 * fc) % 8]
                        bB = banks[(2 * fc + 1) % 8]
                        for h in range(8):
                            P.op("tensor", lambda e, bA=bA, h=h, fc=fc: e.matmul(bA[:, 0:NT], lhsT=wau[0:64, h, fc * 128:(fc + 1) * 128], rhs=oaT[0:64, h, 0:NT],
                                                                             start=(h == 0), stop=(h == 7)), reads=[wau, oaT], writes=[bA])
                        for h in range(8):
                            P.op("tensor", lambda e, bB=bB, h=h, fc=fc: e.matmul(bB[:, 0:NT], lhsT=wbu[0:64, h, fc * 128:(fc + 1) * 128], rhs=obT[0:64, h, 0:NT],
                                                                             start=(h == 0), stop=(h == 7)), reads=[wbu, obT], writes=[bB])
                        P.op("vector", lambda e, bA=bA, sa_=sa_: e.tensor_tensor(out=ma[:, 0:NT], in0=bA[:, 0:NT], in1=sa_[:, 0:NT], op=ALU.mult), reads=[bA, sa_], writes=[ma])
                        P.op("vector", lambda e, bB=bB, sb_=sb_: e.tensor_tensor(out=sb_[:, 0:NT], in0=bB[:, 0:NT], in1=sb_[:, 0:NT], op=ALU.mult), reads=[bB, sb_], writes=[sb_])
                        P.op("vector", lambda e, sb_=sb_, fc=fc: e.tensor_tensor(out=mixT[:, fc, 0:NT], in0=ma[:, 0:NT], in1=sb_[:, 0:NT], op=ALU.add), reads=[ma, sb_], writes=[mixT])
                    for t, (t0, nt) in enumerate(tiles):
                        xr = xre[t % 2]
                        x_fn(t, xr)
                        for hf in range(2):
                            bk = banks[(2 * t + hf) % 8]
                            for c in range(8):
                                P.op("tensor", lambda e, bk=bk, c=c, t0=t0, nt=nt, hf=hf: e.matmul(bk[0:nt, :], lhsT=mixT[:, c, t0:t0 + nt], rhs=wo[:, c, hf * 512:(hf + 1) * 512],
                                                                                         start=(c == 0), stop=(c == 7)), reads=[mixT, wo], writes=[bk])
                            P.op("vector", lambda e, bk=bk, xr=xr, hf=hf, nt=nt: e.scalar_tensor_tensor(out=z[0:nt, hf * 512:(hf + 1) * 512], in0=xr[0:nt, hf * 512:(hf + 1) * 512],
                                                                                                scalar=float(ALPHA), in1=bk[0:nt, :], op0=ALU.mult, op1=ALU.add),
                                 reads=[bk, xr], writes=[z])
                        fin = layer_norm(z, nt, 0, hres[0:nt, t, :], sm3, st6)
                        P.op("vector", fin, reads=[z, lnr], writes=[hres])
                        for g in range(2):
                            bk = banks[(g + 2 * t) % 8]
                            for cc in range(4):
                                c = 4 * g + cc
                                P.op("tensor", lambda e, bk=bk, cc=cc, c=c, t=t, nt=nt: e.transpose(out=bk[:, cc * 128:cc * 128 + nt], in_=hres[0:nt, t, c * 128:(c + 1) * 128],
                                                                                          identity=ident[0:nt, 0:nt]), reads=[hres, ident], writes=[bk])
                            evac(hT[:, 4 * g:4 * g + 4, t0:t0 + nt], bk[:, :].rearrange("p (c q) -> p c q", c=4)[:, :, 0:nt], [bk], [hT])
                    P.barrier()
                chk(9)
                with ExitStack() as s3b:
                    uT = sb(s3b, "uT", [128, 32, 512], BF16)
                    wf1 = [sb(s3b, "wf1_%d" % i, [128, 8, 512], BF16) for i in range(2)]
                    wf2 = [sb(s3b, "wf2_%d" % i, [128, 8, D], BF16) for i in range(2)]
                    rl = [sb(s3b, "rl%d" % i, [128, 512], F32) for i in range(2)]
                    yst = [sb(s3b, "yst%d" % i, [128, D], F32) for i in range(2)]
                    rl_i = 0
                    for fb in range(8):
                        w1 = wf1[fb % 2]
                        P.dma("gpsimd", lambda e, w1=w1, fb=fb: e.dma_start(out=w1[:], in_=w_ff1[:, fb * 512:(fb + 1) * 512].rearrange("(c p) n -> p c n", p=128)), writes=[w1])
                        for fc in range(4):
                            bk = banks[(fb * 4 + fc) % 8]
                            for c in range(8):
                                P.op("tensor", lambda e, bk=bk, w1=w1, c=c, fc=fc: e.matmul(bk[:, 0:NT], lhsT=w1[:, c, fc * 128:(fc + 1) * 128], rhs=hT[:, c, 0:NT],
                                                                                    start=(c == 0), stop=(c == 7)), reads=[w1, hT], writes=[bk])
                            r_ = rl[rl_i % 2]
                            rl_i += 1
                            P.op("scalar", lambda e, bk=bk, r_=r_: e.activation(out=r_[:, 0:NT], in_=bk[:, 0:NT], func=AF.Relu), reads=[bk], writes=[r_])
                            P.op("vector", lambda e, r_=r_, fb=fb, fc=fc: e.tensor_tensor(out=uT[:, fb * 4 + fc, 0:NT], in0=r_[:, 0:NT], in1=r_[:, 0:NT], op=ALU.mult),
                                 reads=[r_], writes=[uT])
                    for blk in range(4):
                        w2 = wf2[blk % 2]
                        P.dma("gpsimd", lambda e, w2=w2, blk=blk: e.dma_start(out=w2[:], in_=w_ff2[blk * 1024:(blk + 1) * 1024, :].rearrange("(c p) n -> p c n", p=128)), writes=[w2])
                        for cc in range(8):
                            ch = blk * 8 + cc
                            for t, (t0, nt) in enumerate(tiles):
                                for hf in range(2):
                                    bk = banks[2 * t + hf]
                                    P.op("tensor", lambda e, bk=bk, w2=w2, cc=cc, ch=ch, t0=t0, nt=nt, hf=hf: e.matmul(
                                        bk[0:nt, :], lhsT=uT[:, ch, t0:t0 + nt], rhs=w2[:, cc, hf * 512:(hf + 1) * 512], start=(ch == 0), stop=(ch == 31)),
                                        reads=[uT, w2], writes=[bk])
                    for t, (t0, nt) in enumerate(tiles):
                        for hf in range(2):
                            bk = banks[2 * t + hf]
                            P.op("vector", lambda e, bk=bk, t=t, hf=hf, nt=nt: e.scalar_tensor_tensor(out=z[0:nt, hf * 512:(hf + 1) * 512], in0=hres[0:nt, t, hf * 512:(hf + 1) * 512],
                                                                                              scalar=float(ALPHA), in1=bk[0:nt, :], op0=ALU.mult, op1=ALU.add),
                                 reads=[bk, hres], writes=[z])
                        ys = yst[t % 2]
                        fin = layer_norm(z, nt, 2, ys[0:nt, :], sm3, st6)
                        P.op("vector", fin, reads=[z, lnr], writes=[ys])
                        out_toks.append(y_fn(t, ys))
                    P.barrier()


        def qgroup(mq):
            nch = 4 * mq + 4
            ntl = 16 * mq + 16
            with ExitStack() as sq:
                oaT = sb(sq, "oaT", [128, 8, 512], BF16)
                obT = sb(sq, "obT", [128, 8, 512], BF16)
                s2 = ExitStack()
                sq.callback(s2.close)
                mbT = sb(s2, "mbT", [128, 64, 512], BF16)
                qaT_g = sb(s2, "qaT_g", [128, 4, 512], BF16)
                qbT_g = sb(s2, "qbT_g", [128, 4, 512], BF16)
                qiT_g = sb(s2, "qiT_g", [128, 2, 512], BF16)
                P.dma("sync", lambda e: e.dma_start(out=qaT_g[:], in_=qaT_d[mq]), writes=[qaT_g, qTall])
                P.dma("sync", lambda e: e.dma_start(out=qbT_g[:], in_=qbT_d[mq]), writes=[qbT_g, qTall])
                P.dma("sync", lambda e: e.dma_start(out=qiT_g[:], in_=qiT_d[mq]), writes=[qiT_g])
                chk(1)
                with ExitStack() as sa:
                    btab = sb(sa, "btab", [128, 9, 512], F32)
                    P.dma("sync", lambda e: e.dma_start(out=btab[:], in_=btab_d), writes=[btab])
                    kib = [sb(sa, "kib%d" % i, [128, 2048], BF16) for i in range(2)]
                    for i in range(4):
                        qt = 4 * mq + i

                        def emit_scores(c, i=i):
                            kb_ = kib[(c // 4) % 2]
                            if c % 4 == 0:
                                for hf in range(2):
                                    P.dma("sync", lambda e, kb_=kb_, c=c, hf=hf: e.dma_start(
                                        out=kb_[64 * hf:64 * hf + 64, :], in_=kiT_d[:, (c // 4) * 2048:(c // 4 + 1) * 2048]), writes=[kb_])
                            for h in range(4):
                                r0 = 64 * (h % 2)
                                P.op("tensor", lambda e, h=h, r0=r0, c=c, kb_=kb_: e.matmul(
                                    banks[h][:, :], lhsT=qiT_g[r0:r0 + 64, h // 2, i * 128:(i + 1) * 128],
                                    rhs=kb_[r0:r0 + 64, (c % 4) * 512:(c % 4 + 1) * 512], start=True, stop=True),
                                    reads=[qiT_g, kb_], writes=[banks[h]])
                        with ExitStack() as si:
                            indexer(si, 128, nch, emit_scores, lambda k, qt=qt: lohi[:, qt, k:k + 1],
                                    lambda c, i=i: (c if c < 3 else (4 + i if c == nch - 1 else 3)), mbT, i * 128, btab)
                        P.op("vector", lambda e: e.memset(ones_t[0:1, 0:1], 1.0), reads=[mbT], writes=[maskall, ones_t])
                    chk(4)
                    P.barrier()
                chk(5)
                with ExitStack() as sbb:
                    A = AttBufs(sbb)
                    mbBT = sb(sbb, "mbBT", [32, 8, 512], BF16)
                    pastb = sb(sbb, "pastb", [128, 2, 32], F32)
                    P.dma("sync", lambda e: e.dma_start(out=pastb[:], in_=pastb_d[mq]), writes=[pastb])
                    for a in range(2):
                        P.op("vector", lambda e, a=a: e.tensor_tensor(out=pastb[:, a, :], in0=pastb[:, a, :], in1=gb2[:], op=ALU.add),
                             reads=[pastb, gb2], writes=[pastb, pastb_t])
                    for i in range(4):
                        def emit_gate(bk, i=i):
                            for h in range(8):
                                r0 = 64 * (h % 2)
                                P.op("tensor", lambda e, h=h, r0=r0: e.matmul(bk[:, h * 32:(h + 1) * 32], lhsT=qbT_g[r0:r0 + 64, h // 2, i * 128:(i + 1) * 128],
                                                                          rhs=meansTb[r0:r0 + 64, h // 2, :], start=True, stop=True),
                                     reads=[qbT_g, meansTb], writes=[bk])
                        with ExitStack() as sg_:
                            moba_gate(sg_, 128, emit_gate, pastb[:, i // 2, :], 8 * mq + 6 + i // 2, mbBT, i * 128)
                    P.op("vector", lambda e: e.memset(ones_t[0:1, 0:1], 1.0), reads=[mbBT], writes=[maskall, ones_t])
                    chk(6)

                    def diag_fn(u):
                        return (512 - 128 * (u - (ntl - 5))) if u >= ntl - 5 else None
                    if DBG_LEVEL >= 7:
                        attend(A, lambda p: kaT_d[p], lambda: va_d, lambda r0, p: qaT_g[r0:r0 + 64, p, :], lambda h: oaT[0:64, h, :], 0, ntl, 512, diag_fn,
                               lambda u, h: (None, mbT[:, u, :]))
                    chk(7)
                    if DBG_LEVEL >= 8:
                        attend(A, lambda p: kbT_d[p], lambda: vb_d, lambda r0, p: qbT_g[r0:r0 + 64, p, :], lambda h: obT[0:64, h, :], 8, ntl, 512, diag_fn,
                               lambda u, h: (ablk[0:32, u // 2, :], mbBT[0:32, h, :]))
                    chk(8)
                    P.op("vector", lambda e: e.memset(ones_t[0:1, 0:1], 1.0), reads=[oall], writes=[oaT, obT, ones_t])
                    P.barrier()
                s2.close()

                def sg_fn(which, fc, dst):
                    src = sga_d if which == 0 else sgb_d
                    P.dma("sync", lambda e: e.dma_start(out=dst[:], in_=src[mq, fc]), writes=[dst])

                def x_fn(t, xr):
                    r0 = (16 * mq + 12 + t) * 128
                    P.dma("sync", lambda e: e.dma_start(out=xr[:], in_=xs[r0:r0 + 128, :]), writes=[xr])

                def y_fn(t, ys):
                    r0 = (4 * mq + t) * 128
                    return P.dma("sync", lambda e: e.dma_start(out=y_p[r0:r0 + 128, :], in_=ys[:]), reads=[ys])
                phase3(512, [(0, 128), (128, 128), (256, 128), (384, 128)], oaT, obT, sg_fn, x_fn, y_fn)

        for mq_ in range(DBG_NQG):
            try:
                qgroup(mq_)
            except _Stop:
                pass
        def sample_group():
            with ExitStack() as ss1:
                ptr = sb(ss1, "ptr", [128, 256], I32)
                iot = sb(ss1, "iot", [128, 1], I32)
                idx = sb(ss1, "idx", [128, 256], I32)
                P.dma("sync", lambda e: e.dma_start(out=ptr[:], in_=ptrep_d), writes=[ptr])
                P.dma("sync", lambda e: e.dma_start(out=iot[:], in_=iot_d), writes=[iot])
                P.op("vector", lambda e: e.tensor_scalar(out=idx[:], in0=ptr[:], scalar1=128.0, scalar2=iot[:, 0:1], op0=ALU.mult, op1=ALU.add),
                     reads=[ptr, iot], writes=[idx])
                gd = [sb(ss1, "gd%d" % i, [128, 1088], F32) for i in range(8)]
                gm = [sb(ss1, "gm%d" % i, [128, 1024], F32) for i in range(8)]
                kst = [sb(ss1, "kst%d" % i, [128, 512], BF16) for i in range(4)]
                vst = [sb(ss1, "vst%d" % i, [128, 8, 65], BF16) for i in range(4)]
                msum = sb(ss1, "msum", [128, 4, 32], F32)
                for v in vst:
                    P.op("vector", lambda e, v=v: e.memset(v[:], 1.0), writes=[v])
                ki_ = 0
                vi_ = 0
                for s in range(DBG_NSEQ):
                    for blk in range(16):
                        gds = []
                        gms = []
                        for t in range(4):
                            pg = blk * 4 + t
                            col = s * 64 + pg
                            g1 = gd[(blk % 2) * 4 + t]
                            g2 = gm[(blk % 2) * 4 + t]
                            P.dma("gpsimd", lambda e, g1=g1, col=col: e.indirect_dma_start(out=g1[:, :], out_offset=None, in_=cdsa[:, :],
                                                                                         in_offset=bass.IndirectOffsetOnAxis(ap=idx[:, col:col + 1], axis=0)),
                                  reads=[idx], writes=[g1])
                            P.dma("gpsimd", lambda e, g2=g2, col=col: e.indirect_dma_start(out=g2[:, :], out_offset=None, in_=cmoba[:, :],
                                                                                         in_offset=bass.IndirectOffsetOnAxis(ap=idx[:, col:col + 1], axis=0)),
                                  reads=[idx], writes=[g2])
                            gds.append(g1)
                            gms.append(g2)
                        for (srcs, c0, M, dst, cidx) in ([(gds, 128 * c, 128, skaT_d[s, c], None) for c in range(4)] + [(gds, 1024, 64, skiT_d[s], None)]
                                                        + [(gms, 128 * c, 128, skbT_d[s, c], c) for c in range(4)]):
                            bk = next_bank()
                            for t in range(4):
                                P.op("tensor", lambda e, bk=bk, g_=srcs[t], c0=c0, M=M, t=t: e.transpose(out=bk[0:M, t * 128:(t + 1) * 128], in_=g_[:, c0:c0 + M], identity=ident[:]),
                                     reads=[srcs[t], ident], writes=[bk])
                            if cidx is not None:
                                for hb in range(2):
                                    P.op("vector", lambda e, bk=bk, cidx=cidx, blk=blk, hb=hb: e.tensor_reduce(
                                        out=msum[:, cidx, 2 * blk + hb:2 * blk + hb + 1], in_=bk[:, hb * 256:(hb + 1) * 256], axis=AX.X, op=ALU.add), reads=[bk], writes=[msum])
                            st = kst[ki_ % 4]
                            ki_ += 1
                            evac(st[0:M, :], bk[0:M, :], [bk], [st])
                            P.dma("sync", lambda e, st=st, dst=dst, M=M, blk=blk: e.dma_start(out=dst[0:M, blk * 512:(blk + 1) * 512], in_=st[0:M, :]), reads=[st], writes=[sscr])
                        for t in range(4):
                            pg = blk * 4 + t
                            for (g_, dstd) in ((gds[t], sva_d), (gms[t], svb_d)):
                                vs_ = vst[vi_ % 4]
                                vi_ += 1
                                ce_ = "vector" if (vi_ % 2 == 0) else "scalar"
                                if ce_ == "vector":
                                    P.op("vector", lambda e, vs_=vs_, g_=g_: e.tensor_copy(out=vs_[:, :, 0:64], in_=g_[:, 512:1024].rearrange("p (h d) -> p h d", h=8)),
                                         reads=[g_], writes=[vs_])
                                else:
                                    P.op("scalar", lambda e, vs_=vs_, g_=g_: e.activation(out=vs_[:, :, 0:64], in_=g_[:, 512:1024].rearrange("p (h d) -> p h d", h=8), func=AF.Copy),
                                         reads=[g_], writes=[vs_])
                                P.dma("sync", lambda e, vs_=vs_, dstd=dstd, s=s, pg=pg: e.dma_start(out=dstd[s, pg], in_=vs_[:].rearrange("p h d -> p (h d)")), reads=[vs_], writes=[sscr])
                    P.op("vector", lambda e, s=s: e.tensor_scalar(out=smeansTb[s][:], in0=msum[:], scalar1=1.0 / 256.0, scalar2=None, op0=ALU.mult),
                         reads=[msum], writes=[smeansTb[s]])
                P.barrier()
            chk(11)
            with ExitStack() as sq:
                oaTs = sb(sq, "oaTs", [128, 8, 512], BF16)
                obTs = sb(sq, "obTs", [128, 8, 512], BF16)
                s2 = ExitStack()
                sq.callback(s2.close)
                mbTs = sb(s2, "mbTs", [128, 68, NSAMP], BF16)
                with ExitStack() as sa:
                    btab = sb(sa, "btab", [128, 9, 512], F32)
                    P.dma("sync", lambda e: e.dma_start(out=btab[:], in_=btab_d), writes=[btab])
                    kibs = [sb(sa, "kibs%d" % i, [128, 2048], BF16) for i in range(4)]

                    def emit_scores(c):
                        if c % 4 == 0:
                            wd_ = min(2048, 8704 - (c // 4) * 2048)
                            for s in range(4):
                                for hf in range(2):
                                    P.dma("sync", lambda e, s=s, c=c, hf=hf, wd_=wd_: e.dma_start(
                                        out=kibs[s][64 * hf:64 * hf + 64, 0:wd_], in_=skiT_d[s][:, (c // 4) * 2048:(c // 4) * 2048 + wd_]), reads=[sscr], writes=[kibs[s]])
                        for h in range(4):
                            r0 = 64 * (h % 2)
                            for s in range(4):
                                P.op("tensor", lambda e, h=h, r0=r0, c=c, s=s: e.matmul(
                                    banks[h][0:NSAMP, :], lhsT=qiTm[s][r0:r0 + 64, h // 2, :], rhs=kibs[s][r0:r0 + 64, (c % 4) * 512:(c % 4 + 1) * 512],
                                    start=(s == 0), stop=(s == 3)), reads=[qiTm[s], kibs[s]], writes=[banks[h]])
                    with ExitStack() as si:
                        indexer(si, NSAMP, 17, emit_scores, lambda k: lohis[0:NSAMP, k:k + 1], lambda c: (8 if c == 16 else 3), mbTs, 0, btab)
                    P.op("vector", lambda e: e.memset(ones_t[0:1, 0:1], 1.0), reads=[mbTs], writes=[maskall, ones_t])
                    P.barrier()
                chk(12)
                with ExitStack() as sbb:
                    A = AttBufs(sbb)
                    mbBTs = sb(sbb, "mbBTs", [32, 8, NSAMP], BF16)
                    zb = sb(sbb, "zb", [128, 32], F32)
                    P.op("vector", lambda e: e.memset(zb[:], 0.0), writes=[zb, pastb_t])

                    def emit_gate(bk):
                        for h in range(8):
                            r0 = 64 * (h % 2)
                            for s in range(4):
                                P.op("tensor", lambda e, h=h, r0=r0, s=s: e.matmul(bk[0:NSAMP, h * 32:(h + 1) * 32], lhsT=qbTm[s][r0:r0 + 64, h // 2, :],
                                                                               rhs=smeansTb[s][r0:r0 + 64, h // 2, :], start=(s == 0), stop=(s == 3)),
                                     reads=[qbTm[s], smeansTb[s]], writes=[bk])
                    with ExitStack() as sg_:
                        moba_gate(sg_, NSAMP, emit_gate, zb[0:NSAMP, :], None, mbBTs, 0)
                    P.op("vector", lambda e: e.memset(ones_t[0:1, 0:1], 1.0), reads=[mbBTs, sscr], writes=[maskall, ones_t])
                    chk(13)

                    def diag_fn(u):
                        return 512 if u == 63 else (384 if u == 64 else None)
                    for s in range(DBG_NSEQ):
                        q0 = 8 * s
                        attend(A, lambda p, s=s: skaT_d[s, p], lambda s=s: sva_d[s], lambda r0, p, q0=q0: qaTs[r0:r0 + 64, p, q0:q0 + 8],
                               lambda h, q0=q0: oaTs[0:64, h, q0:q0 + 8], 0, 65, 8, diag_fn, lambda u, h, q0=q0: (None, mbTs[:, u, q0:q0 + 8]))
                        attend(A, lambda p, s=s: skbT_d[s, p], lambda s=s: svb_d[s], lambda r0, p, q0=q0: qbTs[r0:r0 + 64, p, q0:q0 + 8],
                               lambda h, q0=q0: obTs[0:64, h, q0:q0 + 8], 8, 65, 8, diag_fn,
                               lambda u, h, q0=q0: ((ablk[0:32, u // 2, :], mbBTs[0:32, h, q0:q0 + 8]) if u < 64 else None))
                    P.op("vector", lambda e: e.memset(ones_t[0:1, 0:1], 1.0), reads=[oall], writes=[oaTs, obTs, ones_t])
                    P.barrier()
                s2.close()
                chk(14)

                def sg_fn(which, fc, dst):
                    src = sgas if which == 0 else sgbs
                    P.op("vector", lambda e: e.tensor_copy(out=dst[:, 0:NSAMP], in_=src[:, fc, :]), reads=[src], writes=[dst])

                def x_fn(t, xr):
                    P.dma("sync", lambda e: e.dma_start(out=xr[0:NSAMP, :], in_=xsm[:, :]), writes=[xr])

                def y_fn(t, ys):
                    return P.dma("sync", lambda e: e.dma_start(out=y_s[:, :], in_=ys[0:NSAMP, :]), reads=[ys])
                phase3(NSAMP, [(0, NSAMP)], oaTs, obTs, sg_fn, x_fn, y_fn)

        if DBG_SAMPLE:
            sample_group()
        P.barrier()
        for e in ["sync"]:
            waits = P._waits(e, dict([t for t in out_toks if t is not None]))
            if waits:
                P.ops[e].append((waits, None, None, 0))
        P.emit()
    return nc


def host_consts(rel_bias):
    ki = np.arange(128)[:, None]
    x = np.arange(1024)[None, :]
    d = x - ki - 384
    bkt = t5_bucket_np(d)
    wt = np.empty((128, 16, 1024), np.float32)
    for h in range(16):
        wt[:, h, :] = np.where(d >= 0, rel_bias[bkt, h], np.float32(NEGM))
    b31 = np.broadcast_to(rel_bias[31][None, :], (128, 16)).astype(np.float32).copy()
    qi = np.arange(128)[:, None]
    kk = np.arange(512)[None, :]
    cm = np.empty((128, 4, 512), np.float32)
    for i in range(4):
        cm[:, i, :] = np.where(kk <= i * 128 + qi, 0.0, -BIG)
    pertb = np.broadcast_to((-EPS_TIE * np.arange(512, dtype=np.float64)).astype(np.float32)[None, :], (128, 512)).copy()
    ablk = np.zeros((32, 32, 128), np.float32)
    for u in range(32):
        ablk[u, u, :] = 1.0
    pastb = np.zeros((4, 128, 2, 32), np.float32)
    for mq in range(4):
        for a in range(2):
            pastb[mq, :, a, 8 * mq + 6 + a:] = -BIG
    return wt, b31, cm, pertb, ablk, pastb


def core_consts(j, cm, pertb):
    nph = (12 - 4 * j) * 128
    phb = np.zeros((128, 1536), np.float32)
    phb[:, :nph] = -BIG
    btab = np.empty((128, 9, 512), np.float32)
    kk = np.arange(512)[None, :]
    btab[:, 8, :] = pertb + np.where(kk <= (np.arange(128)[:, None] % 8), 0.0, -BIG).astype(np.float32)
    for c in range(3):
        btab[:, c, :] = pertb + phb[:, c * 512:(c + 1) * 512]
    btab[:, 3, :] = pertb
    for i in range(4):
        btab[:, 4 + i, :] = pertb + cm[:, i, :]
    bval = np.zeros((128, 32), np.float32)
    bval[:, :nph // 256] = -BIG
    return btab, bval, nph


def kernel(x_prompt, x_sample, cache_dsa, cache_moba, page_table, w_in, rel_bias, w_a_up, w_b_up,
           w_out, ln1_g, ln1_b, w_ff1, w_ff2, ln2_g, ln2_b):
    x_prompt = np.asarray(x_prompt, np.float32)
    x_sample = np.asarray(x_sample, np.float32)
    rel_bias = np.asarray(rel_bias, np.float32)
    wt, b31, cm, pertb, ablk, pastb = host_consts(rel_bias)
    lnrep = np.stack([np.broadcast_to(np.asarray(a, np.float32)[0][None, :], (128, D)) for a in (ln1_g, ln1_b, ln2_g, ln2_b)], axis=1).copy()
    ident = np.eye(128, dtype=np.float32)
    common = dict(w_in=np.ascontiguousarray(np.asarray(w_in, np.float32)[0]),
                  w_a_up=np.ascontiguousarray(np.asarray(w_a_up, np.float32)[0]),
                  w_b_up=np.ascontiguousarray(np.asarray(w_b_up, np.float32)[0]),
                  w_out=np.ascontiguousarray(np.asarray(w_out, np.float32)[0]),
                  w_ff1=np.ascontiguousarray(np.asarray(w_ff1, np.float32)[0]),
                  w_ff2=np.ascontiguousarray(np.asarray(w_ff2, np.float32)[0]),
                  lnrep=lnrep, ident=ident, wtab=wt, b31=b31, ablk=ablk, pastb=pastb)
    page_table = np.asarray(page_table, np.int32)
    iot = np.arange(128, dtype=np.int32)[:, None].copy()
    cdsa = np.asarray(cache_dsa, np.float32)[0].reshape(2560 * 128, 1088)
    cmoba = np.asarray(cache_moba, np.float32)[0].reshape(2560 * 128, 1024)
    in_maps = []
    for c in range(8):
        b, j = c // 4, c % 4
        btab, bval, nph = core_consts(j, cm, pertb)
        xsl = np.zeros((T, D), np.float32)
        xsl[nph:] = x_prompt[b, :T - nph]
        m = dict(common)
        ptrep = np.ascontiguousarray(np.broadcast_to(page_table[4 * c:4 * c + 4].reshape(1, 256), (128, 256)))
        m.update(xs=xsl, xsm=np.ascontiguousarray(x_sample[4 * c:4 * c + 4].reshape(NSAMP, D)), btab=btab, bval=bval, ptrep=ptrep, iot=iot, cdsa=cdsa, cmoba=cmoba)
        in_maps.append(m)
    nc = build_program()
    res = run_bass_kernel_spmd(nc, in_maps, core_ids=list(range(8)))
    y_p = np.zeros((2, T, D), np.float32)
    dsa_pp = np.zeros((1, 2, T, 1088), np.float32)
    moba_pp = np.zeros((1, 2, T, 1024), np.float32)
    y_s = np.zeros((32, 8, D), np.float32)
    dsa_ss = np.zeros((1, 32, 8, 1088), np.float32)
    moba_ss = np.zeros((1, 32, 8, 1024), np.float32)
    for c in range(8):
        b, j = c // 4, c % 4
        r = res.results[c]
        for m in range(4):
            g0 = (4 * m + j) * 512
            y_p[b, g0:g0 + 512] = r["y_p"][m * 512:(m + 1) * 512]
            dsa_pp[0, b, g0:g0 + 512] = r["dsa_p"][m * 512:(m + 1) * 512]
            moba_pp[0, b, g0:g0 + 512] = r["moba_p"][m * 512:(m + 1) * 512]
        y_s[4 * c:4 * c + 4] = r["y_s"].reshape(4, 8, D)
        dsa_ss[0, 4 * c:4 * c + 4] = r["dsa_s"].reshape(4, 8, 1088)
        moba_ss[0, 4 * c:4 * c + 4] = r["moba_s"].reshape(4, 8, 1024)
    return (y_p, y_s, dsa_pp, moba_pp, dsa_ss, moba_ss)
```

```python
import math
import numpy as np
from contextlib import ExitStack
import concourse.bass as bass
import concourse.mybir as mybir
from concourse.bass_utils import run_bass_kernel_spmd

F32 = mybir.dt.float32
BF16 = mybir.dt.bfloat16
I32 = mybir.dt.int32
AF = mybir.ActivationFunctionType
ALU = mybir.AluOpType
AX = mybir.AxisListType

D = 1024
T = 8192
NT = 64
QA, KA, VA, QI, WI, KI, QB, KB, VB, GA, GB, DIN = 0, 512, 1024, 1536, 1792, 1796, 1860, 2372, 2884, 3396, 4420, 5444
ALPHA = 2.0 ** 0.25
LN_EPS = 1e-5
NEGM = -30000.0
BIG = 1e30
NIT = 16
NIT2 = 14
ACT_SPLIT = 0.45
EPS_TIE = 1e-12
DFF = 4096
NSAMP = 32
DBG_NBLK = 16
DBG_SAMPLE = True
DBG_LEVEL = 9
DBG_NQG = 4
DBG_Q = 99
DBG_NSEQ = 4


class _Stop(Exception):
    pass


MUTE = [False]


def chk(k):
    if DBG_Q < k:
        MUTE[0] = True

ENGS = ["sync", "scalar", "gpsimd", "vector", "tensor"]
EPOCH = 4096


class Buf:
    __slots__ = ("w", "r", "x")

    def __init__(self):
        self.w = None
        self.r = []
        self.x = False


class TT:
    def __init__(self, t):
        self.t = t
        self.b = Buf()

    def __getitem__(self, k):
        return self.t[k]


class Prog:
    NDMA = 16

    def __init__(self, nc, es):
        self.nc = nc
        self.es = es
        self.ops = {e: [] for e in ENGS}
        self.cnt = {}
        self.sems = {}
        self.waited = {e: {} for e in ENGS}
        self.dma_n = {e: 0 for e in ENGS}
        self.ncomp = {e: 0 for e in ENGS}
        self.last = {}

    def _sem(self, key):
        if key not in self.sems:
            self.sems[key] = self.es.enter_context(self.nc.semaphore(key))
            self.cnt[key] = 0
        return key

    def _need(self, reads, writes):
        need = {}

        def add(t):
            if t is None:
                return
            k, v = t
            if need.get(k, 0) < v:
                need[k] = v
        for b in reads:
            add(b.w)
        for b in writes:
            add(b.w)
            for r in b.r:
                add(r)
        return need

    def _waits(self, eng, need):
        waits = []
        wd = self.waited[eng]
        for k, v in need.items():
            if wd.get(k, 0) < v:
                wd[k] = v
                waits.append((k, v))
        return waits

    def _mark(self, tok, reads, writes):
        for b in reads:
            b.r.append(tok)
            if len(b.r) > 64:
                mx = {}
                for k, v in b.r:
                    if mx.get(k, 0) < v:
                        mx[k] = v
                b.r = list(mx.items())
        for b in writes:
            b.w = tok
            b.r = []
        self.last[tok[0]] = tok[1]

    @staticmethod
    def _bufs(xs):
        return [x.b if isinstance(x, TT) else x for x in xs]

    def op(self, eng, fn, reads=(), writes=()):
        if MUTE[0]:
            return None
        reads = self._bufs(reads)
        writes = self._bufs(writes)
        xr = [b for b in reads if b.x]
        if xr:
            writes = list(writes) + [b for b in xr if b not in writes]
            reads = [b for b in reads if not b.x]
        waits = self._waits(eng, self._need(reads, writes))
        key = "c_" + eng
        self.cnt[key] = self.cnt.get(key, 0) + 1
        tok = (key, self.cnt[key])
        self.ops[eng].append((waits, fn, key, 1))
        self._mark(tok, reads, writes)
        return tok

    def dma(self, eng, fn, reads=(), writes=()):
        if MUTE[0]:
            return None
        reads = self._bufs(reads)
        writes = self._bufs(writes)
        n = self.dma_n[eng]
        self.dma_n[eng] += 1
        key = self._sem("d_%s_%d" % (eng, n % self.NDMA))
        need = self._need(reads, writes)
        prev = self.cnt[key]
        if prev > 0 and need.get(key, 0) < prev:
            need[key] = prev
        waits = self._waits(eng, need)
        self.cnt[key] += 16
        tok = (key, self.cnt[key])
        self.ops[eng].append((waits, fn, key, 16))
        self._mark(tok, reads, writes)
        return tok

    def barrier(self):
        toks = [(k, v) for k, v in self.cnt.items() if v > 0]
        for e in ENGS:
            waits = self._waits(e, dict(toks))
            if waits:
                self.ops[e].append((waits, None, None, 0))

    def emit(self):
        nc = self.nc
        ref = {}
        for e in ENGS:
            for waits, fn, key, inc in self.ops[e]:
                for k, v in waits:
                    if k.startswith("c_"):
                        ref.setdefault(k, set()).add(v)
        rank = {}
        for k, vs in ref.items():
            for i, v in enumerate(sorted(vs)):
                rank[(k, v)] = i
                self._sem("%s_%d" % (k, i // EPOCH))

        def csem(k, v):
            r = rank[(k, v)]
            return self.sems["%s_%d" % (k, r // EPOCH)], r % EPOCH + 1

        with nc.Block() as block:
            for ename in ENGS:
                ops = self.ops[ename]
                if not ops:
                    continue

                def body(eng, ops=ops, ename=ename):
                    n = 0
                    for waits, fn, key, inc in ops:
                        for k, v in waits:
                            if k.startswith("c_"):
                                sm_, val = csem(k, v)
                                eng.wait_ge(sm_, val)
                            else:
                                eng.wait_ge(self.sems[k], v)
                        if fn is not None:
                            if key.startswith("c_"):
                                n += 1
                                ins = fn(eng)
                                if (key, n) in rank:
                                    sm_, val = csem(key, n)
                                    ins.then_inc(sm_, 1)
                            else:
                                fn(eng).then_inc(self.sems[key], inc)
                getattr(block, ename)(body)


def t5_bucket_np(n):
    n = np.maximum(n, 0)
    nf = np.maximum(n, 1).astype(np.float32)
    large = 16 + (np.log(nf / np.float32(16)) / np.float32(math.log(128 / 16)) * np.float32(16)).astype(np.int32)
    large = np.minimum(large, 31)
    return np.where(n < 16, n, large)


def build_program():
    MUTE[0] = False
    nc = bass.Bass("TRN2", target_bir_lowering=False)

    def din(name, shape, dt=F32):
        return nc.dram_tensor(name, list(shape), dt, kind="ExternalInput").ap()

    def dout(name, shape, dt=F32):
        return nc.dram_tensor(name, list(shape), dt, kind="ExternalOutput").ap()

    def dscr(name, shape, dt):
        return nc.dram_tensor(name, list(shape), dt, kind="Internal").ap()

    xs = din("xs", [T, D])
    xsm = din("xsm", [NSAMP, D])
    w_in = din("w_in", [D, DIN])
    w_a_up = din("w_a_up", [512, D])
    w_b_up = din("w_b_up", [512, D])
    w_out = din("w_out", [D, D])
    w_ff1 = din("w_ff1", [D, DFF])
    w_ff2 = din("w_ff2", [DFF, D])
    lnrep = din("lnrep", [128, 4, D])
    ident_d = din("ident", [128, 128])
    wtab_d = din("wtab", [128, 16, 1024])
    b31_d = din("b31", [128, 16])
    bval_d = din("bval", [128, 32])
    ablk_d = din("ablk", [32, 32, 128])
    btab_d = din("btab", [128, 9, 512])
    pastb_d = din("pastb", [4, 128, 2, 32])

    ptrep_d = din("ptrep", [128, 256], I32)
    iot_d = din("iot", [128, 1], I32)
    cdsa = din("cdsa", [2560 * 128, 1088]) if DBG_SAMPLE else None
    cmoba = din("cmoba", [2560 * 128, 1024]) if DBG_SAMPLE else None
    y_p = dout("y_p", [2048, D])
    y_s = dout("y_s", [NSAMP, D])
    dsa_p = dout("dsa_p", [2048, 1088])
    moba_p = dout("moba_p", [2048, 1024])
    dsa_s = dout("dsa_s", [NSAMP, 1088])
    moba_s = dout("moba_s", [NSAMP, 1024])

    kaT_d = dscr("kaT_d", [4, 128, T], BF16)
    kbT_d = dscr("kbT_d", [4, 128, T], BF16)
    kiT_d = dscr("kiT_d", [64, T], BF16)
    va_d = dscr("va_d", [NT, 128, 520], BF16)
    vb_d = dscr("vb_d", [NT, 128, 520], BF16)
    qaT_d = dscr("qaT_d", [4, 128, 4, 512], BF16)
    qbT_d = dscr("qbT_d", [4, 128, 4, 512], BF16)
    qiT_d = dscr("qiT_d", [4, 128, 2, 512], BF16)
    sga_d = dscr("sga_d", [4, 8, 128, 512], F32)
    sgb_d = dscr("sgb_d", [4, 8, 128, 512], F32)
    skaT_d = dscr("skaT_d", [4, 4, 128, 8320], BF16)
    skbT_d = dscr("skbT_d", [4, 4, 128, 8320], BF16)
    skiT_d = dscr("skiT_d", [4, 64, 8704], BF16)
    sva_d = dscr("sva_d", [4, 65, 128, 520], BF16)
    svb_d = dscr("svb_d", [4, 65, 128, 520], BF16)

    out_toks = []

    with ExitStack() as es:
        P = Prog(nc, es)

        uid = [0]

        def sb(st, name, shape, dt):
            uid[0] += 1
            try:
                return TT(st.enter_context(nc.sbuf_tensor("s_%s_%d" % (name, uid[0]), list(shape), dt)))
            except BaseException as ex:
                print("SB ALLOC FAIL", name, shape, ex)
                raise

        def ps(st, name, shape, dt):
            return TT(st.enter_context(nc.psum_tensor("p_" + name, list(shape), dt)))

        banks = [ps(es, "bank%d" % i, [128, 512], F32) for i in range(8)]
        for bk_ in banks:
            bk_.b.x = True
        ident = sb(es, "identf", [128, 128], F32)
        identb = sb(es, "identb", [128, 128], BF16)
        b31 = sb(es, "b31", [128, 16], F32)
        lohi = sb(es, "lohi", [128, 16, 8], F32)
        lohis = sb(es, "lohis", [128, 8], F32)
        meansT = sb(es, "meansT", [128, 4, 32], F32)
        meansTb = sb(es, "meansTb", [128, 4, 32], BF16)
        ones_t = sb(es, "ones_t", [128, 64], F32)
        P.dma("sync", lambda e: e.dma_start(out=ident[:], in_=ident_d), writes=[ident])
        P.dma("gpsimd", lambda e: e.dma_start(out=identb[:], in_=ident_d), writes=[identb])
        P.dma("sync", lambda e: e.dma_start(out=b31[:], in_=b31_d), writes=[b31])
        P.op("vector", lambda e: e.memset(ones_t[:], 1.0), writes=[ones_t])

        qaTs = sb(es, "qaTs", [128, 4, NSAMP], BF16)
        qbTs = sb(es, "qbTs", [128, 4, NSAMP], BF16)
        qiTs = sb(es, "qiTs", [128, 2, NSAMP], BF16)
        qiTm = [sb(es, "qiTm%d" % i, [128, 2, NSAMP], BF16) for i in range(4)]
        qbTm = [sb(es, "qbTm%d" % i, [128, 4, NSAMP], BF16) for i in range(4)]
        sgas = sb(es, "sgas", [128, 8, NSAMP], F32)
        sgbs = sb(es, "sgbs", [128, 8, NSAMP], F32)
        smeansTb = [sb(es, "smeansTb%d" % i, [128, 4, 32], BF16) for i in range(4)]
        qTall = TT(None)
        maskall = TT(None)
        oall = TT(None)
        pastb_t = TT(None)
        sscr = TT(None)
        bank_rr = [0]

        def next_bank(lo=0, hi=8):
            i = bank_rr[0]
            if not (lo <= i < hi):
                i = lo
            bank_rr[0] = i + 1 if i + 1 < hi else lo
            return banks[i]

        evac_rr = [0]

        def evac(out_ap, in_ap, reads, writes, scale=None, func=None, eng=None):
            if eng is None:
                eng = "scalar" if (evac_rr[0] % 2 == 0) else "vector"
                evac_rr[0] += 1
            if func is not None:
                eng = "scalar"
            if eng == "scalar":
                f = func if func is not None else AF.Copy
                if scale is None:
                    P.op("scalar", lambda e: e.activation(out=out_ap, in_=in_ap, func=f), reads=reads, writes=writes)
                else:
                    P.op("scalar", lambda e: e.activation(out=out_ap, in_=in_ap, func=f, scale=scale), reads=reads, writes=writes)
            else:
                if scale is None:
                    P.op("vector", lambda e: e.tensor_copy(out=out_ap, in_=in_ap), reads=reads, writes=writes)
                else:
                    P.op("vector", lambda e: e.tensor_scalar(out=out_ap, in0=in_ap, scalar1=float(scale), scalar2=None, op0=ALU.mult),
                         reads=reads, writes=writes)

        with ExitStack() as s1:
            win = sb(s1, "win", [128, 8, DIN], BF16)
            for c in range(8):
                P.dma("gpsimd", lambda e, c=c: e.dma_start(out=win[:, c, :], in_=w_in[c * 128:(c + 1) * 128, :]), writes=[win])
            xin = [sb(s1, "xin%d" % i, [128, 4, D], F32) for i in range(2)]
            XT = [sb(s1, "XT%d" % i, [128, 8, 512], BF16) for i in range(2)]
            kstg = [sb(s1, "kstg%d" % i, [128, 512], BF16) for i in range(4)]
            vstg = [sb(s1, "vstg%d" % i, [128, 8, 65], BF16) for i in range(4)]
            fstg = [sb(s1, "fstg%d" % i, [128, 512], F32) for i in range(3)]
            rowd = [sb(s1, "rowd%d" % i, [128, 1088], F32) for i in range(2)]
            rowm = [sb(s1, "rowm%d" % i, [128, 1024], F32) for i in range(2)]
            qiw = sb(s1, "qiw", [128, 256], F32)
            wsb = sb(s1, "wsb", [128, 4], F32)
            qistg = sb(s1, "qistg", [128, 2, 512], BF16)
            for v in vstg:
                P.op("vector", lambda e, v=v: e.memset(v[:], 1.0), writes=[v])
            kst_i = [0]
            vst_i = [0]
            fst_i = [0]

            def proj_fm(col0, M, xt, scale=None, func=None, N=512):
                bk = next_bank()
                for dc in range(8):
                    P.op("tensor", lambda e, dc=dc, bk=bk: e.matmul(bk[0:M, 0:N], lhsT=win[:, dc, col0:col0 + M], rhs=xt[:, dc, 0:N],
                                                                  start=(dc == 0), stop=(dc == 7)),
                         reads=[win, xt], writes=[bk])
                return bk

            def proj_tm(col0, N, xt, t, ntok=128):
                bk = next_bank()
                for dc in range(8):
                    P.op("tensor", lambda e, dc=dc, bk=bk: e.matmul(bk[0:ntok, 0:N], lhsT=xt[:, dc, t * 128:t * 128 + ntok],
                                                                  rhs=win[:, dc, col0:col0 + N], start=(dc == 0), stop=(dc == 7)),
                         reads=[win, xt], writes=[bk])
                return bk

            def load_xT(src_rows_ap, xi, xt, ntile, ntok=128, preloaded=False):
                for t in range(ntile):
                    pass
                if not preloaded:
                    P.dma("gpsimd", lambda e: e.dma_start(out=xi[0:ntok, 0:ntile, :], in_=src_rows_ap.rearrange("(t p) d -> p t d", p=ntok)),
                          writes=[xi])
                for c in range(8):
                    bk = next_bank()
                    for t in range(ntile):
                        P.op("tensor", lambda e, c=c, t=t, bk=bk: e.transpose(out=bk[:, t * 128:t * 128 + ntok],
                                                                            in_=xi[0:ntok, t, c * 128:(c + 1) * 128],
                                                                            identity=ident[0:ntok, 0:ntok]),
                             reads=[xi, ident], writes=[bk])
                    w = ntile * 128 if ntok == 128 else ntok
                    evac(xt[:, c, 0:w], bk[:, 0:w], [bk], [xt])

            for sbk in range(DBG_NBLK):
                xi = xin[sbk % 2]
                xt = XT[sbk % 2]
                if sbk == 0:
                    P.dma("gpsimd", lambda e, xi=xi: e.dma_start(out=xi[:, :, :], in_=xs[0:512, :].rearrange("(t p) d -> p t d", p=128)), writes=[xi])
                load_xT(xs[sbk * 512:(sbk + 1) * 512, :], xi, xt, 4, preloaded=True)
                if sbk + 1 < DBG_NBLK:
                    xn = xin[(sbk + 1) % 2]
                    P.dma("gpsimd", lambda e, xn=xn, sbk=sbk: e.dma_start(out=xn[:, :, :], in_=xs[(sbk + 1) * 512:(sbk + 2) * 512, :].rearrange("(t p) d -> p t d", p=128)),
                          writes=[xn])
                own = (sbk % 4 == 3) and DBG_LEVEL >= 5
                mq = sbk // 4
                for (col0, M, dst, is_kb, cidx) in [] if DBG_LEVEL < 3 else (
                        [(KA + 128 * c, 128, kaT_d[c], False, c) for c in range(4)]
                        + [(KI, 64, kiT_d, False, 0)]
                        + [(KB + 128 * c, 128, kbT_d[c], True, c) for c in range(4)]):
                    bk = proj_fm(col0, M, xt)
                    st = kstg[kst_i[0] % 4]
                    kst_i[0] += 1
                    evac(st[0:M, :], bk[0:M, :], [bk], [st])
                    if is_kb:
                        for hb in range(2):
                            P.op("vector", lambda e, bk=bk, cidx=cidx, sbk=sbk, hb=hb: e.tensor_reduce(
                                out=meansT[:, cidx, 2 * sbk + hb:2 * sbk + hb + 1], in_=bk[:, hb * 256:(hb + 1) * 256],
                                axis=AX.X, op=ALU.add), reads=[bk], writes=[meansT])
                    P.dma("sync", lambda e, st=st, dst=dst, M=M, sbk=sbk: e.dma_start(out=dst[0:M, sbk * 512:(sbk + 1) * 512], in_=st[0:M, :]),
                          reads=[st])
                for t in range(4 if DBG_LEVEL >= 4 else 0):
                    u = sbk * 4 + t
                    if own:
                        rd = rowd[t % 2]
                        rm = rowm[t % 2]
                    for (col0, dst, which) in ((VA, va_d, 0), (VB, vb_d, 1)):
                        bk = proj_tm(col0, 512, xt, t)
                        vs_ = vstg[vst_i[0] % 4]
                        vst_i[0] += 1
                        evac(vs_[:, :, 0:64], bk[:, :].rearrange("p (h d) -> p h d", h=8), [bk], [vs_])
                        P.dma("sync", lambda e, vs_=vs_, dst=dst, u=u: e.dma_start(out=dst[u], in_=vs_[:].rearrange("p h d -> p (h d)")),
                              reads=[vs_])
                        if own:
                            tgt = rd if which == 0 else rm
                            evac(tgt[:, 512:1024], bk[:, :], [bk], [tgt])
                    if own:
                        bk = proj_tm(KA, 512, xt, t)
                        evac(rd[:, 0:512], bk[:, :], [bk], [rd])
                        bk = proj_tm(KI, 64, xt, t)
                        evac(rd[:, 1024:1088], bk[:, 0:64], [bk], [rd])
                        bk = proj_tm(KB, 512, xt, t)
                        evac(rm[:, 0:512], bk[:, :], [bk], [rm])
                        r0 = (mq * 4 + t) * 128
                        out_toks.append(P.dma("sync", lambda e, rd=rd, r0=r0: e.dma_start(out=dsa_p[r0:r0 + 128, :], in_=rd[:]), reads=[rd]))
                        out_toks.append(P.dma("sync", lambda e, rm=rm, r0=r0: e.dma_start(out=moba_p[r0:r0 + 128, :], in_=rm[:]), reads=[rm]))
                        qt = mq * 4 + t
                        bk = proj_tm(WI, 4, xt, t)
                        P.op("vector", lambda e, bk=bk: e.tensor_copy(out=wsb[:], in_=bk[:, 0:4]), reads=[bk], writes=[wsb])
                        P.op("vector", lambda e, qt=qt: e.tensor_scalar(out=lohi[:, qt, 0:4], in0=wsb[:], scalar1=0.0, scalar2=-BIG,
                                                                    op0=ALU.is_le, op1=ALU.mult), reads=[wsb], writes=[lohi])
                        P.op("vector", lambda e, qt=qt: e.tensor_scalar(out=lohi[:, qt, 4:8], in0=wsb[:], scalar1=0.0, scalar2=BIG,
                                                                    op0=ALU.is_gt, op1=ALU.mult), reads=[wsb], writes=[lohi])
                        bk = proj_tm(QI, 256, xt, t)
                        for h in range(4):
                            P.op("vector", lambda e, bk=bk, h=h: e.tensor_scalar(out=qiw[:, h * 64:(h + 1) * 64], in0=bk[:, h * 64:(h + 1) * 64],
                                                                             scalar1=wsb[:, h:h + 1], scalar2=None, op0=ALU.mult),
                                 reads=[bk, wsb], writes=[qiw])
                        for pc in range(2):
                            bk2 = next_bank()
                            P.op("tensor", lambda e, bk2=bk2, pc=pc: e.transpose(out=bk2[:, 0:128], in_=qiw[:, pc * 128:(pc + 1) * 128],
                                                                               identity=ident[:]), reads=[qiw, ident], writes=[bk2])
                            evac(qistg[:, pc, t * 128:(t + 1) * 128], bk2[:, 0:128], [bk2], [qistg])
                if own:
                    P.dma("sync", lambda e, mq=mq: e.dma_start(out=qiT_d[mq], in_=qistg[:]), reads=[qistg])
                    for (col0, dst) in ((QA, qaT_d), (QB, qbT_d)):
                        for c in range(4):
                            bk = proj_fm(col0 + 128 * c, 128, xt)
                            st = kstg[kst_i[0] % 4]
                            kst_i[0] += 1
                            evac(st[:], bk[:], [bk], [st], scale=0.125)
                            P.dma("sync", lambda e, st=st, dst=dst, mq=mq, c=c: e.dma_start(out=dst[mq, :, c, :], in_=st[:]), reads=[st])
                    for (col0, dst) in ((GA, sga_d), (GB, sgb_d)):
                        for c in range(8):
                            bk = proj_fm(col0 + 128 * c, 128, xt)
                            st = fstg[fst_i[0] % 3]
                            fst_i[0] += 1
                            evac(st[:], bk[:], [bk], [st], func=AF.Sigmoid)
                            P.dma("sync", lambda e, st=st, dst=dst, mq=mq, c=c: e.dma_start(out=dst[mq, c], in_=st[:]), reads=[st])

            xi = xin[0]
            assert True
            xt = XT[0]
            load_xT(xsm[:, :], xi, xt, 1, ntok=NSAMP)
            rd = rowd[0]
            rm = rowm[0]
            for (col0, N, tgt, o0) in ((KA, 512, rd, 0), (VA, 512, rd, 512), (KI, 64, rd, 1024), (KB, 512, rm, 0), (VB, 512, rm, 512)):
                bk = proj_tm(col0, N, xt, 0, ntok=NSAMP)
                evac(tgt[0:NSAMP, o0:o0 + N], bk[0:NSAMP, 0:N], [bk], [tgt])
            out_toks.append(P.dma("sync", lambda e: e.dma_start(out=dsa_s, in_=rd[0:NSAMP, :]), reads=[rd]))
            out_toks.append(P.dma("sync", lambda e: e.dma_start(out=moba_s, in_=rm[0:NSAMP, :]), reads=[rm]))
            if DBG_SAMPLE:
                zt = sb(s1, "zt", [128, 520], BF16)
                P.op("vector", lambda e: e.memset(zt[:], 0.0), writes=[zt])
                for (col0, dstT) in ((QA, qaTs), (QB, qbTs)):
                    for c in range(4):
                        bk = proj_fm(col0 + 128 * c, 128, xt, N=NSAMP)
                        evac(dstT[:, c, :], bk[:, 0:NSAMP], [bk], [dstT, qTall], scale=0.125)
                for (col0, dstT) in ((GA, sgas), (GB, sgbs)):
                    for c in range(8):
                        bk = proj_fm(col0 + 128 * c, 128, xt, N=NSAMP)
                        evac(dstT[:, c, :], bk[:, 0:NSAMP], [bk], [dstT], func=AF.Sigmoid)
                for (col0, M, dstd, cidx) in ([(KA + 128 * c, 128, skaT_d, c) for c in range(4)] + [(KB + 128 * c, 128, skbT_d, c) for c in range(4)] + [(KI, 64, skiT_d, None)]):
                    bk = proj_fm(col0, M, xt, N=NSAMP)
                    st = kstg[kst_i[0] % 4]
                    kst_i[0] += 1
                    evac(st[0:M, 0:NSAMP], bk[0:M, 0:NSAMP], [bk], [st])
                    for s in range(4):
                        dd = dstd[s, cidx] if cidx is not None else dstd[s]
                        P.dma("sync", lambda e, st=st, dd=dd, M=M, s=s: e.dma_start(out=dd[0:M, 8192:8200], in_=st[0:M, 8 * s:8 * s + 8]), reads=[st], writes=[sscr])
                        wz = (8320 - 8200) if cidx is not None else (8704 - 8200)
                        P.dma("sync", lambda e, dd=dd, M=M, wz=wz: e.dma_start(out=dd[0:M, 8200:8200 + wz], in_=zt[0:M, 0:wz]), reads=[zt], writes=[sscr])
                for (src, o0, dstd) in ((rd, 512, sva_d), (rm, 512, svb_d)):
                    vs_ = vstg[vst_i[0] % 4]
                    vst_i[0] += 1
                    P.op("vector", lambda e, vs_=vs_, src=src, o0=o0: e.tensor_copy(out=vs_[0:NSAMP, :, 0:64], in_=src[0:NSAMP, o0:o0 + 512].rearrange("p (h d) -> p h d", h=8)),
                         reads=[src], writes=[vs_])
                    for s in range(4):
                        P.dma("sync", lambda e, vs_=vs_, dstd=dstd, s=s: e.dma_start(out=dstd[s, 64, 0:8, :], in_=vs_[8 * s:8 * s + 8].rearrange("p h d -> p (h d)")),
                              reads=[vs_], writes=[sscr])
                        P.dma("sync", lambda e, dstd=dstd, s=s: e.dma_start(out=dstd[s, 64, 8:128, :], in_=zt[0:120, :]), reads=[zt], writes=[sscr])
                bk = proj_tm(WI, 4, xt, 0, ntok=NSAMP)
                P.op("vector", lambda e, bk=bk: e.tensor_copy(out=wsb[0:NSAMP, :], in_=bk[0:NSAMP, 0:4]), reads=[bk], writes=[wsb])
                P.op("vector", lambda e: e.tensor_scalar(out=lohis[0:NSAMP, 0:4], in0=wsb[0:NSAMP, :], scalar1=0.0, scalar2=-BIG, op0=ALU.is_le, op1=ALU.mult), reads=[wsb], writes=[lohis])
                P.op("vector", lambda e: e.tensor_scalar(out=lohis[0:NSAMP, 4:8], in0=wsb[0:NSAMP, :], scalar1=0.0, scalar2=BIG, op0=ALU.is_gt, op1=ALU.mult), reads=[wsb], writes=[lohis])
                bk = proj_tm(QI, 256, xt, 0, ntok=NSAMP)
                for h in range(4):
                    P.op("vector", lambda e, bk=bk, h=h: e.tensor_scalar(out=qiw[0:NSAMP, h * 64:(h + 1) * 64], in0=bk[0:NSAMP, h * 64:(h + 1) * 64],
                                                                     scalar1=wsb[0:NSAMP, h:h + 1], scalar2=None, op0=ALU.mult), reads=[bk, wsb], writes=[qiw])
                for pc in range(2):
                    bk2 = next_bank()
                    P.op("tensor", lambda e, bk2=bk2, pc=pc: e.transpose(out=bk2[:, 0:NSAMP], in_=qiw[0:NSAMP, pc * 128:(pc + 1) * 128], identity=ident[0:NSAMP, 0:NSAMP]),
                         reads=[qiw, ident], writes=[bk2])
                    evac(qiTs[:, pc, :], bk2[:, 0:NSAMP], [bk2], [qiTs])
                for s in range(4):
                    P.op("vector", lambda e, s=s: e.memset(qiTm[s][:], 0.0), writes=[qiTm[s]])
                    P.op("vector", lambda e, s=s: e.tensor_copy(out=qiTm[s][:, :, 8 * s:8 * s + 8], in_=qiTs[:, :, 8 * s:8 * s + 8]), reads=[qiTs], writes=[qiTm[s]])
                    P.op("vector", lambda e, s=s: e.memset(qbTm[s][:], 0.0), writes=[qbTm[s]])
                    P.op("vector", lambda e, s=s: e.tensor_copy(out=qbTm[s][:, :, 8 * s:8 * s + 8], in_=qbTs[:, :, 8 * s:8 * s + 8]), reads=[qbTs], writes=[qbTm[s]])
            P.op("vector", lambda e: e.tensor_scalar(out=meansTb[:], in0=meansT[:], scalar1=1.0 / 256.0, scalar2=None, op0=ALU.mult),
                 reads=[meansT], writes=[meansTb])
            P.barrier()

        lnr = sb(es, "lnr", [128, 4, D], F32)
        P.dma("sync", lambda e: e.dma_start(out=lnr[:], in_=lnrep), writes=[lnr])
        ablk = sb(es, "ablk", [32, 32, 128], BF16)
        P.dma("gpsimd", lambda e: e.dma_start(out=ablk[:], in_=ablk_d), writes=[ablk])
        gb2 = sb(es, "gb2", [128, 32], F32)
        P.dma("sync", lambda e: e.dma_start(out=gb2[:], in_=bval_d), writes=[gb2])

        def layer_norm(z, nt, gi, out_ap, sm, st6):
            for hf in range(2):
                P.op("vector", lambda e, hf=hf: e.bn_stats(out=st6[0:nt, hf, :], in_=z[0:nt, hf * 512:(hf + 1) * 512]), reads=[z], writes=[st6])
            P.op("vector", lambda e: e.bn_aggr(out=sm[0:nt, 0:2], in_=st6[0:nt].rearrange("p a b -> p (a b)")), reads=[st6], writes=[sm])
            P.op("vector", lambda e: e.tensor_scalar(out=sm[0:nt, 2:3], in0=sm[0:nt, 1:2], scalar1=LN_EPS, scalar2=None, op0=ALU.add), reads=[sm], writes=[sm])
            P.op("scalar", lambda e: e.activation(out=sm[0:nt, 3:4], in_=sm[0:nt, 2:3], func=AF.Sqrt), reads=[sm], writes=[sm])
            P.op("vector", lambda e: e.reciprocal(out=sm[0:nt, 4:5], in_=sm[0:nt, 3:4]), reads=[sm], writes=[sm])
            P.op("vector", lambda e: e.tensor_scalar(out=z[0:nt, :], in0=z[0:nt, :], scalar1=sm[0:nt, 0:1], scalar2=sm[0:nt, 4:5], op0=ALU.subtract, op1=ALU.mult),
                 reads=[z, sm], writes=[z])
            P.op("vector", lambda e: e.tensor_tensor(out=z[0:nt, :], in0=z[0:nt, :], in1=lnr[0:nt, gi, :], op=ALU.mult), reads=[z, lnr], writes=[z])
            return lambda e: e.tensor_tensor(out=out_ap, in0=z[0:nt, :], in1=lnr[0:nt, gi + 1, :], op=ALU.add)

        def indexer(st, NP, nch, emit_scores, lohi_fn, bi_fn, mbT, qoff, btab):
            nk = nch * 512
            n1 = (int(nk * ACT_SPLIT) // 512) * 512 if nk >= 2048 else nk
            n2 = nk - n1
            sc = sb(st, "sc", [128, nk], F32)
            junk = sb(st, "junk", [128, n1], BF16)
            tmp = [sb(st, "tmpr%d" % i, [128, 512], F32) for i in range(2)]
            mbc = [sb(st, "mbc%d" % i, [128, 512], F32) for i in range(2)]
            sm = sb(st, "sma", [128, 64], F32)
            CMIN, CMAX, LO, HI, MID, CNT, GE, D1, D2 = 0, 20, 40, 41, 42, 43, 44, 45, 46
            tmp_i = 0
            for c in range(nch):
                emit_scores(c)
                scc = sc[0:NP, c * 512:(c + 1) * 512]
                P.op("vector", lambda e, scc=scc: e.tensor_scalar(out=scc, in0=banks[0][0:NP, :], scalar1=lohi_fn(0), scalar2=lohi_fn(4), op0=ALU.max, op1=ALU.min),
                     reads=[banks[0], lohi, lohis], writes=[sc])
                for h in range(1, 4):
                    tp = tmp[tmp_i % 2]
                    tmp_i += 1
                    P.op("vector", lambda e, tp=tp, h=h: e.tensor_scalar(out=tp[0:NP, :], in0=banks[h][0:NP, :], scalar1=lohi_fn(h), scalar2=lohi_fn(4 + h),
                                                                     op0=ALU.max, op1=ALU.min), reads=[banks[h], lohi, lohis], writes=[tp])
                    P.op("vector", lambda e, tp=tp, scc=scc: e.tensor_tensor(out=scc, in0=scc, in1=tp[0:NP, :], op=ALU.add), reads=[sc, tp], writes=[sc])
                P.op("vector", lambda e, scc=scc, c=c: e.tensor_reduce(out=sm[0:NP, CMIN + c:CMIN + c + 1], in_=scc, axis=AX.X, op=ALU.min), reads=[sc], writes=[sm])
                P.op("vector", lambda e, scc=scc, c=c: e.tensor_reduce(out=sm[0:NP, CMAX + c:CMAX + c + 1], in_=scc, axis=AX.X, op=ALU.max), reads=[sc], writes=[sm])
                bi = bi_fn(c)
                P.op("vector", lambda e, scc=scc, c=c, bi=bi: e.scalar_tensor_tensor(out=scc, in0=scc, scalar=float(-EPS_TIE * 512 * c), in1=btab[0:NP, bi, :],
                                                                                 op0=ALU.add, op1=ALU.add), reads=[sc, btab], writes=[sc])
            chk(2)
            P.op("vector", lambda e: e.tensor_reduce(out=sm[0:NP, LO:LO + 1], in_=sm[0:NP, CMIN:CMIN + nch], axis=AX.X, op=ALU.min), reads=[sm], writes=[sm])
            P.op("vector", lambda e: e.tensor_reduce(out=sm[0:NP, HI:HI + 1], in_=sm[0:NP, CMAX:CMAX + nch], axis=AX.X, op=ALU.max), reads=[sm], writes=[sm])
            P.op("vector", lambda e: e.tensor_scalar(out=sm[0:NP, LO:LO + 1], in0=sm[0:NP, LO:LO + 1], scalar1=-1.0, scalar2=None, op0=ALU.add), reads=[sm], writes=[sm])
            P.op("vector", lambda e: e.tensor_scalar(out=sm[0:NP, HI:HI + 1], in0=sm[0:NP, HI:HI + 1], scalar1=1.0, scalar2=None, op0=ALU.add), reads=[sm], writes=[sm])
            steps = [None] * NIT + [float(-EPS_TIE * (nk + 64)), 1e-30] + [None] * NIT2
            midt = sb(st, "midt", [128, 2], F32)
            sact = sb(st, "sact", [128, 2], F32)
            junk2 = sb(st, "junk2", [128, max(n2, 8)], BF16)
            for pv in steps:
                if pv is None:
                    P.op("vector", lambda e: e.tensor_scalar(out=midt[0:NP, 0:1], in0=sm[0:NP, LO:LO + 1], scalar1=sm[0:NP, HI:HI + 1], scalar2=0.5,
                                                             op0=ALU.add, op1=ALU.mult), reads=[sm], writes=[midt])
                else:
                    P.op("vector", lambda e, pv=pv: e.tensor_scalar(out=midt[0:NP, 0:1], in0=sm[0:NP, LO:LO + 1], scalar1=pv, scalar2=sm[0:NP, HI:HI + 1],
                                                                  op0=ALU.max, op1=ALU.min), reads=[sm], writes=[midt])
                if n2 > 0:
                    P.op("scalar", lambda e: e.activation(out=junk2[0:NP, 0:n2], in_=sc[0:NP, n1:nk], func=AF.Sign, bias=midt[0:NP, 0:1], scale=-1.0,
                                                          accum_out=sact[0:NP, 0:1]), reads=[sc, midt], writes=[junk2, sact])
                P.op("vector", lambda e: e.tensor_scalar(out=junk[0:NP, 0:n1], in0=sc[0:NP, 0:n1], scalar1=midt[0:NP, 0:1], scalar2=0.0,
                                                         op0=ALU.is_ge, op1=ALU.add, accum_out=sm[0:NP, CNT:CNT + 1]), reads=[sc, midt], writes=[junk, sm])
                if n2 > 0:
                    P.op("vector", lambda e: e.scalar_tensor_tensor(out=sm[0:NP, CNT:CNT + 1], in0=sact[0:NP, 0:1], scalar=-0.5, in1=sm[0:NP, CNT:CNT + 1],
                                                                    op0=ALU.mult, op1=ALU.add), reads=[sm, sact], writes=[sm])
                P.op("vector", lambda e: e.tensor_scalar(out=sm[0:NP, GE:GE + 1], in0=sm[0:NP, CNT:CNT + 1], scalar1=float(255.5 - 0.5 * n2), scalar2=None, op0=ALU.is_ge),
                     reads=[sm], writes=[sm])
                P.op("vector", lambda e: e.tensor_tensor(out=sm[0:NP, D1:D1 + 1], in0=midt[0:NP, 0:1], in1=sm[0:NP, LO:LO + 1], op=ALU.subtract), reads=[sm, midt], writes=[sm])
                P.op("vector", lambda e: e.tensor_tensor(out=sm[0:NP, D2:D2 + 1], in0=sm[0:NP, HI:HI + 1], in1=midt[0:NP, 0:1], op=ALU.subtract), reads=[sm, midt], writes=[sm])
                P.op("vector", lambda e: e.scalar_tensor_tensor(out=sm[0:NP, LO:LO + 1], in0=sm[0:NP, D1:D1 + 1], scalar=sm[0:NP, GE:GE + 1], in1=sm[0:NP, LO:LO + 1],
                                                                op0=ALU.mult, op1=ALU.add), reads=[sm], writes=[sm])
                P.op("vector", lambda e: e.scalar_tensor_tensor(out=sm[0:NP, HI:HI + 1], in0=sm[0:NP, D2:D2 + 1], scalar=sm[0:NP, GE:GE + 1], in1=midt[0:NP, 0:1],
                                                                op0=ALU.mult, op1=ALU.add), reads=[sm, midt], writes=[sm])
            chk(3)
            for c in range(nch):
                mb_ = mbc[c % 2]
                P.op("vector", lambda e, mb_=mb_, c=c: e.tensor_scalar(out=mb_[0:NP, :], in0=sc[0:NP, c * 512:(c + 1) * 512], scalar1=sm[0:NP, LO:LO + 1],
                                                                   scalar2=None, op0=ALU.is_ge), reads=[sc, sm], writes=[mb_])
                bk = banks[4 + (c % 4)]
                for t in range(4):
                    P.op("tensor", lambda e, bk=bk, mb_=mb_, t=t: e.transpose(out=bk[:, t * 128:t * 128 + NP], in_=mb_[0:NP, t * 128:(t + 1) * 128],
                                                                         identity=ident[0:NP, 0:NP]), reads=[mb_, ident], writes=[bk])
                evac(mbT[:, 4 * c:4 * c + 4, qoff:qoff + NP], bk[:, :].rearrange("p (t q) -> p t q", t=4)[:, :, 0:NP], [bk], [mbT], eng="scalar")

        def moba_gate(st, NP, emit_gate, bias_ap, ubo, mbBT, qoff):
            gsb = sb(st, "gsb", [128, 8, 32], F32)
            m8 = sb(st, "m8", [128, 8, 8], F32)
            thr = sb(st, "thr", [128, 8], F32)
            mbB = sb(st, "mbB", [128, 8, 32], F32)
            bk = banks[0]
            emit_gate(bk)
            for h in range(8):
                P.op("vector", lambda e, h=h: e.tensor_tensor(out=gsb[0:NP, h, :], in0=bk[0:NP, h * 32:(h + 1) * 32], in1=bias_ap, op=ALU.add),
                     reads=[bk, pastb_t], writes=[gsb])
            for h in range(8):
                P.op("vector", lambda e, h=h: e.max(out=m8[0:NP, h, :], in_=gsb[0:NP, h, :]), reads=[gsb], writes=[m8])
            P.op("vector", lambda e: e.tensor_scalar(out=thr[0:NP, :], in0=m8[0:NP, :, 2], scalar1=-1e29, scalar2=None, op0=ALU.max), reads=[m8], writes=[thr])
            for h in range(8):
                P.op("vector", lambda e, h=h: e.tensor_scalar(out=mbB[0:NP, h, :], in0=gsb[0:NP, h, :], scalar1=thr[0:NP, h:h + 1], scalar2=NEGM,
                                                          op0=ALU.is_lt, op1=ALU.mult), reads=[gsb, thr], writes=[mbB])
            if ubo is not None:
                P.op("vector", lambda e: e.memset(mbB[0:NP, :, ubo:ubo + 1], 0.0), writes=[mbB])
            for g in range(2):
                bk2 = banks[4 + g]
                for hh in range(4):
                    h = 4 * g + hh
                    P.op("tensor", lambda e, bk2=bk2, h=h, hh=hh: e.transpose(out=bk2[0:32, hh * 128:hh * 128 + NP], in_=mbB[0:NP, h, :], identity=ident[0:NP, 0:NP]),
                         reads=[mbB, ident], writes=[bk2])
                evac(mbBT[0:32, 4 * g:4 * g + 4, qoff:qoff + NP], bk2[0:32, :].rearrange("p (h q) -> p h q", h=4)[:, :, 0:NP], [bk2], [mbBT], eng="vector")

        class AttBufs:
            def __init__(self, st):
                self.kTs = [sb(st, "kTs%d" % i, [128, 2048], BF16) for i in range(2)]
                self.vss = [sb(st, "vss%d" % i, [128, 16, 130], BF16) for i in range(2)]
                self.PTs = [sb(st, "PT%d" % i, [128, 512], BF16) for i in range(4)]
                self.wts = [sb(st, "wts%d" % i, [128, 1024], BF16) for i in range(4)]
                self.rd = sb(st, "rd", [128, 512], F32)
                self.rb = sb(st, "rb", [128, 512], F32)
                self.kv_i = 0
                self.pt_i = 0
                self.wt_i = 0
                self.lb_i = 0

        def attend(A, kT_fn, v_fn, q_fn, o_fn, hoff, ntl, NQ, diag_fn, mask_fn):
            nkb = (ntl + 15) // 16
            SKEW = 2
            for p in range(4):
                pend = []
                wth = []
                for hh in range(2):
                    w_ = A.wts[A.wt_i % 4]
                    A.wt_i += 1
                    P.dma("gpsimd", lambda e, w_=w_, hd=hoff + 2 * p + hh: e.dma_start(out=w_[:], in_=wtab_d[:, hd, :]), writes=[w_])
                    wth.append(w_)
                for kb in range(nkb):
                    nt_ = min(16, ntl - 16 * kb)
                    kt = A.kTs[A.kv_i % 2]
                    vs = A.vss[A.kv_i % 2]
                    A.kv_i += 1
                    P.dma("sync", lambda e, kt=kt, p=p, kb=kb, nt_=nt_: e.dma_start(out=kt[:, 0:nt_ * 128], in_=kT_fn(p)[:, kb * 2048:kb * 2048 + nt_ * 128]), writes=[kt])
                    P.dma("sync", lambda e, vs=vs, p=p, kb=kb, nt_=nt_: e.dma_start(
                        out=vs[:, 0:nt_, :], in_=v_fn()[kb * 16:kb * 16 + nt_, :, p * 130:(p + 1) * 130].rearrange("t k f -> k t f")), writes=[vs])
                    for tl in range(nt_):
                        u = kb * 16 + tl
                        x0 = diag_fn(u)
                        diag = x0 is not None
                        items = []
                        for hh in range(2):
                            h = 2 * p + hh
                            r0 = 64 * hh
                            L = banks[A.lb_i % 4]
                            A.lb_i += 1
                            mk = mask_fn(u, h)
                            mulmask = None
                            if mk is not None and mk[0] is None:
                                mulmask = mk[1]
                                mk = None
                            last1 = (mk is None) and (not diag)
                            P.op("tensor", lambda e, L=L, kt=kt, r0=r0, tl=tl, p=p, last1=last1: e.matmul(
                                L[:, 0:NQ], lhsT=kt[r0:r0 + 64, tl * 128:(tl + 1) * 128], rhs=q_fn(r0, p), start=True, stop=last1),
                                reads=[kt, qTall], writes=[L])
                            items.append((hh, h, L, mk, mulmask))
                        for (hh, h, L, mk, mulmask) in items:
                            OT = banks[4 + hh]
                            if mk is not None:
                                lh, rh = mk
                                P.op("tensor", lambda e, L=L, lh=lh, rh=rh, diag=diag: e.matmul(L[:, 0:NQ], lhsT=lh, rhs=rh, start=False, stop=(not diag)),
                                     reads=[identb, ablk, maskall], writes=[L])
                            if diag:
                                P.op("tensor", lambda e, L=L, w_=wth[hh], x0=x0: e.matmul(L[:, 0:NQ], lhsT=identb[:], rhs=w_[:, x0:x0 + NQ], start=False, stop=True),
                                     reads=[identb, wth[hh]], writes=[L])
                            PT = A.PTs[A.pt_i % 4]
                            A.pt_i += 1
                            if diag:
                                P.op("scalar", lambda e, PT=PT, L=L: e.activation(out=PT[:, 0:NQ], in_=L[:, 0:NQ], func=AF.Exp), reads=[L], writes=[PT])
                            else:
                                P.op("scalar", lambda e, PT=PT, L=L, hd=hoff + h: e.activation(out=PT[:, 0:NQ], in_=L[:, 0:NQ], func=AF.Exp, bias=b31[:, hd:hd + 1]),
                                     reads=[L, b31], writes=[PT])
                            if mulmask is not None:
                                P.op("vector", lambda e, PT=PT, mm_=mulmask: e.tensor_tensor(out=PT[:, 0:NQ], in0=PT[:, 0:NQ], in1=mm_, op=ALU.mult),
                                     reads=[PT, maskall], writes=[PT])
                            pend.append((lambda e, OT=OT, vs=vs, tl=tl, hh=hh, PT=PT, u=u: e.matmul(
                                OT[0:65, 0:NQ], lhsT=vs[:, tl, hh * 65:(hh + 1) * 65], rhs=PT[:, 0:NQ], start=(u == 0), stop=(u == ntl - 1)),
                                [vs, PT], [OT]))
                        while len(pend) > SKEW:
                            f_, r_, w_2 = pend.pop(0)
                            P.op("tensor", f_, reads=r_, writes=w_2)
                while pend:
                    f_, r_, w_2 = pend.pop(0)
                    P.op("tensor", f_, reads=r_, writes=w_2)
                for hh in range(2):
                    h = 2 * p + hh
                    OT = banks[4 + hh]
                    P.op("vector", lambda e, OT=OT: e.reciprocal(out=A.rd[64:65, 0:NQ], in_=OT[64:65, 0:NQ]), reads=[OT], writes=[A.rd])
                    bx = banks[A.lb_i % 4]
                    A.lb_i += 1
                    P.op("tensor", lambda e, bx=bx: e.matmul(bx[0:64, 0:NQ], lhsT=ones_t[64:65, 0:64], rhs=A.rd[64:65, 0:NQ], start=True, stop=True),
                         reads=[ones_t, A.rd], writes=[bx])
                    P.op("scalar", lambda e, bx=bx: e.activation(out=A.rb[0:64, 0:NQ], in_=bx[0:64, 0:NQ], func=AF.Copy), reads=[bx], writes=[A.rb])
                    P.op("vector", lambda e, OT=OT, h=h: e.tensor_tensor(out=o_fn(h), in0=OT[0:64, 0:NQ], in1=A.rb[0:64, 0:NQ], op=ALU.mult),
                         reads=[OT, A.rb], writes=[oall])

        def phase3(NT, tiles, oaT, obT, sg_fn, x_fn, y_fn):
            with ExitStack() as s3:
                nti = len(tiles)
                hres = sb(s3, "hres", [128, nti, D], F32)
                hT = sb(s3, "hT", [128, 8, 512], BF16)
                sm3 = sb(s3, "sm3", [128, 8], F32)
                st6 = sb(s3, "st6", [128, 2, 6], F32)
                z = sb(s3, "z", [128, D], F32)
                with ExitStack() as s3a:
                    wau = sb(s3a, "wau", [64, 8, D], BF16)
                    wbu = sb(s3a, "wbu", [64, 8, D], BF16)
                    wo = sb(s3a, "wo", [128, 8, D], BF16)
                    P.dma("gpsimd", lambda e: e.dma_start(out=wau[:], in_=w_a_up.rearrange("(h d) n -> d h n", d=64)), writes=[wau])
                    P.dma("gpsimd", lambda e: e.dma_start(out=wbu[:], in_=w_b_up.rearrange("(h d) n -> d h n", d=64)), writes=[wbu])
                    P.dma("gpsimd", lambda e: e.dma_start(out=wo[:], in_=w_out.rearrange("(c p) n -> p c n", p=128)), writes=[wo])
                    mixT = sb(s3a, "mixT", [128, 8, 512], BF16)
                    sga = [sb(s3a, "sga%d" % i, [128, 512], F32) for i in range(2)]
                    sgb = [sb(s3a, "sgb%d" % i, [128, 512], F32) for i in range(2)]
                    ma = sb(s3a, "ma", [128, 512], F32)
                    xre = [sb(s3a, "xre%d" % i, [128, D], F32) for i in range(2)]
                    for fc in range(8):
                        sa_ = sga[fc % 2]
                        sb_ = sgb[fc % 2]
                        sg_fn(0, fc, sa_)
                        sg_fn(1, fc, sb_)
                        bA = banks[(2 * fc) % 8]
                        bB = banks[(2 * fc + 1) % 8]
                        for h in range(8):
                            P.op("tensor", lambda e, bA=bA, h=h, fc=fc: e.matmul(bA[:, 0:NT], lhsT=wau[0:64, h, fc * 128:(fc + 1) * 128], rhs=oaT[0:64, h, 0:NT],
                                                                             start=(h == 0), stop=(h == 7)), reads=[wau, oaT], writes=[bA])
                        for h in range(8):
                            P.op("tensor", lambda e, bB=bB, h=h, fc=fc: e.matmul(bB[:, 0:NT], lhsT=wbu[0:64, h, fc * 128:(fc + 1) * 128], rhs=obT[0:64, h, 0:NT],
                                                                             start=(h == 0), stop=(h == 7)), reads=[wbu, obT], writes=[bB])
                        P.op("vector", lambda e, bA=bA, sa_=sa_: e.tensor_tensor(out=ma[:, 0:NT], in0=bA[:, 0:NT], in1=sa_[:, 0:NT], op=ALU.mult), reads=[bA, sa_], writes=[ma])
                        P.op("vector", lambda e, bB=bB, sb_=sb_: e.tensor_tensor(out=sb_[:, 0:NT], in0=bB[:, 0:NT], in1=sb_[:, 0:NT], op=ALU.mult), reads=[bB, sb_], writes=[sb_])
                        P.op("vector", lambda e, sb_=sb_, fc=fc: e.tensor_tensor(out=mixT[:, fc, 0:NT], in0=ma[:, 0:NT], in1=sb_[:, 0:NT], op=ALU.add), reads=[ma, sb_], writes=[mixT])
                    for t, (t0, nt) in enumerate(tiles):
                        xr = xre[t % 2]
                        x_fn(t, xr)
                        for hf in range(2):
                            bk = banks[(2 * t + hf) % 8]
                            for c in range(8):
                                P.op("tensor", lambda e, bk=bk, c=c, t0=t0, nt=nt, hf=hf: e.matmul(bk[0:nt, :], lhsT=mixT[:, c, t0:t0 + nt], rhs=wo[:, c, hf * 512:(hf + 1) * 512],
                                                                                         start=(c == 0), stop=(c == 7)), reads=[mixT, wo], writes=[bk])
                            P.op("vector", lambda e, bk=bk, xr=xr, hf=hf, nt=nt: e.scalar_tensor_tensor(out=z[0:nt, hf * 512:(hf + 1) * 512], in0=xr[0:nt, hf * 512:(hf + 1) * 512],
                                                                                                scalar=float(ALPHA), in1=bk[0:nt, :], op0=ALU.mult, op1=ALU.add),
                                 reads=[bk, xr], writes=[z])
                        fin = layer_norm(z, nt, 0, hres[0:nt, t, :], sm3, st6)
                        P.op("vector", fin, reads=[z, lnr], writes=[hres])
                        for g in range(2):
                            bk = banks[(g + 2 * t) % 8]
                            for cc in range(4):
                                c = 4 * g + cc
                                P.op("tensor", lambda e, bk=bk, cc=cc, c=c, t=t, nt=nt: e.transpose(out=bk[:, cc * 128:cc * 128 + nt], in_=hres[0:nt, t, c * 128:(c + 1) * 128],
                                                                                          identity=ident[0:nt, 0:nt]), reads=[hres, ident], writes=[bk])
                            evac(hT[:, 4 * g:4 * g + 4, t0:t0 + nt], bk[:, :].rearrange("p (c q) -> p c q", c=4)[:, :, 0:nt], [bk], [hT])
                    P.barrier()
                chk(9)
                with ExitStack() as s3b:
                    uT = sb(s3b, "uT", [128, 32, 512], BF16)
                    wf1 = [sb(s3b, "wf1_%d" % i, [128, 8, 512], BF16) for i in range(2)]
                    wf2 = [sb(s3b, "wf2_%d" % i, [128, 8, D], BF16) for i in range(2)]
                    rl = [sb(s3b, "rl%d" % i, [128, 512], F32) for i in range(2)]
                    yst = [sb(s3b, "yst%d" % i, [128, D], F32) for i in range(2)]
                    rl_i = 0
                    for fb in range(8):
                        w1 = wf1[fb % 2]
                        P.dma("gpsimd", lambda e, w1=w1, fb=fb: e.dma_start(out=w1[:], in_=w_ff1[:, fb * 512:(fb + 1) * 512].rearrange("(c p) n -> p c n", p=128)), writes=[w1])
                        for fc in range(4):
                            bk = banks[(fb * 4 + fc) % 8]
                            for c in range(8):
                                P.op("tensor", lambda e, bk=bk, w1=w1, c=c, fc=fc: e.matmul(bk[:, 0:NT], lhsT=w1[:, c, fc * 128:(fc + 1) * 128], rhs=hT[:, c, 0:NT],
                                                                                    start=(c == 0), stop=(c == 7)), reads=[w1, hT], writes=[bk])
                            r_ = rl[rl_i % 2]
                            rl_i += 1
                            P.op("scalar", lambda e, bk=bk, r_=r_: e.activation(out=r_[:, 0:NT], in_=bk[:, 0:NT], func=AF.Relu), reads=[bk], writes=[r_])
                            P.op("vector", lambda e, r_=r_, fb=fb, fc=fc: e.tensor_tensor(out=uT[:, fb * 4 + fc, 0:NT], in0=r_[:, 0:NT], in1=r_[:, 0:NT], op=ALU.mult),
                                 reads=[r_], writes=[uT])
                    for blk in range(4):
                        w2 = wf2[blk % 2]
                        P.dma("gpsimd", lambda e, w2=w2, blk=blk: e.dma_start(out=w2[:], in_=w_ff2[blk * 1024:(blk + 1) * 1024, :].rearrange("(c p) n -> p c n", p=128)), writes=[w2])
                        for cc in range(8):
                            ch = blk * 8 + cc
                            for t, (t0, nt) in enumerate(tiles):
                                for hf in range(2):
                                    bk = banks[2 * t + hf]
                                    P.op("tensor", lambda e, bk=bk, w2=w2, cc=cc, ch=ch, t0=t0, nt=nt, hf=hf: e.matmul(
                                        bk[0:nt, :], lhsT=uT[:, ch, t0:t0 + nt], rhs=w2[:, cc, hf * 512:(hf + 1) * 512], start=(ch == 0), stop=(ch == 31)),
                                        reads=[uT, w2], writes=[bk])
                    for t, (t0, nt) in enumerate(tiles):
                        for hf in range(2):
                            bk = banks[2 * t + hf]
                            P.op("vector", lambda e, bk=bk, t=t, hf=hf, nt=nt: e.scalar_tensor_tensor(out=z[0:nt, hf * 512:(hf + 1) * 512], in0=hres[0:nt, t, hf * 512:(hf + 1) * 512],
                                                                                              scalar=float(ALPHA), in1=bk[0:nt, :], op0=ALU.mult, op1=ALU.add),
                                 reads=[bk, hres], writes=[z])
                        ys = yst[t % 2]
                        fin = layer_norm(z, nt, 2, ys[0:nt, :], sm3, st6)
                        P.op("vector", fin, reads=[z, lnr], writes=[ys])
                        out_toks.append(y_fn(t, ys))
                    P.barrier()


        def qgroup(mq):
            nch = 4 * mq + 4
            ntl = 16 * mq + 16
            with ExitStack() as sq:
                oaT = sb(sq, "oaT", [128, 8, 512], BF16)
                obT = sb(sq, "obT", [128, 8, 512], BF16)
                s2 = ExitStack()
                sq.callback(s2.close)
                mbT = sb(s2, "mbT", [128, 64, 512], BF16)
                qaT_g = sb(s2, "qaT_g", [128, 4, 512], BF16)
                qbT_g = sb(s2, "qbT_g", [128, 4, 512], BF16)
                qiT_g = sb(s2, "qiT_g", [128, 2, 512], BF16)
                P.dma("sync", lambda e: e.dma_start(out=qaT_g[:], in_=qaT_d[mq]), writes=[qaT_g, qTall])
                P.dma("sync", lambda e: e.dma_start(out=qbT_g[:], in_=qbT_d[mq]), writes=[qbT_g, qTall])
                P.dma("sync", lambda e: e.dma_start(out=qiT_g[:], in_=qiT_d[mq]), writes=[qiT_g])
                chk(1)
                with ExitStack() as sa:
                    btab = sb(sa, "btab", [128, 9, 512], F32)
                    P.dma("sync", lambda e: e.dma_start(out=btab[:], in_=btab_d), writes=[btab])
                    kib = [sb(sa, "kib%d" % i, [128, 2048], BF16) for i in range(2)]
                    for i in range(4):
                        qt = 4 * mq + i

                        def emit_scores(c, i=i):
                            kb_ = kib[(c // 4) % 2]
                            if c % 4 == 0:
                                for hf in range(2):
                                    P.dma("sync", lambda e, kb_=kb_, c=c, hf=hf: e.dma_start(
                                        out=kb_[64 * hf:64 * hf + 64, :], in_=kiT_d[:, (c // 4) * 2048:(c // 4 + 1) * 2048]), writes=[kb_])
                            for h in range(4):
                                r0 = 64 * (h % 2)
                                P.op("tensor", lambda e, h=h, r0=r0, c=c, kb_=kb_: e.matmul(
                                    banks[h][:, :], lhsT=qiT_g[r0:r0 + 64, h // 2, i * 128:(i + 1) * 128],
                                    rhs=kb_[r0:r0 + 64, (c % 4) * 512:(c % 4 + 1) * 512], start=True, stop=True),
                                    reads=[qiT_g, kb_], writes=[banks[h]])
                        with ExitStack() as si:
                            indexer(si, 128, nch, emit_scores, lambda k, qt=qt: lohi[:, qt, k:k + 1],
                                    lambda c, i=i: (c if c < 3 else (4 + i if c == nch - 1 else 3)), mbT, i * 128, btab)
                        P.op("vector", lambda e: e.memset(ones_t[0:1, 0:1], 1.0), reads=[mbT], writes=[maskall, ones_t])
                    chk(4)
                    P.barrier()
                chk(5)
                with ExitStack() as sbb:
                    A = AttBufs(sbb)
                    mbBT = sb(sbb, "mbBT", [32, 8, 512], BF16)
                    pastb = sb(sbb, "pastb", [128, 2, 32], F32)
                    P.dma("sync", lambda e: e.dma_start(out=pastb[:], in_=pastb_d[mq]), writes=[pastb])
                    for a in range(2):
                        P.op("vector", lambda e, a=a: e.tensor_tensor(out=pastb[:, a, :], in0=pastb[:, a, :], in1=gb2[:], op=ALU.add),
                             reads=[pastb, gb2], writes=[pastb, pastb_t])
                    for i in range(4):
                        def emit_gate(bk, i=i):
                            for h in range(8):
                                r0 = 64 * (h % 2)
                                P.op("tensor", lambda e, h=h, r0=r0: e.matmul(bk[:, h * 32:(h + 1) * 32], lhsT=qbT_g[r0:r0 + 64, h // 2, i * 128:(i + 1) * 128],
                                                                          rhs=meansTb[r0:r0 + 64, h // 2, :], start=True, stop=True),
                                     reads=[qbT_g, meansTb], writes=[bk])
                        with ExitStack() as sg_:
                            moba_gate(sg_, 128, emit_gate, pastb[:, i // 2, :], 8 * mq + 6 + i // 2, mbBT, i * 128)
                    P.op("vector", lambda e: e.memset(ones_t[0:1, 0:1], 1.0), reads=[mbBT], writes=[maskall, ones_t])
                    chk(6)

                    def diag_fn(u):
                        return (512 - 128 * (u - (ntl - 5))) if u >= ntl - 5 else None
                    if DBG_LEVEL >= 7:
                        attend(A, lambda p: kaT_d[p], lambda: va_d, lambda r0, p: qaT_g[r0:r0 + 64, p, :], lambda h: oaT[0:64, h, :], 0, ntl, 512, diag_fn,
                               lambda u, h: (None, mbT[:, u, :]))
                    chk(7)
                    if DBG_LEVEL >= 8:
                        attend(A, lambda p: kbT_d[p], lambda: vb_d, lambda r0, p: qbT_g[r0:r0 + 64, p, :], lambda h: obT[0:64, h, :], 8, ntl, 512, diag_fn,
                               lambda u, h: (ablk[0:32, u // 2, :], mbBT[0:32, h, :]))
                    chk(8)
                    P.op("vector", lambda e: e.memset(ones_t[0:1, 0:1], 1.0), reads=[oall], writes=[oaT, obT, ones_t])
                    P.barrier()
                s2.close()

                def sg_fn(which, fc, dst):
                    src = sga_d if which == 0 else sgb_d
                    P.dma("sync", lambda e: e.dma_start(out=dst[:], in_=src[mq, fc]), writes=[dst])

                def x_fn(t, xr):
                    r0 = (16 * mq + 12 + t) * 128
                    P.dma("sync", lambda e: e.dma_start(out=xr[:], in_=xs[r0:r0 + 128, :]), writes=[xr])

                def y_fn(t, ys):
                    r0 = (4 * mq + t) * 128
                    return P.dma("sync", lambda e: e.dma_start(out=y_p[r0:r0 + 128, :], in_=ys[:]), reads=[ys])
                phase3(512, [(0, 128), (128, 128), (256, 128), (384, 128)], oaT, obT, sg_fn, x_fn, y_fn)

        for mq_ in range(DBG_NQG):
            try:
                qgroup(mq_)
            except _Stop:
                pass
        def sample_group():
            with ExitStack() as ss1:
                ptr = sb(ss1, "ptr", [128, 256], I32)
                iot = sb(ss1, "iot", [128, 1], I32)
                idx = sb(ss1, "idx", [128, 256], I32)
                P.dma("sync", lambda e: e.dma_start(out=ptr[:], in_=ptrep_d), writes=[ptr])
                P.dma("sync", lambda e: e.dma_start(out=iot[:], in_=iot_d), writes=[iot])
                P.op("vector", lambda e: e.tensor_scalar(out=idx[:], in0=ptr[:], scalar1=128.0, scalar2=iot[:, 0:1], op0=ALU.mult, op1=ALU.add),
                     reads=[ptr, iot], writes=[idx])
                gd = [sb(ss1, "gd%d" % i, [128, 1088], F32) for i in range(8)]
                gm = [sb(ss1, "gm%d" % i, [128, 1024], F32) for i in range(8)]
                kst = [sb(ss1, "kst%d" % i, [128, 512], BF16) for i in range(4)]
                vst = [sb(ss1, "vst%d" % i, [128, 8, 65], BF16) for i in range(4)]
                msum = sb(ss1, "msum", [128, 4, 32], F32)
                for v in vst:
                    P.op("vector", lambda e, v=v: e.memset(v[:], 1.0), writes=[v])
                ki_ = 0
                vi_ = 0
                for s in range(DBG_NSEQ):
                    for blk in range(16):
                        gds = []
                        gms = []
                        for t in range(4):
                            pg = blk * 4 + t
                            col = s * 64 + pg
                            g1 = gd[(blk % 2) * 4 + t]
                            g2 = gm[(blk % 2) * 4 + t]
                            P.dma("gpsimd", lambda e, g1=g1, col=col: e.indirect_dma_start(out=g1[:, :], out_offset=None, in_=cdsa[:, :],
                                                                                         in_offset=bass.IndirectOffsetOnAxis(ap=idx[:, col:col + 1], axis=0)),
                                  reads=[idx], writes=[g1])
                            P.dma("gpsimd", lambda e, g2=g2, col=col: e.indirect_dma_start(out=g2[:, :], out_offset=None, in_=cmoba[:, :],
                                                                                         in_offset=bass.IndirectOffsetOnAxis(ap=idx[:, col:col + 1], axis=0)),
                                  reads=[idx], writes=[g2])
                            gds.append(g1)
                            gms.append(g2)
                        for (srcs, c0, M, dst, cidx) in ([(gds, 128 * c, 128, skaT_d[s, c], None) for c in range(4)] + [(gds, 1024, 64, skiT_d[s], None)]
                                                        + [(gms, 128 * c, 128, skbT_d[s, c], c) for c in range(4)]):
                            bk = next_bank()
                            for t in range(4):
                                P.op("tensor", lambda e, bk=bk, g_=srcs[t], c0=c0, M=M, t=t: e.transpose(out=bk[0:M, t * 128:(t + 1) * 128], in_=g_[:, c0:c0 + M], identity=ident[:]),
                                     reads=[srcs[t], ident], writes=[bk])
                            if cidx is not None:
                                for hb in range(2):
                                    P.op("vector", lambda e, bk=bk, cidx=cidx, blk=blk, hb=hb: e.tensor_reduce(
                                        out=msum[:, cidx, 2 * blk + hb:2 * blk + hb + 1], in_=bk[:, hb * 256:(hb + 1) * 256], axis=AX.X, op=ALU.add), reads=[bk], writes=[msum])
                            st = kst[ki_ % 4]
                            ki_ += 1
                            evac(st[0:M, :], bk[0:M, :], [bk], [st])
                            P.dma("sync", lambda e, st=st, dst=dst, M=M, blk=blk: e.dma_start(out=dst[0:M, blk * 512:(blk + 1) * 512], in_=st[0:M, :]), reads=[st], writes=[sscr])
                        for t in range(4):
                            pg = blk * 4 + t
                            for (g_, dstd) in ((gds[t], sva_d), (gms[t], svb_d)):
                                vs_ = vst[vi_ % 4]
                                vi_ += 1
                                ce_ = "vector" if (vi_ % 2 == 0) else "scalar"
                                if ce_ == "vector":
                                    P.op("vector", lambda e, vs_=vs_, g_=g_: e.tensor_copy(out=vs_[:, :, 0:64], in_=g_[:, 512:1024].rearrange("p (h d) -> p h d", h=8)),
                                         reads=[g_], writes=[vs_])
                                else:
                                    P.op("scalar", lambda e, vs_=vs_, g_=g_: e.activation(out=vs_[:, :, 0:64], in_=g_[:, 512:1024].rearrange("p (h d) -> p h d", h=8), func=AF.Copy),
                                         reads=[g_], writes=[vs_])
                                P.dma("sync", lambda e, vs_=vs_, dstd=dstd, s=s, pg=pg: e.dma_start(out=dstd[s, pg], in_=vs_[:].rearrange("p h d -> p (h d)")), reads=[vs_], writes=[sscr])
                    P.op("vector", lambda e, s=s: e.tensor_scalar(out=smeansTb[s][:], in0=msum[:], scalar1=1.0 / 256.0, scalar2=None, op0=ALU.mult),
                         reads=[msum], writes=[smeansTb[s]])
                P.barrier()
            chk(11)
            with ExitStack() as sq:
                oaTs = sb(sq, "oaTs", [128, 8, 512], BF16)
                obTs = sb(sq, "obTs", [128, 8, 512], BF16)
                s2 = ExitStack()
                sq.callback(s2.close)
                mbTs = sb(s2, "mbTs", [128, 68, NSAMP], BF16)
                with ExitStack() as sa:
                    btab = sb(sa, "btab", [128, 9, 512], F32)
                    P.dma("sync", lambda e: e.dma_start(out=btab[:], in_=btab_d), writes=[btab])
                    kibs = [sb(sa, "kibs%d" % i, [128, 2048], BF16) for i in range(4)]

                    def emit_scores(c):
                        if c % 4 == 0:
                            wd_ = min(2048, 8704 - (c // 4) * 2048)
                            for s in range(4):
                                for hf in range(2):
                                    P.dma("sync", lambda e, s=s, c=c, hf=hf, wd_=wd_: e.dma_start(
                                        out=kibs[s][64 * hf:64 * hf + 64, 0:wd_], in_=skiT_d[s][:, (c // 4) * 2048:(c // 4) * 2048 + wd_]), reads=[sscr], writes=[kibs[s]])
                        for h in range(4):
                            r0 = 64 * (h % 2)
                            for s in range(4):
                                P.op("tensor", lambda e, h=h, r0=r0, c=c, s=s: e.matmul(
                                    banks[h][0:NSAMP, :], lhsT=qiTm[s][r0:r0 + 64, h // 2, :], rhs=kibs[s][r0:r0 + 64, (c % 4) * 512:(c % 4 + 1) * 512],
                                    start=(s == 0), stop=(s == 3)), reads=[qiTm[s], kibs[s]], writes=[banks[h]])
                    with ExitStack() as si:
                        indexer(si, NSAMP, 17, emit_scores, lambda k: lohis[0:NSAMP, k:k + 1], lambda c: (8 if c == 16 else 3), mbTs, 0, btab)
                    P.op("vector", lambda e: e.memset(ones_t[0:1, 0:1], 1.0), reads=[mbTs], writes=[maskall, ones_t])
                    P.barrier()
                chk(12)
                with ExitStack() as sbb:
                    A = AttBufs(sbb)
                    mbBTs = sb(sbb, "mbBTs", [32, 8, NSAMP], BF16)
                    zb = sb(sbb, "zb", [128, 32], F32)
                    P.op("vector", lambda e: e.memset(zb[:], 0.0), writes=[zb, pastb_t])

                    def emit_gate(bk):
                        for h in range(8):
                            r0 = 64 * (h % 2)
                            for s in range(4):
                                P.op("tensor", lambda e, h=h, r0=r0, s=s: e.matmul(bk[0:NSAMP, h * 32:(h + 1) * 32], lhsT=qbTm[s][r0:r0 + 64, h // 2, :],
                                                                               rhs=smeansTb[s][r0:r0 + 64, h // 2, :], start=(s == 0), stop=(s == 3)),
                                     reads=[qbTm[s], smeansTb[s]], writes=[bk])
                    with ExitStack() as sg_:
                        moba_gate(sg_, NSAMP, emit_gate, zb[0:NSAMP, :], None, mbBTs, 0)
                    P.op("vector", lambda e: e.memset(ones_t[0:1, 0:1], 1.0), reads=[mbBTs, sscr], writes=[maskall, ones_t])
                    chk(13)

                    def diag_fn(u):
                        return 512 if u == 63 else (384 if u == 64 else None)
                    for s in range(DBG_NSEQ):
                        q0 = 8 * s
                        attend(A, lambda p, s=s: skaT_d[s, p], lambda s=s: sva_d[s], lambda r0, p, q0=q0: qaTs[r0:r0 + 64, p, q0:q0 + 8],
                               lambda h, q0=q0: oaTs[0:64, h, q0:q0 + 8], 0, 65, 8, diag_fn, lambda u, h, q0=q0: (None, mbTs[:, u, q0:q0 + 8]))
                        attend(A, lambda p, s=s: skbT_d[s, p], lambda s=s: svb_d[s], lambda r0, p, q0=q0: qbTs[r0:r0 + 64, p, q0:q0 + 8],
                               lambda h, q0=q0: obTs[0:64, h, q0:q0 + 8], 8, 65, 8, diag_fn,
                               lambda u, h, q0=q0: ((ablk[0:32, u // 2, :], mbBTs[0:32, h, q0:q0 + 8]) if u < 64 else None))
                    P.op("vector", lambda e: e.memset(ones_t[0:1, 0:1], 1.0), reads=[oall], writes=[oaTs, obTs, ones_t])
                    P.barrier()
                s2.close()
                chk(14)

                def sg_fn(which, fc, dst):
                    src = sgas if which == 0 else sgbs
                    P.op("vector", lambda e: e.tensor_copy(out=dst[:, 0:NSAMP], in_=src[:, fc, :]), reads=[src], writes=[dst])

                def x_fn(t, xr):
                    P.dma("sync", lambda e: e.dma_start(out=xr[0:NSAMP, :], in_=xsm[:, :]), writes=[xr])

                def y_fn(t, ys):
                    return P.dma("sync", lambda e: e.dma_start(out=y_s[:, :], in_=ys[0:NSAMP, :]), reads=[ys])
                phase3(NSAMP, [(0, NSAMP)], oaTs, obTs, sg_fn, x_fn, y_fn)

        if DBG_SAMPLE:
            sample_group()
        P.barrier()
        for e in ["sync"]:
            waits = P._waits(e, dict([t for t in out_toks if t is not None]))
            if waits:
                P.ops[e].append((waits, None, None, 0))
        P.emit()
    return nc


def host_consts(rel_bias):
    ki = np.arange(128)[:, None]
    x = np.arange(1024)[None, :]
    d = x - ki - 384
    bkt = t5_bucket_np(d)
    wt = np.empty((128, 16, 1024), np.float32)
    for h in range(16):
        wt[:, h, :] = np.where(d >= 0, rel_bias[bkt, h], np.float32(NEGM))
    b31 = np.broadcast_to(rel_bias[31][None, :], (128, 16)).astype(np.float32).copy()
    qi = np.arange(128)[:, None]
    kk = np.arange(512)[None, :]
    cm = np.empty((128, 4, 512), np.float32)
    for i in range(4):
        cm[:, i, :] = np.where(kk <= i * 128 + qi, 0.0, -BIG)
    pertb = np.broadcast_to((-EPS_TIE * np.arange(512, dtype=np.float64)).astype(np.float32)[None, :], (128, 512)).copy()
    ablk = np.zeros((32, 32, 128), np.float32)
    for u in range(32):
        ablk[u, u, :] = 1.0
    pastb = np.zeros((4, 128, 2, 32), np.float32)
    for mq in range(4):
        for a in range(2):
            pastb[mq, :, a, 8 * mq + 6 + a:] = -BIG
    return wt, b31, cm, pertb, ablk, pastb


def core_consts(j, cm, pertb):
    nph = (12 - 4 * j) * 128
    phb = np.zeros((128, 1536), np.float32)
    phb[:, :nph] = -BIG
    btab = np.empty((128, 9, 512), np.float32)
    kk = np.arange(512)[None, :]
    btab[:, 8, :] = pertb + np.where(kk <= (np.arange(128)[:, None] % 8), 0.0, -BIG).astype(np.float32)
    for c in range(3):
        btab[:, c, :] = pertb + phb[:, c * 512:(c + 1) * 512]
    btab[:, 3, :] = pertb
    for i in range(4):
        btab[:, 4 + i, :] = pertb + cm[:, i, :]
    bval = np.zeros((128, 32), np.float32)
    bval[:, :nph // 256] = -BIG
    return btab, bval, nph


def kernel(x_prompt, x_sample, cache_dsa, cache_moba, page_table, w_in, rel_bias, w_a_up, w_b_up,
           w_out, ln1_g, ln1_b, w_ff1, w_ff2, ln2_g, ln2_b):
    x_prompt = np.asarray(x_prompt, np.float32)
    x_sample = np.asarray(x_sample, np.float32)
    rel_bias = np.asarray(rel_bias, np.float32)
    wt, b31, cm, pertb, ablk, pastb = host_consts(rel_bias)
    lnrep = np.stack([np.broadcast_to(np.asarray(a, np.float32)[0][None, :], (128, D)) for a in (ln1_g, ln1_b, ln2_g, ln2_b)], axis=1).copy()
    ident = np.eye(128, dtype=np.float32)
    common = dict(w_in=np.ascontiguousarray(np.asarray(w_in, np.float32)[0]),
                  w_a_up=np.ascontiguousarray(np.asarray(w_a_up, np.float32)[0]),
                  w_b_up=np.ascontiguousarray(np.asarray(w_b_up, np.float32)[0]),
                  w_out=np.ascontiguousarray(np.asarray(w_out, np.float32)[0]),
                  w_ff1=np.ascontiguousarray(np.asarray(w_ff1, np.float32)[0]),
                  w_ff2=np.ascontiguousarray(np.asarray(w_ff2, np.float32)[0]),
                  lnrep=lnrep, ident=ident, wtab=wt, b31=b31, ablk=ablk, pastb=pastb)
    page_table = np.asarray(page_table, np.int32)
    iot = np.arange(128, dtype=np.int32)[:, None].copy()
    cdsa = np.asarray(cache_dsa, np.float32)[0].reshape(2560 * 128, 1088)
    cmoba = np.asarray(cache_moba, np.float32)[0].reshape(2560 * 128, 1024)
    in_maps = []
    for c in range(8):
        b, j = c // 4, c % 4
        btab, bval, nph = core_consts(j, cm, pertb)
        xsl = np.zeros((T, D), np.float32)
        xsl[nph:] = x_prompt[b, :T - nph]
        m = dict(common)
        ptrep = np.ascontiguousarray(np.broadcast_to(page_table[4 * c:4 * c + 4].reshape(1, 256), (128, 256)))
        m.update(xs=xsl, xsm=np.ascontiguousarray(x_sample[4 * c:4 * c + 4].reshape(NSAMP, D)), btab=btab, bval=bval, ptrep=ptrep, iot=iot, cdsa=cdsa, cmoba=cmoba)
        in_maps.append(m)
    nc = build_program()
    res = run_bass_kernel_spmd(nc, in_maps, core_ids=list(range(8)))
    y_p = np.zeros((2, T, D), np.float32)
    dsa_pp = np.zeros((1, 2, T, 1088), np.float32)
    moba_pp = np.zeros((1, 2, T, 1024), np.float32)
    y_s = np.zeros((32, 8, D), np.float32)
    dsa_ss = np.zeros((1, 32, 8, 1088), np.float32)
    moba_ss = np.zeros((1, 32, 8, 1024), np.float32)
    for c in range(8):
        b, j = c // 4, c % 4
        r = res.results[c]
        for m in range(4):
            g0 = (4 * m + j) * 512
            y_p[b, g0:g0 + 512] = r["y_p"][m * 512:(m + 1) * 512]
            dsa_pp[0, b, g0:g0 + 512] = r["dsa_p"][m * 512:(m + 1) * 512]
            moba_pp[0, b, g0:g0 + 512] = r["moba_p"][m * 512:(m + 1) * 512]
        y_s[4 * c:4 * c + 4] = r["y_s"].reshape(4, 8, D)
        dsa_ss[0, 4 * c:4 * c + 4] = r["dsa_s"].reshape(4, 8, 1088)
        moba_ss[0, 4 * c:4 * c + 4] = r["moba_s"].reshape(4, 8, 1024)
    return (y_p, y_s, dsa_pp, moba_pp, dsa_ss, moba_ss)
```

```python
import math
import numpy as np
from contextlib import ExitStack
import concourse.bass as bass
import concourse.mybir as mybir
from concourse.bass_utils import run_bass_kernel_spmd

F32 = mybir.dt.float32
BF16 = mybir.dt.bfloat16
I32 = mybir.dt.int32
AF = mybir.ActivationFunctionType
ALU = mybir.AluOpType
AX = mybir.AxisListType

D = 1024
T = 8192
NT = 64
QA, KA, VA, QI, WI, KI, QB, KB, VB, GA, GB, DIN = 0, 512, 1024, 1536, 1792, 1796, 1860, 2372, 2884, 3396, 4420, 5444
ALPHA = 2.0 ** 0.25
LN_EPS = 1e-5
NEGM = -30000.0
BIG = 1e30
NIT = 16
NIT2 = 14
ACT_SPLIT = 0.45
EPS_TIE = 1e-12
DFF = 4096
NSAMP = 32
DBG_NBLK = 16
DBG_SAMPLE = True
DBG_LEVEL = 9
DBG_NQG = 4
DBG_Q = 99
DBG_NSEQ = 4


class _Stop(Exception):
    pass


MUTE = [False]


def chk(k):
    if DBG_Q < k:
        MUTE[0] = True

ENGS = ["sync", "scalar", "gpsimd", "vector", "tensor"]
EPOCH = 4096


class Buf:
    __slots__ = ("w", "r", "x")

    def __init__(self):
        self.w = None
        self.r = []
        self.x = False


class TT:
    def __init__(self, t):
        self.t = t
        self.b = Buf()

    def __getitem__(self, k):
        return self.t[k]


class Prog:
    NDMA = 16

    def __init__(self, nc, es):
        self.nc = nc
        self.es = es
        self.ops = {e: [] for e in ENGS}
        self.cnt = {}
        self.sems = {}
        self.waited = {e: {} for e in ENGS}
        self.dma_n = {e: 0 for e in ENGS}
        self.ncomp = {e: 0 for e in ENGS}
        self.last = {}

    def _sem(self, key):
        if key not in self.sems:
            self.sems[key] = self.es.enter_context(self.nc.semaphore(key))
            self.cnt[key] = 0
        return key

    def _need(self, reads, writes):
        need = {}

        def add(t):
            if t is None:
                return
            k, v = t
            if need.get(k, 0) < v:
                need[k] = v
        for b in reads:
            add(b.w)
        for b in writes:
            add(b.w)
            for r in b.r:
                add(r)
        return need

    def _waits(self, eng, need):
        waits = []
        wd = self.waited[eng]
        for k, v in need.items():
            if wd.get(k, 0) < v:
                wd[k] = v
                waits.append((k, v))
        return waits

    def _mark(self, tok, reads, writes):
        for b in reads:
            b.r.append(tok)
            if len(b.r) > 64:
                mx = {}
                for k, v in b.r:
                    if mx.get(k, 0) < v:
                        mx[k] = v
                b.r = list(mx.items())
        for b in writes:
            b.w = tok
            b.r = []
        self.last[tok[0]] = tok[1]

    @staticmethod
    def _bufs(xs):
        return [x.b if isinstance(x, TT) else x for x in xs]

    def op(self, eng, fn, reads=(), writes=()):
        if MUTE[0]:
            return None
        reads = self._bufs(reads)
        writes = self._bufs(writes)
        xr = [b for b in reads if b.x]
        if xr:
            writes = list(writes) + [b for b in xr if b not in writes]
            reads = [b for b in reads if not b.x]
        waits = self._waits(eng, self._need(reads, writes))
        key = "c_" + eng
        self.cnt[key] = self.cnt.get(key, 0) + 1
        tok = (key, self.cnt[key])
        self.ops[eng].append((waits, fn, key, 1))
        self._mark(tok, reads, writes)
        return tok

    def dma(self, eng, fn, reads=(), writes=()):
        if MUTE[0]:
            return None
        reads = self._bufs(reads)
        writes = self._bufs(writes)
        n = self.dma_n[eng]
        self.dma_n[eng] += 1
        key = self._sem("d_%s_%d" % (eng, n % self.NDMA))
        need = self._need(reads, writes)
        prev = self.cnt[key]
        if prev > 0 and need.get(key, 0) < prev:
            need[key] = prev
        waits = self._waits(eng, need)
        self.cnt[key] += 16
        tok = (key, self.cnt[key])
        self.ops[eng].append((waits, fn, key, 16))
        self._mark(tok, reads, writes)
        return tok

    def barrier(self):
        toks = [(k, v) for k, v in self.cnt.items() if v > 0]
        for e in ENGS:
            waits = self._waits(e, dict(toks))
            if waits:
                self.ops[e].append((waits, None, None, 0))

    def emit(self):
        nc = self.nc
        ref = {}
        for e in ENGS:
            for waits, fn, key, inc in self.ops[e]:
                for k, v in waits:
                    if k.startswith("c_"):
                        ref.setdefault(k, set()).add(v)
        rank = {}
        for k, vs in ref.items():
            for i, v in enumerate(sorted(vs)):
                rank[(k, v)] = i
                self._sem("%s_%d" % (k, i // EPOCH))

        def csem(k, v):
            r = rank[(k, v)]
            return self.sems["%s_%d" % (k, r // EPOCH)], r % EPOCH + 1

        with nc.Block() as block:
            for ename in ENGS:
                ops = self.ops[ename]
                if not ops:
                    continue

                def body(eng, ops=ops, ename=ename):
                    n = 0
                    for waits, fn, key, inc in ops:
                        for k, v in waits:
                            if k.startswith("c_"):
                                sm_, val = csem(k, v)
                                eng.wait_ge(sm_, val)
                            else:
                                eng.wait_ge(self.sems[k], v)
                        if fn is not None:
                            if key.startswith("c_"):
                                n += 1
                                ins = fn(eng)
                                if (key, n) in rank:
                                    sm_, val = csem(key, n)
                                    ins.then_inc(sm_, 1)
                            else:
                                fn(eng).then_inc(self.sems[key], inc)
                getattr(block, ename)(body)


def t5_bucket_np(n):
    n = np.maximum(n, 0)
    nf = np.maximum(n, 1).astype(np.float32)
    large = 16 + (np.log(nf / np.float32(16)) / np.float32(math.log(128 / 16)) * np.float32(16)).astype(np.int32)
    large = np.minimum(large, 31)
    return np.where(n < 16, n, large)


def build_program():
    MUTE[0] = False
    nc = bass.Bass("TRN2", target_bir_lowering=False)

    def din(name, shape, dt=F32):
        return nc.dram_tensor(name, list(shape), dt, kind="ExternalInput").ap()

    def dout(name, shape, dt=F32):
        return nc.dram_tensor(name, list(shape), dt, kind="ExternalOutput").ap()

    def dscr(name, shape, dt):
        return nc.dram_tensor(name, list(shape), dt, kind="Internal").ap()

    xs = din("xs", [T, D])
    xsm = din("xsm", [NSAMP, D])
    w_in = din("w_in", [D, DIN])
    w_a_up = din("w_a_up", [512, D])
    w_b_up = din("w_b_up", [512, D])
    w_out = din("w_out", [D, D])
    w_ff1 = din("w_ff1", [D, DFF])
    w_ff2 = din("w_ff2", [DFF, D])
    lnrep = din("lnrep", [128, 4, D])
    ident_d = din("ident", [128, 128])
    wtab_d = din("wtab", [128, 16, 1024])
    b31_d = din("b31", [128, 16])
    bval_d = din("bval", [128, 32])
    ablk_d = din("ablk", [32, 32, 128])
    btab_d = din("btab", [128, 9, 512])
    pastb_d = din("pastb", [4, 128, 2, 32])

    ptrep_d = din("ptrep", [128, 256], I32)
    iot_d = din("iot", [128, 1], I32)
    cdsa = din("cdsa", [2560 * 128, 1088]) if DBG_SAMPLE else None
    cmoba = din("cmoba", [2560 * 128, 1024]) if DBG_SAMPLE else None
    y_p = dout("y_p", [2048, D])
    y_s = dout("y_s", [NSAMP, D])
    dsa_p = dout("dsa_p", [2048, 1088])
    moba_p = dout("moba_p", [2048, 1024])
    dsa_s = dout("dsa_s", [NSAMP, 1088])
    moba_s = dout("moba_s", [NSAMP, 1024])

    kaT_d = dscr("kaT_d", [4, 128, T], BF16)
    kbT_d = dscr("kbT_d", [4, 128, T], BF16)
    kiT_d = dscr("kiT_d", [64, T], BF16)
    va_d = dscr("va_d", [NT, 128, 520], BF16)
    vb_d = dscr("vb_d", [NT, 128, 520], BF16)
    qaT_d = dscr("qaT_d", [4, 128, 4, 512], BF16)
    qbT_d = dscr("qbT_d", [4, 128, 4, 512], BF16)
    qiT_d = dscr("qiT_d", [4, 128, 2, 512], BF16)
    sga_d = dscr("sga_d", [4, 8, 128, 512], F32)
    sgb_d = dscr("sgb_d", [4, 8, 128, 512], F32)
    skaT_d = dscr("skaT_d", [4, 4, 128, 8320], BF16)
    skbT_d = dscr("skbT_d", [4, 4, 128, 8320], BF16)
    skiT_d = dscr("skiT_d", [4, 64, 8704], BF16)
    sva_d = dscr("sva_d", [4, 65, 128, 520], BF16)
    svb_d = dscr("svb_d", [4, 65, 128, 520], BF16)

    out_toks = []

    with ExitStack() as es:
        P = Prog(nc, es)

        uid = [0]

        def sb(st, name, shape, dt):
            uid[0] += 1
            try:
                return TT(st.enter_context(nc.sbuf_tensor("s_%s_%d" % (name, uid[0]), list(shape), dt)))
            except BaseException as ex:
                print("SB ALLOC FAIL", name, shape, ex)
                raise

        def ps(st, name, shape, dt):
            return TT(st.enter_context(nc.psum_tensor("p_" + name, list(shape), dt)))

        banks = [ps(es, "bank%d" % i, [128, 512], F32) for i in range(8)]
        for bk_ in banks:
            bk_.b.x = True
        ident = sb(es, "identf", [128, 128], F32)
        identb = sb(es, "identb", [128, 128], BF16)
        b31 = sb(es, "b31", [128, 16], F32)
        lohi = sb(es, "lohi", [128, 16, 8], F32)
        lohis = sb(es, "lohis", [128, 8], F32)
        meansT = sb(es, "meansT", [128, 4, 32], F32)
        meansTb = sb(es, "meansTb", [128, 4, 32], BF16)
        ones_t = sb(es, "ones_t", [128, 64], F32)
        P.dma("sync", lambda e: e.dma_start(out=ident[:], in_=ident_d), writes=[ident])
        P.dma("gpsimd", lambda e: e.dma_start(out=identb[:], in_=ident_d), writes=[identb])
        P.dma("sync", lambda e: e.dma_start(out=b31[:], in_=b31_d), writes=[b31])
        P.op("vector", lambda e: e.memset(ones_t[:], 1.0), writes=[ones_t])

        qaTs = sb(es, "qaTs", [128, 4, NSAMP], BF16)
        qbTs = sb(es, "qbTs", [128, 4, NSAMP], BF16)
        qiTs = sb(es, "qiTs", [128, 2, NSAMP], BF16)
        qiTm = [sb(es, "qiTm%d" % i, [128, 2, NSAMP], BF16) for i in range(4)]
        qbTm = [sb(es, "qbTm%d" % i, [128, 4, NSAMP], BF16) for i in range(4)]
        sgas = sb(es, "sgas", [128, 8, NSAMP], F32)
        sgbs = sb(es, "sgbs", [128, 8, NSAMP], F32)
        smeansTb = [sb(es, "smeansTb%d" % i, [128, 4, 32], BF16) for i in range(4)]
        qTall = TT(None)
        maskall = TT(None)
        oall = TT(None)
        pastb_t = TT(None)
        sscr = TT(None)
        bank_rr = [0]

        def next_bank(lo=0, hi=8):
            i = bank_rr[0]
            if not (lo <= i < hi):
                i = lo
            bank_rr[0] = i + 1 if i + 1 < hi else lo
            return banks[i]

        evac_rr = [0]

        def evac(out_ap, in_ap, reads, writes, scale=None, func=None, eng=None):
            if eng is None:
                eng = "scalar" if (evac_rr[0] % 2 == 0) else "vector"
                evac_rr[0] += 1
            if func is not None:
                eng = "scalar"
            if eng == "scalar":
                f = func if func is not None else AF.Copy
                if scale is None:
                    P.op("scalar", lambda e: e.activation(out=out_ap, in_=in_ap, func=f), reads=reads, writes=writes)
                else:
                    P.op("scalar", lambda e: e.activation(out=out_ap, in_=in_ap, func=f, scale=scale), reads=reads, writes=writes)
            else:
                if scale is None:
                    P.op("vector", lambda e: e.tensor_copy(out=out_ap, in_=in_ap), reads=reads, writes=writes)
                else:
                    P.op("vector", lambda e: e.tensor_scalar(out=out_ap, in0=in_ap, scalar1=float(scale), scalar2=None, op0=ALU.mult),
                         reads=reads, writes=writes)

        with ExitStack() as s1:
            win = sb(s1, "win", [128, 8, DIN], BF16)
            for c in range(8):
                P.dma("gpsimd", lambda e, c=c: e.dma_start(out=win[:, c, :], in_=w_in[c * 128:(c + 1) * 128, :]), writes=[win])
            xin = [sb(s1, "xin%d" % i, [128, 4, D], F32) for i in range(2)]
            XT = [sb(s1, "XT%d" % i, [128, 8, 512], BF16) for i in range(2)]
            kstg = [sb(s1, "kstg%d" % i, [128, 512], BF16) for i in range(4)]
            vstg = [sb(s1, "vstg%d" % i, [128, 8, 65], BF16) for i in range(4)]
            fstg = [sb(s1, "fstg%d" % i, [128, 512], F32) for i in range(2)]
            rowd = [sb(s1, "rowd%d" % i, [128, 1088], F32) for i in range(1)]
            rowm = [sb(s1, "rowm%d" % i, [128, 1024], F32) for i in range(1)]
            qiw = sb(s1, "qiw", [128, 256], F32)
            wsb = sb(s1, "wsb", [128, 4], F32)
            qistg = sb(s1, "qistg", [128, 2, 512], BF16)
            for v in vstg:
                P.op("vector", lambda e, v=v: e.memset(v[:], 1.0), writes=[v])
            kst_i = [0]
            vst_i = [0]
            fst_i = [0]

            def proj_fm(col0, M, xt, scale=None, func=None, N=512):
                bk = next_bank()
                for dc in range(8):
                    P.op("tensor", lambda e, dc=dc, bk=bk: e.matmul(bk[0:M, 0:N], lhsT=win[:, dc, col0:col0 + M], rhs=xt[:, dc, 0:N],
                                                                  start=(dc == 0), stop=(dc == 7)),
                         reads=[win, xt], writes=[bk])
                return bk

            def proj_tm(col0, N, xt, t, ntok=128):
                bk = next_bank()
                for dc in range(8):
                    P.op("tensor", lambda e, dc=dc, bk=bk: e.matmul(bk[0:ntok, 0:N], lhsT=xt[:, dc, t * 128:t * 128 + ntok],
                                                                  rhs=win[:, dc, col0:col0 + N], start=(dc == 0), stop=(dc == 7)),
                         reads=[win, xt], writes=[bk])
                return bk

            def load_xT(src_rows_ap, xi, xt, ntile, ntok=128, preloaded=False):
                for t in range(ntile):
                    pass
                if not preloaded:
                    P.dma("gpsimd", lambda e: e.dma_start(out=xi[0:ntok, 0:ntile, :], in_=src_rows_ap.rearrange("(t p) d -> p t d", p=ntok)),
                          writes=[xi])
                for c in range(8):
                    bk = next_bank()
                    for t in range(ntile):
                        P.op("tensor", lambda e, c=c, t=t, bk=bk: e.transpose(out=bk[:, t * 128:t * 128 + ntok],
                                                                            in_=xi[0:ntok, t, c * 128:(c + 1) * 128],
                                                                            identity=ident[0:ntok, 0:ntok]),
                             reads=[xi, ident], writes=[bk])
                    w = ntile * 128 if ntok == 128 else ntok
                    evac(xt[:, c, 0:w], bk[:, 0:w], [bk], [xt])

            if DBG_SAMPLE:
                ptr = sb(s1, "ptr", [128, 256], I32)
                iot = sb(s1, "iot", [128, 1], I32)
                idx = sb(s1, "idx", [128, 256], I32)
                P.dma("sync", lambda e: e.dma_start(out=ptr[:], in_=ptrep_d), writes=[ptr])
                P.dma("sync", lambda e: e.dma_start(out=iot[:], in_=iot_d), writes=[iot])
                P.op("vector", lambda e: e.tensor_scalar(out=idx[:], in0=ptr[:], scalar1=128.0, scalar2=iot[:, 0:1], op0=ALU.mult, op1=ALU.add),
                     reads=[ptr, iot], writes=[idx])
                gd = [sb(s1, "gd%d" % i, [128, 1088], F32) for i in range(4)]
                gm = [sb(s1, "gm%d" % i, [128, 1024], F32) for i in range(4)]
                msum = sb(s1, "msum", [128, 4, 32], F32)

            def stage_gather(s, blk):
                gds = []
                gms = []
                for t in range(4):
                    pg = blk * 4 + t
                    col = s * 64 + pg
                    g1 = gd[t]
                    g2 = gm[t]
                    P.dma("gpsimd", lambda e, g1=g1, col=col: e.indirect_dma_start(out=g1[:, :], out_offset=None, in_=cdsa[:, :],
                                                                                 in_offset=bass.IndirectOffsetOnAxis(ap=idx[:, col:col + 1], axis=0)),
                          reads=[idx], writes=[g1])
                    P.dma("gpsimd", lambda e, g2=g2, col=col: e.indirect_dma_start(out=g2[:, :], out_offset=None, in_=cmoba[:, :],
                                                                                 in_offset=bass.IndirectOffsetOnAxis(ap=idx[:, col:col + 1], axis=0)),
                          reads=[idx], writes=[g2])
                    gds.append(g1)
                    gms.append(g2)
                return gds, gms

            def stage_compute(s, blk, gds, gms):
                for (srcs, c0, M, dst, cidx) in ([(gds, 128 * c, 128, skaT_d[s, c], None) for c in range(4)] + [(gds, 1024, 64, skiT_d[s], None)]
                                                + [(gms, 128 * c, 128, skbT_d[s, c], c) for c in range(4)]):
                    bk = next_bank()
                    for t in range(4):
                        P.op("tensor", lambda e, bk=bk, g_=srcs[t], c0=c0, M=M, t=t: e.transpose(out=bk[0:M, t * 128:(t + 1) * 128], in_=g_[:, c0:c0 + M], identity=ident[:]),
                             reads=[srcs[t], ident], writes=[bk])
                    if cidx is not None:
                        for hb in range(2):
                            P.op("vector", lambda e, bk=bk, cidx=cidx, blk=blk, hb=hb: e.tensor_reduce(
                                out=msum[:, cidx, 2 * blk + hb:2 * blk + hb + 1], in_=bk[:, hb * 256:(hb + 1) * 256], axis=AX.X, op=ALU.add), reads=[bk], writes=[msum])
                    st = kstg[kst_i[0] % 4]
                    kst_i[0] += 1
                    evac(st[0:M, :], bk[0:M, :], [bk], [st])
                    P.dma("sync", lambda e, st=st, dst=dst, M=M, blk=blk: e.dma_start(out=dst[0:M, blk * 512:(blk + 1) * 512], in_=st[0:M, :]), reads=[st], writes=[sscr])
                for t in range(4):
                    pg = blk * 4 + t
                    for (g_, dstd) in ((gds[t], sva_d), (gms[t], svb_d)):
                        vs_ = vstg[vst_i[0] % 4]
                        vst_i[0] += 1
                        if vst_i[0] % 2 == 0:
                            P.op("vector", lambda e, vs_=vs_, g_=g_: e.tensor_copy(out=vs_[:, :, 0:64], in_=g_[:, 512:1024].rearrange("p (h d) -> p h d", h=8)),
                                 reads=[g_], writes=[vs_])
                        else:
                            P.op("scalar", lambda e, vs_=vs_, g_=g_: e.activation(out=vs_[:, :, 0:64], in_=g_[:, 512:1024].rearrange("p (h d) -> p h d", h=8), func=AF.Copy),
                                 reads=[g_], writes=[vs_])
                        P.dma("sync", lambda e, vs_=vs_, dstd=dstd, s=s, pg=pg: e.dma_start(out=dstd[s, pg], in_=vs_[:].rearrange("p h d -> p (h d)")), reads=[vs_], writes=[sscr])
                if blk == 15:
                    P.op("vector", lambda e, s=s: e.tensor_scalar(out=smeansTb[s][:], in0=msum[:], scalar1=1.0 / 256.0, scalar2=None, op0=ALU.mult),
                         reads=[msum], writes=[smeansTb[s]])

            stg_pend = {}

            def stg(sbk, step):
                if not DBG_SAMPLE:
                    return
                s_ = sbk // 4
                b0 = 4 * (sbk % 4)
                if step >= 1:
                    g_ = stg_pend.pop(step - 1)
                    stage_compute(s_, b0 + step - 1, *g_)
                if step <= 3:
                    stg_pend[step] = stage_gather(s_, b0 + step)

            for sbk in range(DBG_NBLK):
                stg(sbk, 0)
                xi = xin[sbk % 2]
                xt = XT[sbk % 2]
                if sbk == 0:
                    P.dma("gpsimd", lambda e, xi=xi: e.dma_start(out=xi[:, :, :], in_=xs[0:512, :].rearrange("(t p) d -> p t d", p=128)), writes=[xi])
                load_xT(xs[sbk * 512:(sbk + 1) * 512, :], xi, xt, 4, preloaded=True)
                if sbk + 1 < DBG_NBLK:
                    xn = xin[(sbk + 1) % 2]
                    P.dma("gpsimd", lambda e, xn=xn, sbk=sbk: e.dma_start(out=xn[:, :, :], in_=xs[(sbk + 1) * 512:(sbk + 2) * 512, :].rearrange("(t p) d -> p t d", p=128)),
                          writes=[xn])
                own = (sbk % 4 == 3) and DBG_LEVEL >= 5
                mq = sbk // 4
                for kidx_, (col0, M, dst, is_kb, cidx) in enumerate([] if DBG_LEVEL < 3 else (
                        [(KA + 128 * c, 128, kaT_d[c], False, c) for c in range(4)]
                        + [(KI, 64, kiT_d, False, 0)]
                        + [(KB + 128 * c, 128, kbT_d[c], True, c) for c in range(4)])):
                    if kidx_ == 4:
                        stg(sbk, 1)
                    bk = proj_fm(col0, M, xt)
                    st = kstg[kst_i[0] % 4]
                    kst_i[0] += 1
                    evac(st[0:M, :], bk[0:M, :], [bk], [st])
                    if is_kb:
                        for hb in range(2):
                            P.op("vector", lambda e, bk=bk, cidx=cidx, sbk=sbk, hb=hb: e.tensor_reduce(
                                out=meansT[:, cidx, 2 * sbk + hb:2 * sbk + hb + 1], in_=bk[:, hb * 256:(hb + 1) * 256],
                                axis=AX.X, op=ALU.add), reads=[bk], writes=[meansT])
                    P.dma("sync", lambda e, st=st, dst=dst, M=M, sbk=sbk: e.dma_start(out=dst[0:M, sbk * 512:(sbk + 1) * 512], in_=st[0:M, :]),
                          reads=[st])
                stg(sbk, 2)
                for t in range(4 if DBG_LEVEL >= 4 else 0):
                    if t == 2:
                        stg(sbk, 3)
                    u = sbk * 4 + t
                    if own:
                        rd = rowd[0]
                        rm = rowm[0]
                    for (col0, dst, which) in ((VA, va_d, 0), (VB, vb_d, 1)):
                        bk = proj_tm(col0, 512, xt, t)
                        vs_ = vstg[vst_i[0] % 4]
                        vst_i[0] += 1
                        evac(vs_[:, :, 0:64], bk[:, :].rearrange("p (h d) -> p h d", h=8), [bk], [vs_])
                        P.dma("sync", lambda e, vs_=vs_, dst=dst, u=u: e.dma_start(out=dst[u], in_=vs_[:].rearrange("p h d -> p (h d)")),
                              reads=[vs_])
                        if own:
                            tgt = rd if which == 0 else rm
                            evac(tgt[:, 512:1024], bk[:, :], [bk], [tgt])
                    if own:
                        bk = proj_tm(KA, 512, xt, t)
                        evac(rd[:, 0:512], bk[:, :], [bk], [rd])
                        bk = proj_tm(KI, 64, xt, t)
                        evac(rd[:, 1024:1088], bk[:, 0:64], [bk], [rd])
                        bk = proj_tm(KB, 512, xt, t)
                        evac(rm[:, 0:512], bk[:, :], [bk], [rm])
                        r0 = (mq * 4 + t) * 128
                        out_toks.append(P.dma("sync", lambda e, rd=rd, r0=r0: e.dma_start(out=dsa_p[r0:r0 + 128, :], in_=rd[:]), reads=[rd]))
                        out_toks.append(P.dma("sync", lambda e, rm=rm, r0=r0: e.dma_start(out=moba_p[r0:r0 + 128, :], in_=rm[:]), reads=[rm]))
                        qt = mq * 4 + t
                        bk = proj_tm(WI, 4, xt, t)
                        P.op("vector", lambda e, bk=bk: e.tensor_copy(out=wsb[:], in_=bk[:, 0:4]), reads=[bk], writes=[wsb])
                        P.op("vector", lambda e, qt=qt: e.tensor_scalar(out=lohi[:, qt, 0:4], in0=wsb[:], scalar1=0.0, scalar2=-BIG,
                                                                    op0=ALU.is_le, op1=ALU.mult), reads=[wsb], writes=[lohi])
                        P.op("vector", lambda e, qt=qt: e.tensor_scalar(out=lohi[:, qt, 4:8], in0=wsb[:], scalar1=0.0, scalar2=BIG,
                                                                    op0=ALU.is_gt, op1=ALU.mult), reads=[wsb], writes=[lohi])
                        bk = proj_tm(QI, 256, xt, t)
                        for h in range(4):
                            P.op("vector", lambda e, bk=bk, h=h: e.tensor_scalar(out=qiw[:, h * 64:(h + 1) * 64], in0=bk[:, h * 64:(h + 1) * 64],
                                                                             scalar1=wsb[:, h:h + 1], scalar2=None, op0=ALU.mult),
                                 reads=[bk, wsb], writes=[qiw])
                        for pc in range(2):
                            bk2 = next_bank()
                            P.op("tensor", lambda e, bk2=bk2, pc=pc: e.transpose(out=bk2[:, 0:128], in_=qiw[:, pc * 128:(pc + 1) * 128],
                                                                               identity=ident[:]), reads=[qiw, ident], writes=[bk2])
                            evac(qistg[:, pc, t * 128:(t + 1) * 128], bk2[:, 0:128], [bk2], [qistg])
                stg(sbk, 4)
                if own:
                    P.dma("sync", lambda e, mq=mq: e.dma_start(out=qiT_d[mq], in_=qistg[:]), reads=[qistg])
                    for (col0, dst) in ((QA, qaT_d), (QB, qbT_d)):
                        for c in range(4):
                            bk = proj_fm(col0 + 128 * c, 128, xt)
                            st = kstg[kst_i[0] % 4]
                            kst_i[0] += 1
                            evac(st[:], bk[:], [bk], [st], scale=0.125)
                            P.dma("sync", lambda e, st=st, dst=dst, mq=mq, c=c: e.dma_start(out=dst[mq, :, c, :], in_=st[:]), reads=[st])
                    for (col0, dst) in ((GA, sga_d), (GB, sgb_d)):
                        for c in range(8):
                            bk = proj_fm(col0 + 128 * c, 128, xt)
                            st = fstg[fst_i[0] % 2]
                            fst_i[0] += 1
                            evac(st[:], bk[:], [bk], [st], func=AF.Sigmoid)
                            P.dma("sync", lambda e, st=st, dst=dst, mq=mq, c=c: e.dma_start(out=dst[mq, c], in_=st[:]), reads=[st])

            xi = xin[0]
            assert True
            xt = XT[0]
            load_xT(xsm[:, :], xi, xt, 1, ntok=NSAMP)
            rd = rowd[0]
            rm = rowm[0]
            for (col0, N, tgt, o0) in ((KA, 512, rd, 0), (VA, 512, rd, 512), (KI, 64, rd, 1024), (KB, 512, rm, 0), (VB, 512, rm, 512)):
                bk = proj_tm(col0, N, xt, 0, ntok=NSAMP)
                evac(tgt[0:NSAMP, o0:o0 + N], bk[0:NSAMP, 0:N], [bk], [tgt])
            out_toks.append(P.dma("sync", lambda e: e.dma_start(out=dsa_s, in_=rd[0:NSAMP, :]), reads=[rd]))
            out_toks.append(P.dma("sync", lambda e: e.dma_start(out=moba_s, in_=rm[0:NSAMP, :]), reads=[rm]))
            if DBG_SAMPLE:
                zt = sb(s1, "zt", [128, 520], BF16)
                P.op("vector", lambda e: e.memset(zt[:], 0.0), writes=[zt])
                for (col0, dstT) in ((QA, qaTs), (QB, qbTs)):
                    for c in range(4):
                        bk = proj_fm(col0 + 128 * c, 128, xt, N=NSAMP)
                        evac(dstT[:, c, :], bk[:, 0:NSAMP], [bk], [dstT, qTall], scale=0.125)
                for (col0, dstT) in ((GA, sgas), (GB, sgbs)):
                    for c in range(8):
                        bk = proj_fm(col0 + 128 * c, 128, xt, N=NSAMP)
                        evac(dstT[:, c, :], bk[:, 0:NSAMP], [bk], [dstT], func=AF.Sigmoid)
                for (col0, M, dstd, cidx) in ([(KA + 128 * c, 128, skaT_d, c) for c in range(4)] + [(KB + 128 * c, 128, skbT_d, c) for c in range(4)] + [(KI, 64, skiT_d, None)]):
                    bk = proj_fm(col0, M, xt, N=NSAMP)
                    st = kstg[kst_i[0] % 4]
                    kst_i[0] += 1
                    evac(st[0:M, 0:NSAMP], bk[0:M, 0:NSAMP], [bk], [st])
                    for s in range(4):
                        dd = dstd[s, cidx] if cidx is not None else dstd[s]
                        P.dma("sync", lambda e, st=st, dd=dd, M=M, s=s: e.dma_start(out=dd[0:M, 8192:8200], in_=st[0:M, 8 * s:8 * s + 8]), reads=[st], writes=[sscr])
                        wz = (8320 - 8200) if cidx is not None else (8704 - 8200)
                        P.dma("sync", lambda e, dd=dd, M=M, wz=wz: e.dma_start(out=dd[0:M, 8200:8200 + wz], in_=zt[0:M, 0:wz]), reads=[zt], writes=[sscr])
                for (src, o0, dstd) in ((rd, 512, sva_d), (rm, 512, svb_d)):
                    vs_ = vstg[vst_i[0] % 4]
                    vst_i[0] += 1
                    P.op("vector", lambda e, vs_=vs_, src=src, o0=o0: e.tensor_copy(out=vs_[0:NSAMP, :, 0:64], in_=src[0:NSAMP, o0:o0 + 512].rearrange("p (h d) -> p h d", h=8)),
                         reads=[src], writes=[vs_])
                    for s in range(4):
                        P.dma("sync", lambda e, vs_=vs_, dstd=dstd, s=s: e.dma_start(out=dstd[s, 64, 0:8, :], in_=vs_[8 * s:8 * s + 8].rearrange("p h d -> p (h d)")),
                              reads=[vs_], writes=[sscr])
                        P.dma("sync", lambda e, dstd=dstd, s=s: e.dma_start(out=dstd[s, 64, 8:128, :], in_=zt[0:120, :]), reads=[zt], writes=[sscr])
                bk = proj_tm(WI, 4, xt, 0, ntok=NSAMP)
                P.op("vector", lambda e, bk=bk: e.tensor_copy(out=wsb[0:NSAMP, :], in_=bk[0:NSAMP, 0:4]), reads=[bk], writes=[wsb])
                P.op("vector", lambda e: e.tensor_scalar(out=lohis[0:NSAMP, 0:4], in0=wsb[0:NSAMP, :], scalar1=0.0, scalar2=-BIG, op0=ALU.is_le, op1=ALU.mult), reads=[wsb], writes=[lohis])
                P.op("vector", lambda e: e.tensor_scalar(out=lohis[0:NSAMP, 4:8], in0=wsb[0:NSAMP, :], scalar1=0.0, scalar2=BIG, op0=ALU.is_gt, op1=ALU.mult), reads=[wsb], writes=[lohis])
                bk = proj_tm(QI, 256, xt, 0, ntok=NSAMP)
                for h in range(4):
                    P.op("vector", lambda e, bk=bk, h=h: e.tensor_scalar(out=qiw[0:NSAMP, h * 64:(h + 1) * 64], in0=bk[0:NSAMP, h * 64:(h + 1) * 64],
                                                                     scalar1=wsb[0:NSAMP, h:h + 1], scalar2=None, op0=ALU.mult), reads=[bk, wsb], writes=[qiw])
                for pc in range(2):
                    bk2 = next_bank()
                    P.op("tensor", lambda e, bk2=bk2, pc=pc: e.transpose(out=bk2[:, 0:NSAMP], in_=qiw[0:NSAMP, pc * 128:(pc + 1) * 128], identity=ident[0:NSAMP, 0:NSAMP]),
                         reads=[qiw, ident], writes=[bk2])
                    evac(qiTs[:, pc, :], bk2[:, 0:NSAMP], [bk2], [qiTs])
                for s in range(4):
                    P.op("vector", lambda e, s=s: e.memset(qiTm[s][:], 0.0), writes=[qiTm[s]])
                    P.op("vector", lambda e, s=s: e.tensor_copy(out=qiTm[s][:, :, 8 * s:8 * s + 8], in_=qiTs[:, :, 8 * s:8 * s + 8]), reads=[qiTs], writes=[qiTm[s]])
                    P.op("vector", lambda e, s=s: e.memset(qbTm[s][:], 0.0), writes=[qbTm[s]])
                    P.op("vector", lambda e, s=s: e.tensor_copy(out=qbTm[s][:, :, 8 * s:8 * s + 8], in_=qbTs[:, :, 8 * s:8 * s + 8]), reads=[qbTs], writes=[qbTm[s]])
            P.op("vector", lambda e: e.tensor_scalar(out=meansTb[:], in0=meansT[:], scalar1=1.0 / 256.0, scalar2=None, op0=ALU.mult),
                 reads=[meansT], writes=[meansTb])
            P.barrier()

        lnr = sb(es, "lnr", [128, 4, D], F32)
        P.dma("sync", lambda e: e.dma_start(out=lnr[:], in_=lnrep), writes=[lnr])
        ablk = sb(es, "ablk", [32, 32, 128], BF16)
        P.dma("gpsimd", lambda e: e.dma_start(out=ablk[:], in_=ablk_d), writes=[ablk])
        gb2 = sb(es, "gb2", [128, 32], F32)
        P.dma("sync", lambda e: e.dma_start(out=gb2[:], in_=bval_d), writes=[gb2])

        def layer_norm(z, nt, gi, out_ap, sm, st6):
            for hf in range(2):
                P.op("vector", lambda e, hf=hf: e.bn_stats(out=st6[0:nt, hf, :], in_=z[0:nt, hf * 512:(hf + 1) * 512]), reads=[z], writes=[st6])
            P.op("vector", lambda e: e.bn_aggr(out=sm[0:nt, 0:2], in_=st6[0:nt].rearrange("p a b -> p (a b)")), reads=[st6], writes=[sm])
            P.op("vector", lambda e: e.tensor_scalar(out=sm[0:nt, 2:3], in0=sm[0:nt, 1:2], scalar1=LN_EPS, scalar2=None, op0=ALU.add), reads=[sm], writes=[sm])
            P.op("scalar", lambda e: e.activation(out=sm[0:nt, 3:4], in_=sm[0:nt, 2:3], func=AF.Sqrt), reads=[sm], writes=[sm])
            P.op("vector", lambda e: e.reciprocal(out=sm[0:nt, 4:5], in_=sm[0:nt, 3:4]), reads=[sm], writes=[sm])
            P.op("vector", lambda e: e.tensor_scalar(out=z[0:nt, :], in0=z[0:nt, :], scalar1=sm[0:nt, 0:1], scalar2=sm[0:nt, 4:5], op0=ALU.subtract, op1=ALU.mult),
                 reads=[z, sm], writes=[z])
            P.op("vector", lambda e: e.tensor_tensor(out=z[0:nt, :], in0=z[0:nt, :], in1=lnr[0:nt, gi, :], op=ALU.mult), reads=[z, lnr], writes=[z])
            return lambda e: e.tensor_tensor(out=out_ap, in0=z[0:nt, :], in1=lnr[0:nt, gi + 1, :], op=ALU.add)

        def indexer(st, NP, nch, emit_scores, lohi_fn, bi_fn, mbT, qoff, btab):
            nk = nch * 512
            n1 = (int(nk * ACT_SPLIT) // 512) * 512 if nk >= 2048 else nk
            n2 = nk - n1
            sc = sb(st, "sc", [128, nk], F32)
            junk = sb(st, "junk", [128, n1], BF16)
            tmp = [sb(st, "tmpr%d" % i, [128, 512], F32) for i in range(2)]
            mbc = [sb(st, "mbc%d" % i, [128, 512], F32) for i in range(2)]
            sm = sb(st, "sma", [128, 64], F32)
            CMIN, CMAX, LO, HI, MID, CNT, GE, D1, D2 = 0, 20, 40, 41, 42, 43, 44, 45, 46
            tmp_i = 0
            for c in range(nch):
                emit_scores(c)
                scc = sc[0:NP, c * 512:(c + 1) * 512]
                P.op("vector", lambda e, scc=scc: e.tensor_scalar(out=scc, in0=banks[0][0:NP, :], scalar1=lohi_fn(0), scalar2=lohi_fn(4), op0=ALU.max, op1=ALU.min),
                     reads=[banks[0], lohi, lohis], writes=[sc])
                for h in range(1, 4):
                    tp = tmp[tmp_i % 2]
                    tmp_i += 1
                    P.op("vector", lambda e, tp=tp, h=h: e.tensor_scalar(out=tp[0:NP, :], in0=banks[h][0:NP, :], scalar1=lohi_fn(h), scalar2=lohi_fn(4 + h),
                                                                     op0=ALU.max, op1=ALU.min), reads=[banks[h], lohi, lohis], writes=[tp])
                    P.op("vector", lambda e, tp=tp, scc=scc: e.tensor_tensor(out=scc, in0=scc, in1=tp[0:NP, :], op=ALU.add), reads=[sc, tp], writes=[sc])
                P.op("vector", lambda e, scc=scc, c=c: e.tensor_reduce(out=sm[0:NP, CMIN + c:CMIN + c + 1], in_=scc, axis=AX.X, op=ALU.min), reads=[sc], writes=[sm])
                P.op("vector", lambda e, scc=scc, c=c: e.tensor_reduce(out=sm[0:NP, CMAX + c:CMAX + c + 1], in_=scc, axis=AX.X, op=ALU.max), reads=[sc], writes=[sm])
                bi = bi_fn(c)
                P.op("vector", lambda e, scc=scc, c=c, bi=bi: e.scalar_tensor_tensor(out=scc, in0=scc, scalar=float(-EPS_TIE * 512 * c), in1=btab[0:NP, bi, :],
                                                                                 op0=ALU.add, op1=ALU.add), reads=[sc, btab], writes=[sc])
            chk(2)
            P.op("vector", lambda e: e.tensor_reduce(out=sm[0:NP, LO:LO + 1], in_=sm[0:NP, CMIN:CMIN + nch], axis=AX.X, op=ALU.min), reads=[sm], writes=[sm])
            P.op("vector", lambda e: e.tensor_reduce(out=sm[0:NP, HI:HI + 1], in_=sm[0:NP, CMAX:CMAX + nch], axis=AX.X, op=ALU.max), reads=[sm], writes=[sm])
            P.op("vector", lambda e: e.tensor_scalar(out=sm[0:NP, LO:LO + 1], in0=sm[0:NP, LO:LO + 1], scalar1=-1.0, scalar2=None, op0=ALU.add), reads=[sm], writes=[sm])
            P.op("vector", lambda e: e.tensor_scalar(out=sm[0:NP, HI:HI + 1], in0=sm[0:NP, HI:HI + 1], scalar1=1.0, scalar2=None, op0=ALU.add), reads=[sm], writes=[sm])
            steps = [None] * NIT + [float(-EPS_TIE * (nk + 64)), 1e-30] + [None] * NIT2
            midt = sb(st, "midt", [128, 2], F32)
            sact = sb(st, "sact", [128, 2], F32)
            junk2 = sb(st, "junk2", [128, max(n2, 8)], BF16)
            for pv in steps:
                if pv is None:
                    P.op("vector", lambda e: e.tensor_scalar(out=midt[0:NP, 0:1], in0=sm[0:NP, LO:LO + 1], scalar1=sm[0:NP, HI:HI + 1], scalar2=0.5,
                                                             op0=ALU.add, op1=ALU.mult), reads=[sm], writes=[midt])
                else:
                    P.op("vector", lambda e, pv=pv: e.tensor_scalar(out=midt[0:NP, 0:1], in0=sm[0:NP, LO:LO + 1], scalar1=pv, scalar2=sm[0:NP, HI:HI + 1],
                                                                  op0=ALU.max, op1=ALU.min), reads=[sm], writes=[midt])
                if n2 > 0:
                    P.op("scalar", lambda e: e.activation(out=junk2[0:NP, 0:n2], in_=sc[0:NP, n1:nk], func=AF.Sign, bias=midt[0:NP, 0:1], scale=-1.0,
                                                          accum_out=sact[0:NP, 0:1]), reads=[sc, midt], writes=[junk2, sact])
                P.op("vector", lambda e: e.tensor_scalar(out=junk[0:NP, 0:n1], in0=sc[0:NP, 0:n1], scalar1=midt[0:NP, 0:1], scalar2=0.0,
                                                         op0=ALU.is_ge, op1=ALU.add, accum_out=sm[0:NP, CNT:CNT + 1]), reads=[sc, midt], writes=[junk, sm])
                if n2 > 0:
                    P.op("vector", lambda e: e.scalar_tensor_tensor(out=sm[0:NP, CNT:CNT + 1], in0=sact[0:NP, 0:1], scalar=-0.5, in1=sm[0:NP, CNT:CNT + 1],
                                                                    op0=ALU.mult, op1=ALU.add), reads=[sm, sact], writes=[sm])
                P.op("vector", lambda e: e.tensor_scalar(out=sm[0:NP, GE:GE + 1], in0=sm[0:NP, CNT:CNT + 1], scalar1=float(255.5 - 0.5 * n2), scalar2=None, op0=ALU.is_ge),
                     reads=[sm], writes=[sm])
                P.op("vector", lambda e: e.tensor_tensor(out=sm[0:NP, D1:D1 + 1], in0=midt[0:NP, 0:1], in1=sm[0:NP, LO:LO + 1], op=ALU.subtract), reads=[sm, midt], writes=[sm])
                P.op("vector", lambda e: e.tensor_tensor(out=sm[0:NP, D2:D2 + 1], in0=sm[0:NP, HI:HI + 1], in1=midt[0:NP, 0:1], op=ALU.subtract), reads=[sm, midt], writes=[sm])
                P.op("vector", lambda e: e.scalar_tensor_tensor(out=sm[0:NP, LO:LO + 1], in0=sm[0:NP, D1:D1 + 1], scalar=sm[0:NP, GE:GE + 1], in1=sm[0:NP, LO:LO + 1],
                                                                op0=ALU.mult, op1=ALU.add), reads=[sm], writes=[sm])
                P.op("vector", lambda e: e.scalar_tensor_tensor(out=sm[0:NP, HI:HI + 1], in0=sm[0:NP, D2:D2 + 1], scalar=sm[0:NP, GE:GE + 1], in1=midt[0:NP, 0:1],
                                                                op0=ALU.mult, op1=ALU.add), reads=[sm, midt], writes=[sm])
            chk(3)
            for c in range(nch):
                mb_ = mbc[c % 2]
                P.op("vector", lambda e, mb_=mb_, c=c: e.tensor_scalar(out=mb_[0:NP, :], in0=sc[0:NP, c * 512:(c + 1) * 512], scalar1=sm[0:NP, LO:LO + 1],
                                                                   scalar2=None, op0=ALU.is_ge), reads=[sc, sm], writes=[mb_])
                bk = banks[4 + (c % 4)]
                for t in range(4):
                    P.op("tensor", lambda e, bk=bk, mb_=mb_, t=t: e.transpose(out=bk[:, t * 128:t * 128 + NP], in_=mb_[0:NP, t * 128:(t + 1) * 128],
                                                                         identity=ident[0:NP, 0:NP]), reads=[mb_, ident], writes=[bk])
                evac(mbT[:, 4 * c:4 * c + 4, qoff:qoff + NP], bk[:, :].rearrange("p (t q) -> p t q", t=4)[:, :, 0:NP], [bk], [mbT], eng="scalar")

        def moba_gate(st, NP, emit_gate, bias_ap, ubo, mbBT, qoff):
            gsb = sb(st, "gsb", [128, 8, 32], F32)
            m8 = sb(st, "m8", [128, 8, 8], F32)
            thr = sb(st, "thr", [128, 8], F32)
            mbB = sb(st, "mbB", [128, 8, 32], F32)
            bk = banks[0]
            emit_gate(bk)
            for h in range(8):
                P.op("vector", lambda e, h=h: e.tensor_tensor(out=gsb[0:NP, h, :], in0=bk[0:NP, h * 32:(h + 1) * 32], in1=bias_ap, op=ALU.add),
                     reads=[bk, pastb_t], writes=[gsb])
            for h in range(8):
                P.op("vector", lambda e, h=h: e.max(out=m8[0:NP, h, :], in_=gsb[0:NP, h, :]), reads=[gsb], writes=[m8])
            P.op("vector", lambda e: e.tensor_scalar(out=thr[0:NP, :], in0=m8[0:NP, :, 2], scalar1=-1e29, scalar2=None, op0=ALU.max), reads=[m8], writes=[thr])
            for h in range(8):
                P.op("vector", lambda e, h=h: e.tensor_scalar(out=mbB[0:NP, h, :], in0=gsb[0:NP, h, :], scalar1=thr[0:NP, h:h + 1], scalar2=NEGM,
                                                          op0=ALU.is_lt, op1=ALU.mult), reads=[gsb, thr], writes=[mbB])
            if ubo is not None:
                P.op("vector", lambda e: e.memset(mbB[0:NP, :, ubo:ubo + 1], 0.0), writes=[mbB])
            for g in range(2):
                bk2 = banks[4 + g]
                for hh in range(4):
                    h = 4 * g + hh
                    P.op("tensor", lambda e, bk2=bk2, h=h, hh=hh: e.transpose(out=bk2[0:32, hh * 128:hh * 128 + NP], in_=mbB[0:NP, h, :], identity=ident[0:NP, 0:NP]),
                         reads=[mbB, ident], writes=[bk2])
                evac(mbBT[0:32, 4 * g:4 * g + 4, qoff:qoff + NP], bk2[0:32, :].rearrange("p (h q) -> p h q", h=4)[:, :, 0:NP], [bk2], [mbBT], eng="vector")

        class AttBufs:
            def __init__(self, st):
                self.kTs = [sb(st, "kTs%d" % i, [128, 2048], BF16) for i in range(2)]
                self.vss = [sb(st, "vss%d" % i, [128, 16, 130], BF16) for i in range(2)]
                self.PTs = [sb(st, "PT%d" % i, [128, 512], BF16) for i in range(4)]
                self.wts = [sb(st, "wts%d" % i, [128, 1024], BF16) for i in range(4)]
                self.rd = sb(st, "rd", [128, 512], F32)
                self.rb = sb(st, "rb", [128, 512], F32)
                self.kv_i = 0
                self.pt_i = 0
                self.wt_i = 0
                self.lb_i = 0

        def attend(A, kT_fn, v_fn, q_fn, o_fn, hoff, ntl, NQ, diag_fn, mask_fn):
            nkb = (ntl + 15) // 16
            SKEW = 2
            for p in range(4):
                pend = []
                wth = []
                for hh in range(2):
                    w_ = A.wts[A.wt_i % 4]
                    A.wt_i += 1
                    P.dma("gpsimd", lambda e, w_=w_, hd=hoff + 2 * p + hh: e.dma_start(out=w_[:], in_=wtab_d[:, hd, :]), writes=[w_])
                    wth.append(w_)
                for kb in range(nkb):
                    nt_ = min(16, ntl - 16 * kb)
                    kt = A.kTs[A.kv_i % 2]
                    vs = A.vss[A.kv_i % 2]
                    A.kv_i += 1
                    P.dma("sync", lambda e, kt=kt, p=p, kb=kb, nt_=nt_: e.dma_start(out=kt[:, 0:nt_ * 128], in_=kT_fn(p)[:, kb * 2048:kb * 2048 + nt_ * 128]), writes=[kt])
                    P.dma("sync", lambda e, vs=vs, p=p, kb=kb, nt_=nt_: e.dma_start(
                        out=vs[:, 0:nt_, :], in_=v_fn()[kb * 16:kb * 16 + nt_, :, p * 130:(p + 1) * 130].rearrange("t k f -> k t f")), writes=[vs])
                    for tl in range(nt_):
                        u = kb * 16 + tl
                        x0 = diag_fn(u)
                        diag = x0 is not None
                        items = []
                        for hh in range(2):
                            h = 2 * p + hh
                            r0 = 64 * hh
                            L = banks[A.lb_i % 4]
                            A.lb_i += 1
                            mk = mask_fn(u, h)
                            mulmask = None
                            if mk is not None and mk[0] is None:
                                mulmask = mk[1]
                                mk = None
                            last1 = (mk is None) and (not diag)
                            P.op("tensor", lambda e, L=L, kt=kt, r0=r0, tl=tl, p=p, last1=last1: e.matmul(
                                L[:, 0:NQ], lhsT=kt[r0:r0 + 64, tl * 128:(tl + 1) * 128], rhs=q_fn(r0, p), start=True, stop=last1),
                                reads=[kt, qTall], writes=[L])
                            items.append((hh, h, L, mk, mulmask))
                        for (hh, h, L, mk, mulmask) in items:
                            OT = banks[4 + hh]
                            if mk is not None:
                                lh, rh = mk
                                P.op("tensor", lambda e, L=L, lh=lh, rh=rh, diag=diag: e.matmul(L[:, 0:NQ], lhsT=lh, rhs=rh, start=False, stop=(not diag)),
                                     reads=[identb, ablk, maskall], writes=[L])
                            if diag:
                                P.op("tensor", lambda e, L=L, w_=wth[hh], x0=x0: e.matmul(L[:, 0:NQ], lhsT=identb[:], rhs=w_[:, x0:x0 + NQ], start=False, stop=True),
                                     reads=[identb, wth[hh]], writes=[L])
                            PT = A.PTs[A.pt_i % 4]
                            A.pt_i += 1
                            if diag:
                                P.op("scalar", lambda e, PT=PT, L=L: e.activation(out=PT[:, 0:NQ], in_=L[:, 0:NQ], func=AF.Exp), reads=[L], writes=[PT])
                            else:
                                P.op("scalar", lambda e, PT=PT, L=L, hd=hoff + h: e.activation(out=PT[:, 0:NQ], in_=L[:, 0:NQ], func=AF.Exp, bias=b31[:, hd:hd + 1]),
                                     reads=[L, b31], writes=[PT])
                            if mulmask is not None:
                                P.op("vector", lambda e, PT=PT, mm_=mulmask: e.tensor_tensor(out=PT[:, 0:NQ], in0=PT[:, 0:NQ], in1=mm_, op=ALU.mult),
                                     reads=[PT, maskall], writes=[PT])
                            pend.append((lambda e, OT=OT, vs=vs, tl=tl, hh=hh, PT=PT, u=u: e.matmul(
                                OT[0:65, 0:NQ], lhsT=vs[:, tl, hh * 65:(hh + 1) * 65], rhs=PT[:, 0:NQ], start=(u == 0), stop=(u == ntl - 1)),
                                [vs, PT], [OT]))
                        while len(pend) > SKEW:
                            f_, r_, w_2 = pend.pop(0)
                            P.op("tensor", f_, reads=r_, writes=w_2)
                while pend:
                    f_, r_, w_2 = pend.pop(0)
                    P.op("tensor", f_, reads=r_, writes=w_2)
                for hh in range(2):
                    h = 2 * p + hh
                    OT = banks[4 + hh]
                    P.op("vector", lambda e, OT=OT: e.reciprocal(out=A.rd[64:65, 0:NQ], in_=OT[64:65, 0:NQ]), reads=[OT], writes=[A.rd])
                    bx = banks[A.lb_i % 4]
                    A.lb_i += 1
                    P.op("tensor", lambda e, bx=bx: e.matmul(bx[0:64, 0:NQ], lhsT=ones_t[64:65, 0:64], rhs=A.rd[64:65, 0:NQ], start=True, stop=True),
                         reads=[ones_t, A.rd], writes=[bx])
                    P.op("scalar", lambda e, bx=bx: e.activation(out=A.rb[0:64, 0:NQ], in_=bx[0:64, 0:NQ], func=AF.Copy), reads=[bx], writes=[A.rb])
                    P.op("vector", lambda e, OT=OT, h=h: e.tensor_tensor(out=o_fn(h), in0=OT[0:64, 0:NQ], in1=A.rb[0:64, 0:NQ], op=ALU.mult),
                         reads=[OT, A.rb], writes=[oall])

        def phase3(NT, tiles, oaT, obT, sg_fn, x_fn, y_fn):
            with ExitStack() as s3:
                nti = len(tiles)
                hres = sb(s3, "hres", [128, nti, D], F32)
                hT = sb(s3, "hT", [128, 8, 512], BF16)
                sm3 = sb(s3, "sm3", [128, 8], F32)
                st6 = sb(s3, "st6", [128, 2, 6], F32)
                z = sb(s3, "z", [128, D], F32)
                with ExitStack() as s3a:
                    wau = sb(s3a, "wau", [64, 8, D], BF16)
                    wbu = sb(s3a, "wbu", [64, 8, D], BF16)
                    wo = sb(s3a, "wo", [128, 8, D], BF16)
                    P.dma("gpsimd", lambda e: e.dma_start(out=wau[:], in_=w_a_up.rearrange("(h d) n -> d h n", d=64)), writes=[wau])
                    P.dma("gpsimd", lambda e: e.dma_start(out=wbu[:], in_=w_b_up.rearrange("(h d) n -> d h n", d=64)), writes=[wbu])
                    P.dma("gpsimd", lambda e: e.dma_start(out=wo[:], in_=w_out.rearrange("(c p) n -> p c n", p=128)), writes=[wo])
                    mixT = sb(s3a, "mixT", [128, 8, 512], BF16)
                    sga = [sb(s3a, "sga%d" % i, [128, 512], F32) for i in range(2)]
                    sgb = [sb(s3a, "sgb%d" % i, [128, 512], F32) for i in range(2)]
                    ma = sb(s3a, "ma", [128, 512], F32)
                    xre = [sb(s3a, "xre%d" % i, [128, D], F32) for i in range(2)]
                    for fc in range(8):
                        sa_ = sga[fc % 2]
                        sb_ = sgb[fc % 2]
                        sg_fn(0, fc, sa_)
                        sg_fn(1, fc, sb_)
                        bA = banks[(2 * fc) % 8]
                        bB = banks[(2 * fc + 1) % 8]
                        for h in range(8):
                            P.op("tensor", lambda e, bA=bA, h=h, fc=fc: e.matmul(bA[:, 0:NT], lhsT=wau[0:64, h, fc * 128:(fc + 1) * 128], rhs=oaT[0:64, h, 0:NT],
                                                                             start=(h == 0), stop=(h == 7)), reads=[wau, oaT], writes=[bA])
                        for h in range(8):
                            P.op("tensor", lambda e, bB=bB, h=h, fc=fc: e.matmul(bB[:, 0:NT], lhsT=wbu[0:64, h, fc * 128:(fc + 1) * 128], rhs=obT[0:64, h, 0:NT],
                                                                             start=(h == 0), stop=(h == 7)), reads=[wbu, obT], writes=[bB])
                        P.op("vector", lambda e, bA=bA, sa_=sa_: e.tensor_tensor(out=ma[:, 0:NT], in0=bA[:, 0:NT], in1=sa_[:, 0:NT], op=ALU.mult), reads=[bA, sa_], writes=[ma])
                        P.op("vector", lambda e, bB=bB, sb_=sb_: e.tensor_tensor(out=sb_[:, 0:NT], in0=bB[:, 0:NT], in1=sb_[:, 0:NT], op=ALU.mult), reads=[bB, sb_], writes=[sb_])
                        P.op("vector", lambda e, sb_=sb_, fc=fc: e.tensor_tensor(out=mixT[:, fc, 0:NT], in0=ma[:, 0:NT], in1=sb_[:, 0:NT], op=ALU.add), reads=[ma, sb_], writes=[mixT])
                    for t, (t0, nt) in enumerate(tiles):
                        xr = xre[t % 2]
                        x_fn(t, xr)
                        for hf in range(2):
                            bk = banks[(2 * t + hf) % 8]
                            for c in range(8):
                                P.op("tensor", lambda e, bk=bk, c=c, t0=t0, nt=nt, hf=hf: e.matmul(bk[0:nt, :], lhsT=mixT[:, c, t0:t0 + nt], rhs=wo[:, c, hf * 512:(hf + 1) * 512],
                                                                                         start=(c == 0), stop=(c == 7)), reads=[mixT, wo], writes=[bk])
                            P.op("vector", lambda e, bk=bk, xr=xr, hf=hf, nt=nt: e.scalar_tensor_tensor(out=z[0:nt, hf * 512:(hf + 1) * 512], in0=xr[0:nt, hf * 512:(hf + 1) * 512],
                                                                                                scalar=float(ALPHA), in1=bk[0:nt, :], op0=ALU.mult, op1=ALU.add),
                                 reads=[bk, xr], writes=[z])
                        fin = layer_norm(z, nt, 0, hres[0:nt, t, :], sm3, st6)
                        P.op("vector", fin, reads=[z, lnr], writes=[hres])
                        for g in range(2):
                            bk = banks[(g + 2 * t) % 8]
                            for cc in range(4):
                                c = 4 * g + cc
                                P.op("tensor", lambda e, bk=bk, cc=cc, c=c, t=t, nt=nt: e.transpose(out=bk[:, cc * 128:cc * 128 + nt], in_=hres[0:nt, t, c * 128:(c + 1) * 128],
                                                                                          identity=ident[0:nt, 0:nt]), reads=[hres, ident], writes=[bk])
                            evac(hT[:, 4 * g:4 * g + 4, t0:t0 + nt], bk[:, :].rearrange("p (c q) -> p c q", c=4)[:, :, 0:nt], [bk], [hT])
                    P.barrier()
                chk(9)
                with ExitStack() as s3b:
                    uT = sb(s3b, "uT", [128, 32, 512], BF16)
                    wf1 = [sb(s3b, "wf1_%d" % i, [128, 8, 512], BF16) for i in range(2)]
                    wf2 = [sb(s3b, "wf2_%d" % i, [128, 8, D], BF16) for i in range(2)]
                    rl = [sb(s3b, "rl%d" % i, [128, 512], F32) for i in range(2)]
                    yst = [sb(s3b, "yst%d" % i, [128, D], F32) for i in range(2)]
                    rl_i = 0
                    for fb in range(8):
                        w1 = wf1[fb % 2]
                        P.dma("gpsimd", lambda e, w1=w1, fb=fb: e.dma_start(out=w1[:], in_=w_ff1[:, fb * 512:(fb + 1) * 512].rearrange("(c p) n -> p c n", p=128)), writes=[w1])
                        for fc in range(4):
                            bk = banks[(fb * 4 + fc) % 8]
                            for c in range(8):
                                P.op("tensor", lambda e, bk=bk, w1=w1, c=c, fc=fc: e.matmul(bk[:, 0:NT], lhsT=w1[:, c, fc * 128:(fc + 1) * 128], rhs=hT[:, c, 0:NT],
                                                                                    start=(c == 0), stop=(c == 7)), reads=[w1, hT], writes=[bk])
                            r_ = rl[rl_i % 2]
                            rl_i += 1
                            P.op("scalar", lambda e, bk=bk, r_=r_: e.activation(out=r_[:, 0:NT], in_=bk[:, 0:NT], func=AF.Relu), reads=[bk], writes=[r_])
                            P.op("vector", lambda e, r_=r_, fb=fb, fc=fc: e.tensor_tensor(out=uT[:, fb * 4 + fc, 0:NT], in0=r_[:, 0:NT], in1=r_[:, 0:NT], op=ALU.mult),
                                 reads=[r_], writes=[uT])
                    for blk in range(4):
                        w2 = wf2[blk % 2]
                        P.dma("gpsimd", lambda e, w2=w2, blk=blk: e.dma_start(out=w2[:], in_=w_ff2[blk * 1024:(blk + 1) * 1024, :].rearrange("(c p) n -> p c n", p=128)), writes=[w2])
                        for cc in range(8):
                            ch = blk * 8 + cc
                            for t, (t0, nt) in enumerate(tiles):
                                for hf in range(2):
                                    bk = banks[2 * t + hf]
                                    P.op("tensor", lambda e, bk=bk, w2=w2, cc=cc, ch=ch, t0=t0, nt=nt, hf=hf: e.matmul(
                                        bk[0:nt, :], lhsT=uT[:, ch, t0:t0 + nt], rhs=w2[:, cc, hf * 512:(hf + 1) * 512], start=(ch == 0), stop=(ch == 31)),
                                        reads=[uT, w2], writes=[bk])
                    for t, (t0, nt) in enumerate(tiles):
                        for hf in range(2):
                            bk = banks[2 * t + hf]
                            P.op("vector", lambda e, bk=bk, t=t, hf=hf, nt=nt: e.scalar_tensor_tensor(out=z[0:nt, hf * 512:(hf + 1) * 512], in0=hres[0:nt, t, hf * 512:(hf + 1) * 512],
                                                                                              scalar=float(ALPHA), in1=bk[0:nt, :], op0=ALU.mult, op1=ALU.add),
                                 reads=[bk, hres], writes=[z])
                        ys = yst[t % 2]
                        fin = layer_norm(z, nt, 2, ys[0:nt, :], sm3, st6)
                        P.op("vector", fin, reads=[z, lnr], writes=[ys])
                        out_toks.append(y_fn(t, ys))
                    P.barrier()


        def qgroup(mq):
            nch = 4 * mq + 4
            ntl = 16 * mq + 16
            with ExitStack() as sq:
                oaT = sb(sq, "oaT", [128, 8, 512], BF16)
                obT = sb(sq, "obT", [128, 8, 512], BF16)
                s2 = ExitStack()
                sq.callback(s2.close)
                mbT = sb(s2, "mbT", [128, 64, 512], BF16)
                qaT_g = sb(s2, "qaT_g", [128, 4, 512], BF16)
                qbT_g = sb(s2, "qbT_g", [128, 4, 512], BF16)
                qiT_g = sb(s2, "qiT_g", [128, 2, 512], BF16)
                P.dma("sync", lambda e: e.dma_start(out=qaT_g[:], in_=qaT_d[mq]), writes=[qaT_g, qTall])
                P.dma("sync", lambda e: e.dma_start(out=qbT_g[:], in_=qbT_d[mq]), writes=[qbT_g, qTall])
                P.dma("sync", lambda e: e.dma_start(out=qiT_g[:], in_=qiT_d[mq]), writes=[qiT_g])
                chk(1)
                with ExitStack() as sa:
                    btab = sb(sa, "btab", [128, 9, 512], F32)
                    P.dma("sync", lambda e: e.dma_start(out=btab[:], in_=btab_d), writes=[btab])
                    kib = [sb(sa, "kib%d" % i, [128, 2048], BF16) for i in range(2)]
                    for i in range(4):
                        qt = 4 * mq + i

                        def emit_scores(c, i=i):
                            kb_ = kib[(c // 4) % 2]
                            if c % 4 == 0:
                                for hf in range(2):
                                    P.dma("sync", lambda e, kb_=kb_, c=c, hf=hf: e.dma_start(
                                        out=kb_[64 * hf:64 * hf + 64, :], in_=kiT_d[:, (c // 4) * 2048:(c // 4 + 1) * 2048]), writes=[kb_])
                            for h in range(4):
                                r0 = 64 * (h % 2)
                                P.op("tensor", lambda e, h=h, r0=r0, c=c, kb_=kb_: e.matmul(
                                    banks[h][:, :], lhsT=qiT_g[r0:r0 + 64, h // 2, i * 128:(i + 1) * 128],
                                    rhs=kb_[r0:r0 + 64, (c % 4) * 512:(c % 4 + 1) * 512], start=True, stop=True),
                                    reads=[qiT_g, kb_], writes=[banks[h]])
                        with ExitStack() as si:
                            indexer(si, 128, nch, emit_scores, lambda k, qt=qt: lohi[:, qt, k:k + 1],
                                    lambda c, i=i: (c if c < 3 else (4 + i if c == nch - 1 else 3)), mbT, i * 128, btab)
                        P.op("vector", lambda e: e.memset(ones_t[0:1, 0:1], 1.0), reads=[mbT], writes=[maskall, ones_t])
                    chk(4)
                    P.barrier()
                chk(5)
                with ExitStack() as sbb:
                    A = AttBufs(sbb)
                    mbBT = sb(sbb, "mbBT", [32, 8, 512], BF16)
                    pastb = sb(sbb, "pastb", [128, 2, 32], F32)
                    P.dma("sync", lambda e: e.dma_start(out=pastb[:], in_=pastb_d[mq]), writes=[pastb])
                    for a in range(2):
                        P.op("vector", lambda e, a=a: e.tensor_tensor(out=pastb[:, a, :], in0=pastb[:, a, :], in1=gb2[:], op=ALU.add),
                             reads=[pastb, gb2], writes=[pastb, pastb_t])
                    for i in range(4):
                        def emit_gate(bk, i=i):
                            for h in range(8):
                                r0 = 64 * (h % 2)
                                P.op("tensor", lambda e, h=h, r0=r0: e.matmul(bk[:, h * 32:(h + 1) * 32], lhsT=qbT_g[r0:r0 + 64, h // 2, i * 128:(i + 1) * 128],
                                                                          rhs=meansTb[r0:r0 + 64, h // 2, :], start=True, stop=True),
                                     reads=[qbT_g, meansTb], writes=[bk])
                        with ExitStack() as sg_:
                            moba_gate(sg_, 128, emit_gate, pastb[:, i // 2, :], 8 * mq + 6 + i // 2, mbBT, i * 128)
                    P.op("vector", lambda e: e.memset(ones_t[0:1, 0:1], 1.0), reads=[mbBT], writes=[maskall, ones_t])
                    chk(6)

                    def diag_fn(u):
                        return (512 - 128 * (u - (ntl - 5))) if u >= ntl - 5 else None
                    if DBG_LEVEL >= 7:
                        attend(A, lambda p: kaT_d[p], lambda: va_d, lambda r0, p: qaT_g[r0:r0 + 64, p, :], lambda h: oaT[0:64, h, :], 0, ntl, 512, diag_fn,
                               lambda u, h: (None, mbT[:, u, :]))
                    chk(7)
                    if DBG_LEVEL >= 8:
                        attend(A, lambda p: kbT_d[p], lambda: vb_d, lambda r0, p: qbT_g[r0:r0 + 64, p, :], lambda h: obT[0:64, h, :], 8, ntl, 512, diag_fn,
                               lambda u, h: (ablk[0:32, u // 2, :], mbBT[0:32, h, :]))
                    chk(8)
                    P.op("vector", lambda e: e.memset(ones_t[0:1, 0:1], 1.0), reads=[oall], writes=[oaT, obT, ones_t])
                    P.barrier()
                s2.close()

                def sg_fn(which, fc, dst):
                    src = sga_d if which == 0 else sgb_d
                    P.dma("sync", lambda e: e.dma_start(out=dst[:], in_=src[mq, fc]), writes=[dst])

                def x_fn(t, xr):
                    r0 = (16 * mq + 12 + t) * 128
                    P.dma("sync", lambda e: e.dma_start(out=xr[:], in_=xs[r0:r0 + 128, :]), writes=[xr])

                def y_fn(t, ys):
                    r0 = (4 * mq + t) * 128
                    return P.dma("sync", lambda e: e.dma_start(out=y_p[r0:r0 + 128, :], in_=ys[:]), reads=[ys])
                phase3(512, [(0, 128), (128, 128), (256, 128), (384, 128)], oaT, obT, sg_fn, x_fn, y_fn)

        for mq_ in range(DBG_NQG):
            try:
                qgroup(mq_)
            except _Stop:
                pass
        def sample_group():
            chk(11)
            with ExitStack() as sq:
                oaTs = sb(sq, "oaTs", [128, 8, 512], BF16)
                obTs = sb(sq, "obTs", [128, 8, 512], BF16)
                s2 = ExitStack()
                sq.callback(s2.close)
                mbTs = sb(s2, "mbTs", [128, 68, NSAMP], BF16)
                with ExitStack() as sa:
                    btab = sb(sa, "btab", [128, 9, 512], F32)
                    P.dma("sync", lambda e: e.dma_start(out=btab[:], in_=btab_d), writes=[btab])
                    kibs = [sb(sa, "kibs%d" % i, [128, 2048], BF16) for i in range(4)]

                    def emit_scores(c):
                        if c % 4 == 0:
                            wd_ = min(2048, 8704 - (c // 4) * 2048)
                            for s in range(4):
                                for hf in range(2):
                                    P.dma("sync", lambda e, s=s, c=c, hf=hf, wd_=wd_: e.dma_start(
                                        out=kibs[s][64 * hf:64 * hf + 64, 0:wd_], in_=skiT_d[s][:, (c // 4) * 2048:(c // 4) * 2048 + wd_]), reads=[sscr], writes=[kibs[s]])
                        for h in range(4):
                            r0 = 64 * (h % 2)
                            for s in range(4):
                                P.op("tensor", lambda e, h=h, r0=r0, c=c, s=s: e.matmul(
                                    banks[h][0:NSAMP, :], lhsT=qiTm[s][r0:r0 + 64, h // 2, :], rhs=kibs[s][r0:r0 + 64, (c % 4) * 512:(c % 4 + 1) * 512],
                                    start=(s == 0), stop=(s == 3)), reads=[qiTm[s], kibs[s]], writes=[banks[h]])
                    with ExitStack() as si:
                        indexer(si, NSAMP, 17, emit_scores, lambda k: lohis[0:NSAMP, k:k + 1], lambda c: (8 if c == 16 else 3), mbTs, 0, btab)
                    P.op("vector", lambda e: e.memset(ones_t[0:1, 0:1], 1.0), reads=[mbTs], writes=[maskall, ones_t])
                    P.barrier()
                chk(12)
                with ExitStack() as sbb:
                    A = AttBufs(sbb)
                    mbBTs = sb(sbb, "mbBTs", [32, 8, NSAMP], BF16)
                    zb = sb(sbb, "zb", [128, 32], F32)
                    P.op("vector", lambda e: e.memset(zb[:], 0.0), writes=[zb, pastb_t])

                    def emit_gate(bk):
                        for h in range(8):
                            r0 = 64 * (h % 2)
                            for s in range(4):
                                P.op("tensor", lambda e, h=h, r0=r0, s=s: e.matmul(bk[0:NSAMP, h * 32:(h + 1) * 32], lhsT=qbTm[s][r0:r0 + 64, h // 2, :],
                                                                               rhs=smeansTb[s][r0:r0 + 64, h // 2, :], start=(s == 0), stop=(s == 3)),
                                     reads=[qbTm[s], smeansTb[s]], writes=[bk])
                    with ExitStack() as sg_:
                        moba_gate(sg_, NSAMP, emit_gate, zb[0:NSAMP, :], None, mbBTs, 0)
                    P.op("vector", lambda e: e.memset(ones_t[0:1, 0:1], 1.0), reads=[mbBTs, sscr], writes=[maskall, ones_t])
                    chk(13)

                    def diag_fn(u):
                        return 512 if u == 63 else (384 if u == 64 else None)
                    for s in range(DBG_NSEQ):
                        q0 = 8 * s
                        attend(A, lambda p, s=s: skaT_d[s, p], lambda s=s: sva_d[s], lambda r0, p, q0=q0: qaTs[r0:r0 + 64, p, q0:q0 + 8],
                               lambda h, q0=q0: oaTs[0:64, h, q0:q0 + 8], 0, 65, 8, diag_fn, lambda u, h, q0=q0: (None, mbTs[:, u, q0:q0 + 8]))
                        attend(A, lambda p, s=s: skbT_d[s, p], lambda s=s: svb_d[s], lambda r0, p, q0=q0: qbTs[r0:r0 + 64, p, q0:q0 + 8],
                               lambda h, q0=q0: obTs[0:64, h, q0:q0 + 8], 8, 65, 8, diag_fn,
                               lambda u, h, q0=q0: ((ablk[0:32, u // 2, :], mbBTs[0:32, h, q0:q0 + 8]) if u < 64 else None))
                    P.op("vector", lambda e: e.memset(ones_t[0:1, 0:1], 1.0), reads=[oall], writes=[oaTs, obTs, ones_t])
                    P.barrier()
                s2.close()
                chk(14)

                def sg_fn(which, fc, dst):
                    src = sgas if which == 0 else sgbs
                    P.op("vector", lambda e: e.tensor_copy(out=dst[:, 0:NSAMP], in_=src[:, fc, :]), reads=[src], writes=[dst])

                def x_fn(t, xr):
                    P.dma("sync", lambda e: e.dma_start(out=xr[0:NSAMP, :], in_=xsm[:, :]), writes=[xr])

                def y_fn(t, ys):
                    return P.dma("sync", lambda e: e.dma_start(out=y_s[:, :], in_=ys[0:NSAMP, :]), reads=[ys])
                phase3(NSAMP, [(0, NSAMP)], oaTs, obTs, sg_fn, x_fn, y_fn)

        if DBG_SAMPLE:
            sample_group()
        P.barrier()
        for e in ["sync"]:
            waits = P._waits(e, dict([t for t in out_toks if t is not None]))
            if waits:
                P.ops[e].append((waits, None, None, 0))
        P.emit()
    return nc


def host_consts(rel_bias):
    ki = np.arange(128)[:, None]
    x = np.arange(1024)[None, :]
    d = x - ki - 384
    bkt = t5_bucket_np(d)
    wt = np.empty((128, 16, 1024), np.float32)
    for h in range(16):
        wt[:, h, :] = np.where(d >= 0, rel_bias[bkt, h], np.float32(NEGM))
    b31 = np.broadcast_to(rel_bias[31][None, :], (128, 16)).astype(np.float32).copy()
    qi = np.arange(128)[:, None]
    kk = np.arange(512)[None, :]
    cm = np.empty((128, 4, 512), np.float32)
    for i in range(4):
        cm[:, i, :] = np.where(kk <= i * 128 + qi, 0.0, -BIG)
    pertb = np.broadcast_to((-EPS_TIE * np.arange(512, dtype=np.float64)).astype(np.float32)[None, :], (128, 512)).copy()
    ablk = np.zeros((32, 32, 128), np.float32)
    for u in range(32):
        ablk[u, u, :] = 1.0
    pastb = np.zeros((4, 128, 2, 32), np.float32)
    for mq in range(4):
        for a in range(2):
            pastb[mq, :, a, 8 * mq + 6 + a:] = -BIG
    return wt, b31, cm, pertb, ablk, pastb


def core_consts(j, cm, pertb):
    nph = (12 - 4 * j) * 128
    phb = np.zeros((128, 1536), np.float32)
    phb[:, :nph] = -BIG
    btab = np.empty((128, 9, 512), np.float32)
    kk = np.arange(512)[None, :]
    btab[:, 8, :] = pertb + np.where(kk <= (np.arange(128)[:, None] % 8), 0.0, -BIG).astype(np.float32)
    for c in range(3):
        btab[:, c, :] = pertb + phb[:, c * 512:(c + 1) * 512]
    btab[:, 3, :] = pertb
    for i in range(4):
        btab[:, 4 + i, :] = pertb + cm[:, i, :]
    bval = np.zeros((128, 32), np.float32)
    bval[:, :nph // 256] = -BIG
    return btab, bval, nph


def kernel(x_prompt, x_sample, cache_dsa, cache_moba, page_table, w_in, rel_bias, w_a_up, w_b_up,
           w_out, ln1_g, ln1_b, w_ff1, w_ff2, ln2_g, ln2_b):
    x_prompt = np.asarray(x_prompt, np.float32)
    x_sample = np.asarray(x_sample, np.float32)
    rel_bias = np.asarray(rel_bias, np.float32)
    wt, b31, cm, pertb, ablk, pastb = host_consts(rel_bias)
    lnrep = np.stack([np.broadcast_to(np.asarray(a, np.float32)[0][None, :], (128, D)) for a in (ln1_g, ln1_b, ln2_g, ln2_b)], axis=1).copy()
    ident = np.eye(128, dtype=np.float32)
    common = dict(w_in=np.ascontiguousarray(np.asarray(w_in, np.float32)[0]),
                  w_a_up=np.ascontiguousarray(np.asarray(w_a_up, np.float32)[0]),
                  w_b_up=np.ascontiguousarray(np.asarray(w_b_up, np.float32)[0]),
                  w_out=np.ascontiguousarray(np.asarray(w_out, np.float32)[0]),
                  w_ff1=np.ascontiguousarray(np.asarray(w_ff1, np.float32)[0]),
                  w_ff2=np.ascontiguousarray(np.asarray(w_ff2, np.float32)[0]),
                  lnrep=lnrep, ident=ident, wtab=wt, b31=b31, ablk=ablk, pastb=pastb)
    page_table = np.asarray(page_table, np.int32)
    iot = np.arange(128, dtype=np.int32)[:, None].copy()
    cdsa = np.asarray(cache_dsa, np.float32)[0].reshape(2560 * 128, 1088)
    cmoba = np.asarray(cache_moba, np.float32)[0].reshape(2560 * 128, 1024)
    in_maps = []
    for c in range(8):
        b, j = c // 4, c % 4
        btab, bval, nph = core_consts(j, cm, pertb)
        xsl = np.zeros((T, D), np.float32)
        xsl[nph:] = x_prompt[b, :T - nph]
        m = dict(common)
        ptrep = np.ascontiguousarray(np.broadcast_to(page_table[4 * c:4 * c + 4].reshape(1, 256), (128, 256)))
        m.update(xs=xsl, xsm=np.ascontiguousarray(x_sample[4 * c:4 * c + 4].reshape(NSAMP, D)), btab=btab, bval=bval, ptrep=ptrep, iot=iot, cdsa=cdsa, cmoba=cmoba)
        in_maps.append(m)
    nc = build_program()
    res = run_bass_kernel_spmd(nc, in_maps, core_ids=list(range(8)))
    y_p = np.zeros((2, T, D), np.float32)
    dsa_pp = np.zeros((1, 2, T, 1088), np.float32)
    moba_pp = np.zeros((1, 2, T, 1024), np.float32)
    y_s = np.zeros((32, 8, D), np.float32)
    dsa_ss = np.zeros((1, 32, 8, 1088), np.float32)
    moba_ss = np.zeros((1, 32, 8, 1024), np.float32)
    for c in range(8):
        b, j = c // 4, c % 4
        r = res.results[c]
        for m in range(4):
            g0 = (4 * m + j) * 512
            y_p[b, g0:g0 + 512] = r["y_p"][m * 512:(m + 1) * 512]
            dsa_pp[0, b, g0:g0 + 512] = r["dsa_p"][m * 512:(m + 1) * 512]
            moba_pp[0, b, g0:g0 + 512] = r["moba_p"][m * 512:(m + 1) * 512]
        y_s[4 * c:4 * c + 4] = r["y_s"].reshape(4, 8, D)
        dsa_ss[0, 4 * c:4 * c + 4] = r["dsa_s"].reshape(4, 8, 1088)
        moba_ss[0, 4 * c:4 * c + 4] = r["moba_s"].reshape(4, 8, 1024)
    return (y_p, y_s, dsa_pp, moba_pp, dsa_ss, moba_ss)
```
